# Optimizing a Trainium2 kernel written in Bass

```python
import jax, jax.numpy as jnp
from jax import lax
import numpy as np

D_MODEL = 2048
BATCH = 4
SEQ = 2048
DEPTH = 1

MEM_LEN = 256
D_CONV = D_MODEL // 2
CONV_WIDTH = 31
ATT_HEAD_DIM = 128
ATT_HEADS = (D_MODEL // 2) // ATT_HEAD_DIM
D_ATT = ATT_HEADS * ATT_HEAD_DIM
MOBA_BLOCK = 256
MOBA_TOPK = 3
Q_CHUNK = 16
CROSS_HEADS = 4
CROSS_HEAD_DIM = D_MODEL // CROSS_HEADS
D_FF = -(-(8 * D_MODEL) // (3 * 256)) * 256
IN_COLS = 2 * D_CONV + 3 * D_ATT + 2 * D_MODEL
EPS = 1e-6

kernel_name = "hybrid_gated_conformer_moba_block"


def rmsnorm(x, g):
    xf = x.astype(jnp.float32)
    y = xf * lax.rsqrt(jnp.mean(xf * xf, axis=-1, keepdims=True) + EPS)
    return (y * g.astype(jnp.float32)).astype(x.dtype)


def layernorm(x, g, b):
    xf = x.astype(jnp.float32)
    mu = jnp.mean(xf, axis=-1, keepdims=True)
    var = jnp.mean(jnp.square(xf - mu), axis=-1, keepdims=True)
    y = (xf - mu) * lax.rsqrt(var + EPS)
    return (y * g.astype(jnp.float32) + b.astype(jnp.float32)).astype(x.dtype)


def conformer_conv(u, conv_w, conv_b, ln_g, ln_b, w_proj):
    a, g = jnp.split(u, 2, axis=-1)
    v = a * jax.nn.sigmoid(g)
    y = lax.conv_general_dilated(
        v, conv_w[:, None, :].astype(v.dtype), window_strides=(1,),
        padding=[(CONV_WIDTH - 1, 0)],
        dimension_numbers=('NWC', 'WIO', 'NWC'),
        feature_group_count=D_CONV) + conv_b
    y = jax.nn.silu(layernorm(y, ln_g, ln_b))
    return y @ w_proj


def moba_attention(q, k, v):
    B, S, H, hd = q.shape
    n_blocks = -(-S // MOBA_BLOCK)
    s_pad = n_blocks * MOBA_BLOCK
    pad = [(0, 0), (0, 0), (0, s_pad - S), (0, 0)]
    qt = jnp.pad(jnp.transpose(q, (0, 2, 1, 3)), pad)
    kt = jnp.pad(jnp.transpose(k, (0, 2, 1, 3)), pad)
    vt = jnp.pad(jnp.transpose(v, (0, 2, 1, 3)), pad)
    kb = kt.reshape(B, H, n_blocks, MOBA_BLOCK, hd)
    vb = vt.reshape(B, H, n_blocks, MOBA_BLOCK, hd)
    scale = 1.0 / np.sqrt(hd)

    k_mean = jnp.mean(kb.astype(jnp.float32), axis=3)
    gate = jnp.einsum('bhsd,bhnd->bhsn', qt.astype(jnp.float32), k_mean)
    q_block = jnp.arange(s_pad) // MOBA_BLOCK
    past = jnp.arange(n_blocks)[None, :] < q_block[:, None]
    gate = jnp.where(past[None, None], gate, -jnp.inf)
    k_sel = min(MOBA_TOPK, n_blocks)
    _, sel = lax.top_k(gate, k_sel)
    sel_valid = jnp.arange(k_sel)[None, :] < q_block[:, None]

    n_chunks = s_pad // Q_CHUNK
    q_c = jnp.moveaxis(qt.reshape(B, H, n_chunks, Q_CHUNK, hd), 2, 0)
    sel_c = jnp.moveaxis(sel.reshape(B, H, n_chunks, Q_CHUNK, k_sel), 2, 0)
    valid_c = sel_valid.reshape(n_chunks, Q_CHUNK, k_sel)
    starts = jnp.arange(n_chunks, dtype=jnp.int32) * Q_CHUNK
    gather_blocks = jax.vmap(jax.vmap(lambda blocks, idx: blocks[idx]))

    def chunk(args):
        qc, selc, validc, start = args
        blk = start // MOBA_BLOCK
        k_own = lax.dynamic_index_in_dim(kb, blk, axis=2, keepdims=False)
        v_own = lax.dynamic_index_in_dim(vb, blk, axis=2, keepdims=False)
        k_past = gather_blocks(kb, selc)
        v_past = gather_blocks(vb, selc)
        s_past = jnp.einsum('bhcd,bhcjpd->bhcjp', qc, k_past,
                            preferred_element_type=jnp.float32) * scale
        s_own = jnp.einsum('bhcd,bhpd->bhcp', qc, k_own,
                           preferred_element_type=jnp.float32) * scale
        q_pos = start + jnp.arange(Q_CHUNK)
        k_pos = blk * MOBA_BLOCK + jnp.arange(MOBA_BLOCK)
        causal = k_pos[None, :] <= q_pos[:, None]
        s_past = jnp.where(validc[None, None, :, :, None], s_past, -jnp.inf)
        s_own = jnp.where(causal[None, None], s_own, -jnp.inf)
        logits = jnp.concatenate(
            [s_past.reshape(B, H, Q_CHUNK, k_sel * MOBA_BLOCK), s_own], axis=-1)
        p = jax.nn.softmax(logits, axis=-1)
        p_past = p[..., :k_sel * MOBA_BLOCK].reshape(B, H, Q_CHUNK, k_sel, MOBA_BLOCK)
        p_own = p[..., k_sel * MOBA_BLOCK:]
        out = (jnp.einsum('bhcjp,bhcjpd->bhcd', p_past.astype(v_past.dtype), v_past)
               + jnp.einsum('bhcp,bhpd->bhcd', p_own.astype(v_own.dtype), v_own))
        return out.astype(v.dtype)

    out = lax.map(chunk, (q_c, sel_c, valid_c, starts))
    out = jnp.moveaxis(out, 0, 2).reshape(B, H, s_pad, hd)[:, :, :S]
    return jnp.transpose(out, (0, 2, 1, 3)).reshape(B, S, H * hd)


def cross_attention(h, m, w_q, w_kv, w_o):
    B, S, _ = h.shape
    M = m.shape[1]
    q = (h @ w_q).reshape(B, S, CROSS_HEADS, CROSS_HEAD_DIM)
    k, v = jnp.split(m @ w_kv, 2, axis=-1)
    k = k.reshape(B, M, CROSS_HEADS, CROSS_HEAD_DIM)
    v = v.reshape(B, M, CROSS_HEADS, CROSS_HEAD_DIM)
    logits = jnp.einsum('bshd,bmhd->bhsm', q, k,
                        preferred_element_type=jnp.float32) / np.sqrt(CROSS_HEAD_DIM)
    p = jax.nn.softmax(logits, axis=-1).astype(v.dtype)
    out = jnp.einsum('bhsm,bmhd->bshd', p, v).reshape(B, S, D_MODEL)
    return out @ w_o


def swiglu(h, w_in, w_out):
    g, u = jnp.split(h @ w_in, 2, axis=-1)
    return (jax.nn.silu(g) * u) @ w_out


def setup_inputs(seed: int = 0) -> dict:
    key = jax.random.key(seed)
    ks = jax.random.split(key, 24)

    def w(k, shape, fan_in):
        return jax.random.normal(k, shape, jnp.float32) * (fan_in ** -0.5)

    def gain(k, shape):
        return 1.0 + 0.02 * jax.random.normal(k, shape, jnp.float32)

    def bias(k, shape):
        return 0.02 * jax.random.normal(k, shape, jnp.float32)

    L = DEPTH
    return {
        "x": jax.random.normal(ks[0], (BATCH, SEQ, D_MODEL), jnp.float32),
        "mem": jax.random.normal(ks[1], (BATCH, MEM_LEN, D_MODEL), jnp.float32),
        "norm_mix_g": gain(ks[2], (L, D_MODEL)),
        "w_in": w(ks[3], (L, D_MODEL, IN_COLS), D_MODEL),
        "b_in": bias(ks[4], (L, IN_COLS)),
        "conv_w": w(ks[5], (L, CONV_WIDTH, D_CONV), CONV_WIDTH),
        "conv_b": bias(ks[6], (L, D_CONV)),
        "conv_ln_g": gain(ks[7], (L, D_CONV)),
        "conv_ln_b": bias(ks[8], (L, D_CONV)),
        "w_conv_out": w(ks[9], (L, D_CONV, D_MODEL), D_CONV),
        "w_att_out": w(ks[10], (L, D_ATT, D_MODEL), D_ATT),
        "w_out": w(ks[11], (L, D_MODEL, D_MODEL), D_MODEL),
        "norm_cross_g": gain(ks[12], (L, D_MODEL)),
        "norm_mem_g": gain(ks[13], (L, D_MODEL)),
        "w_cq": w(ks[14], (L, D_MODEL, D_MODEL), D_MODEL),
        "w_ckv": w(ks[15], (L, D_MODEL, 2 * D_MODEL), D_MODEL),
        "w_co": w(ks[16], (L, D_MODEL, D_MODEL), D_MODEL),
        "norm_ffn_g": gain(ks[17], (L, D_MODEL)),
        "w_ffn_in": w(ks[18], (L, D_MODEL, 2 * D_FF), D_MODEL),
        "w_ffn_out": w(ks[19], (L, D_FF, D_MODEL), D_FF),
        "norm_final_g": gain(ks[20], (D_MODEL,)),
    }


def reference(x, mem, norm_mix_g, w_in, b_in, conv_w, conv_b, conv_ln_g, conv_ln_b,
              w_conv_out, w_att_out, w_out, norm_cross_g, norm_mem_g, w_cq, w_ckv,
              w_co, norm_ffn_g, w_ffn_in, w_ffn_out, norm_final_g):
    B, S, _ = x.shape
    split_at = [2 * D_CONV, 2 * D_CONV + D_ATT, 2 * D_CONV + 2 * D_ATT,
                2 * D_CONV + 3 * D_ATT, 2 * D_CONV + 3 * D_ATT + D_MODEL]
    for l in range(DEPTH):
        h = rmsnorm(x, norm_mix_g[l])
        proj = h @ w_in[l] + b_in[l]
        u_conv, q, k, v, g_conv, g_att = jnp.split(proj, split_at, axis=-1)
        y_conv = conformer_conv(u_conv, conv_w[l], conv_b[l], conv_ln_g[l],
                                conv_ln_b[l], w_conv_out[l])
        hs = (B, S, ATT_HEADS, ATT_HEAD_DIM)
        y_att = moba_attention(q.reshape(hs), k.reshape(hs), v.reshape(hs)) @ w_att_out[l]
        merged = jax.nn.sigmoid(g_conv) * y_conv + jax.nn.sigmoid(g_att) * y_att
        x = x + merged @ w_out[l]
        h = rmsnorm(x, norm_cross_g[l])
        m = rmsnorm(mem, norm_mem_g[l])
        x = x + cross_attention(h, m, w_cq[l], w_ckv[l], w_co[l])
        h = rmsnorm(x, norm_ffn_g[l])
        x = x + swiglu(h, w_ffn_in[l], w_ffn_out[l])
    return rmsnorm(x, norm_final_g)
```

```python
import numpy as np
import ml_dtypes
import concourse.bass as bass
import concourse.mybir as mybir
from concourse.bass_utils import run_bass_kernel_spmd

F32 = mybir.dt.float32
F32R = mybir.dt.float32r
BF16 = mybir.dt.bfloat16
AF = mybir.ActivationFunctionType
ALU = mybir.AluOpType
AX = mybir.AxisListType

D = 2048
S_OWN = 1024
TCH = 512
A0, G0, Q0, K0, V0, GC0, GA0 = 0, 1024, 2048, 3072, 4096, 5120, 7168
DFF = 5632
EPS = 1e-6
NEG = -32768.0

C_BIN = 0
C_GMIX = 72
C_GCROSS = 88
C_GMEM = 104
C_GFFN = 120
C_CONVW = 136
C_CONVB = 384
C_LNG = 392
C_LNB = 400
C_FLAG = 408
C_GMASK = 409
NCOLP = 473

DEBUG = False
NSLOT = 6
SLOTW = 2048
import os
KSTAGE = float(os.environ.get("KSTAGE", "99"))
KCORES = os.environ.get("KCORES", "")
KSUB = int(os.environ.get("KSUB", "99"))


class Sched:
    def __init__(self, nc, sems):
        self.nc = nc
        self.free_sems = list(sems)
        self.engs = ["pe", "act", "dve", "pool", "sp"]
        self.streams = {e: [] for e in self.engs}
        self.esem = {e: self.free_sems.pop() for e in self.engs}
        self.ecnt = {e: 0 for e in self.engs}
        self.waited = {e: {} for e in self.engs}
        self.keys = {}
        self.dsem = {}
        self.dcnt = {}
        self.semobj = {}
        for e in self.engs:
            self.semobj[id(self.esem[e])] = self.esem[e]

    def _dma_sem(self, name):
        if name not in self.dsem:
            s = self.free_sems.pop()
            self.dsem[name] = s
            self.dcnt[name] = 0
            self.semobj[id(s)] = s
        return self.dsem[name]

    def mark(self, n):
        if n >= KSTAGE:
            self.stopped = True

    def op(self, eng, fn, reads=(), writes=(), dma=None):
        if getattr(self, "stopped", False):
            return None
        need = {}

        def merge(d):
            for sid, v in d.items():
                if need.get(sid, 0) < v:
                    need[sid] = v
        for k in reads:
            if k in self.keys:
                if isinstance(k, tuple) and k[0] == "PS" and self.keys[k][1]:
                    merge(self.keys[k][1])
                else:
                    merge(self.keys[k][0])
        for k in writes:
            if k in self.keys:
                merge(self.keys[k][0])
                merge(self.keys[k][1])
        if dma is not None:
            sem = self._dma_sem(dma)
            self.dcnt[dma] += 16
            ev = (id(sem), self.dcnt[dma])
            inc = 16
        else:
            sem = self.esem[eng]
            self.ecnt[eng] += 1
            ev = (id(sem), self.ecnt[eng])
            inc = 1
        waits = []
        own = id(self.esem[eng])
        for sid, v in need.items():
            if eng == "pe" and sid == own:
                continue
            if self.waited[eng].get(sid, 0) < v:
                waits.append((self.semobj[sid], v))
                self.waited[eng][sid] = v
        self.streams[eng].append((fn, waits, (sem, inc)))
        for k in reads:
            ent = self.keys.setdefault(k, ({}, {}))
            if isinstance(k, tuple) and k[0] == "PS":
                ent[1].clear()
            if ent[1].get(ev[0], 0) < ev[1]:
                ent[1][ev[0]] = ev[1]
        for k in writes:
            self.keys[k] = ({ev[0]: ev[1]}, {})
        return ev

    def alias(self, old_prefixes, new_keys):
        merged = {}
        for k, (w, r) in self.keys.items():
            name = k[0] if isinstance(k, tuple) else k
            if name in old_prefixes:
                for d in (w, r):
                    for sid, v in d.items():
                        if merged.get(sid, 0) < v:
                            merged[sid] = v
        for k in new_keys:
            ent = self.keys.get(k)
            if ent is None:
                self.keys[k] = (dict(merged), {})
            else:
                for sid, v in merged.items():
                    if ent[0].get(sid, 0) < v:
                        ent[0][sid] = v

    def final_waits(self, eng):
        need = {}
        for e in self.engs:
            if self.ecnt[e] > 0:
                need[id(self.esem[e])] = self.ecnt[e]
        for name, s in self.dsem.items():
            need[id(s)] = self.dcnt[name]
        waits = [(self.semobj[sid], v) for sid, v in need.items() if sid != id(self.esem[eng])]
        self.streams[eng].append((None, waits, None))

    def replay(self, name, eng):
        for fn, waits, inc in self.streams[name]:
            for s_, v in waits:
                eng.wait_ge(s_, v)
            if fn is None:
                continue
            ins = fn(eng)
            ins.then_inc(inc[0], inc[1])


def build_program():
    nc = bass.Bass("TRN2", target_bir_lowering=False)
    nc.dge_precook = False

    class Lazy:
        def __init__(self, name, shape, dt):
            self.a = (name, list(shape), dt)
            self.v = None

        def ap(self):
            if self.v is None:
                self.v = nc.dram_tensor(self.a[0], self.a[1], self.a[2], kind="ExternalInput").ap()
            return self.v

        def __getitem__(self, k):
            return self.ap()[k]

        def partition_broadcast(self, n):
            return self.ap().partition_broadcast(n)

    def din(name, shape, dt):
        return Lazy(name, shape, dt)

    xo = din("xo", [S_OWN, D], F32)
    xp = din("xp", [S_OWN, D], F32)
    memd = din("mem", [256, D], F32)
    w_in = din("w_in", [D, 9216], F32R)
    w_conv_out = din("w_conv_out", [1024, D], F32R)
    w_att_out = din("w_att_out", [1024, D], F32R)
    w_out = din("w_out", [D, D], F32R)
    w_cq = din("w_cq", [D, D], F32R)
    w_ckv = din("w_ckv", [D, 2 * D], F32R)
    w_co = din("w_co", [D, D], F32R)
    w_ffn_in = din("w_ffn_in", [D, 2 * DFF], F32R)
    w_ffn_out = din("w_ffn_out", [DFF, D], F32R)
    colp_d = din("colp", [128, NCOLP], F32)
    gfin_d = din("gfin", [D], F32)
    gmix_d = din("gmix", [D], F32)
    gcross_d = din("gcross", [D], F32)
    gmem_d = din("gmem", [D], F32)
    gffn_d = din("gffn", [D], F32)
    ident_d = din("ident", [128, 128], F32)
    onesf_d = din("onesf", [128, 128], F32)
    identb_d = din("identb", [128, 128], BF16)
    onesb_d = din("onesb", [128, 128], BF16)
    cmask_d = din("cmask", [128, 512], BF16)
    esel_d = din("esel", [128, 9 * 128], BF16)
    out_d = nc.dram_tensor("out", [S_OWN, D], F32, kind="ExternalOutput").ap()
    dbg_d = {}
    if DEBUG:
        dbg_d["OT"] = nc.dram_tensor("dbg_OT", [128, 8 * 1024], F32, kind="ExternalOutput").ap()
        dbg_d["X1"] = nc.dram_tensor("dbg_X1", [128, 4 * 2048], F32, kind="ExternalOutput").ap()
        dbg_d["X2"] = nc.dram_tensor("dbg_X2", [128, 4 * 2048], F32, kind="ExternalOutput").ap()
        dbg_d["YC"] = nc.dram_tensor("dbg_YC", [128, 8 * 512], F32, kind="ExternalOutput").ap()
        dbg_d["SB"] = nc.dram_tensor("dbg_SB", [128, 512], F32, kind="ExternalOutput").ap()

    import contextlib
    es = contextlib.ExitStack()
    with es:
        REGW = 29696
        reg = es.enter_context(nc.sbuf_tensor("sb_reg", [128, REGW], F32))
        wring = es.enter_context(nc.sbuf_tensor("sb_wring", [128, NSLOT, SLOTW], F32R))
        OTr = es.enter_context(nc.sbuf_tensor("sb_OT", [128, 8, 1024], F32R))
        colp = es.enter_context(nc.sbuf_tensor("sb_colp", [128, NCOLP], F32))
        ident = es.enter_context(nc.sbuf_tensor("sb_ident", [128, 128], F32))
        onesf = es.enter_context(nc.sbuf_tensor("sb_onesf", [128, 128], F32))
        identb = es.enter_context(nc.sbuf_tensor("sb_identb", [128, 128], BF16))
        onesb = es.enter_context(nc.sbuf_tensor("sb_onesb", [128, 128], BF16))
        cmask = es.enter_context(nc.sbuf_tensor("sb_cmask", [128, 512], BF16))
        esel = es.enter_context(nc.sbuf_tensor("sb_esel", [128, 9 * 128], BF16))
        epsc = es.enter_context(nc.sbuf_tensor("sb_epsc", [128, 1], F32))
        stat = es.enter_context(nc.sbuf_tensor("sb_stat", [128, 64], F32))
        kmean = es.enter_context(nc.sbuf_tensor("sb_kmean", [128, 8, 8], F32))
        sbt = es.enter_context(nc.sbuf_tensor("sb_sbt", [128, 8 * 8 * 8], F32))
        gtmp = es.enter_context(nc.sbuf_tensor("sb_gtmp", [128, 128], F32))
        vhalo = es.enter_context(nc.sbuf_tensor("sb_vhalo", [128, 8, 32], F32))
        sbTb = es.enter_context(nc.sbuf_tensor("sb_sbTb", [128, 1, 512], BF16))
        PS = [es.enter_context(nc.psum_tensor(f"ps{i}", [128, 512], F32)) for i in range(8)]
        sems = [es.enter_context(nc.semaphore(f"s{i}")) for i in range(48)]
        S = Sched(nc, sems)
        reg_addr = nc.lookup_mloc(reg).addr
        ot_addr = nc.lookup_mloc(OTr).addr
        cnames = {}

        def carve(off_w, nwords, dt, pat=None, base=None, **kw):
            esz = 2 if dt is BF16 else 4
            nel = nwords * 4 // esz
            if pat:
                (dk, dv), = kw.items()
                shape = [128, dv, nel // dv]
            else:
                shape = [128, nel]
            i = cnames.get("n", 0)
            cnames["n"] = i + 1
            return nc.alloc_sbuf_tensor_at(f"cv{i}", shape, dt, offset=(reg_addr if base is None else base) + off_w * 4)[:]

        def cp(c, n=1):
            return colp[:, c:c + n]

        for nm, t_, d_ in (("colp", colp, colp_d), ("ident", ident, ident_d), ("onesf", onesf, onesf_d),
                           ("identb", identb, identb_d), ("onesb", onesb, onesb_d), ("cmask", cmask, cmask_d),
                           ("esel", esel, esel_d)):
            S.op("pool", (lambda e, t_=t_, d_=d_: e.dma_start(out=t_[:], in_=d_[:, :])), writes=[nm], dma="c_" + nm)
        S.op("dve", lambda e: e.memset(kmean[:], 0.0), writes=["kmean"])
        S.op("dve", lambda e: e.memset(sbTb[:], 0.0), writes=[("sbT", 0)])
        S.op("dve", lambda e: e.memset(sbTb[0:16, :, :], -1.0), writes=[("sbT", 0)])
        S.op("dve", lambda e: e.memset(epsc[:], EPS), writes=["epsc"])

        rot_state = {"all": 0, "A": 0, "B": 0}

        def bank(pool="all"):
            if pool == "all":
                i = rot_state["all"] % 8
            elif pool == "A":
                i = rot_state["A"] % 4
            else:
                i = 4 + rot_state["B"] % 4
            rot_state[pool] += 1
            return i

        wstate = {"n": 0}

        def wblock(w2d, kc, ncols):
            slot = wstate["n"] % NSLOT
            assert kc * ncols <= SLOTW
            wstate["n"] += 1
            view = wring[:, slot, 0:kc * ncols].rearrange("p (k n) -> p k n", k=kc)
            src = w2d.rearrange("(k p) n -> p k n", p=128)
            key = ("W", slot)
            S.op("sp", (lambda e: e.dma_start(out=view, in_=src)), writes=[key], dma=f"w{slot}")
            return view, key

        def mm_group(bk, out_ap, pairs, reads):
            n = len(pairs)

            def fn(e):
                ins = None
                for i, (l, r) in enumerate(pairs):
                    ins = e.matmul(out_ap, l, r, start=(i == 0), stop=(i == n - 1))
                return ins
            S.op("pe", fn, reads=reads, writes=[("PS", bk)])

        flip = {"n": 0}

        def alt():
            flip["n"] += 1
            return "act" if flip["n"] % 2 else "dve"

        def copy_evac(eng, out_ap, in_ap, reads, writes):
            if eng == "act":
                S.op("act", lambda e: e.activation(out=out_ap, in_=in_ap, func=AF.Copy), reads=reads, writes=writes)
            else:
                S.op("dve", lambda e: e.tensor_copy(out=out_ap, in_=in_ap), reads=reads, writes=writes)

        def make_hT(tiles, tile_keys, gvec, gbc, hTv, hkey, scratch=None, scratch_keys=None, loader=None, preload=None, geng="pool", junk=None, gload=True):
            nt = len(tiles)
            if gload:
                S.op(geng, lambda e: e.dma_start(out=gbc, in_=gvec.partition_broadcast(128)), writes=["gbc"], dma="gbc")
            xs_of = {}

            def stageA(t):
                xt = tiles[t]
                sc = stat[:, 2 * (t % 4):2 * (t % 4) + 1]
                rs = stat[:, 2 * (t % 4) + 1:2 * (t % 4) + 2]
                skey = ("stat", t % 4)
                jv = hTv[:, :, t * 128:(t + 1) * 128]
                if junk is None:
                    S.op("act", lambda e: e.activation(out=jv, in_=xt.rearrange("p (k n) -> p k n", k=16), func=AF.Square, accum_out=sc),
                         reads=[tile_keys[t]], writes=[(hkey, t, k) for k in range(16)] + [skey])
                else:
                    S.op("act", lambda e: e.activation(out=junk, in_=xt, func=AF.Square, accum_out=sc),
                         reads=[tile_keys[t]], writes=["junkP", skey])
                S.op("act", lambda e: e.activation(out=rs, in_=sc, func=AF.Sqrt, scale=1.0 / D, bias=epsc[:]),
                     reads=[skey, "epsc"], writes=[skey])
                S.op("dve", lambda e: e.reciprocal(out=rs, in_=rs), reads=[skey], writes=[skey])
                if scratch is None:
                    xs, xskey = xt, tile_keys[t]
                else:
                    xs, xskey = scratch[t % len(scratch)], scratch_keys[t % len(scratch)]
                S.op("dve", lambda e: e.scalar_tensor_tensor(out=xs, in0=xt, scalar=rs, in1=gbc, op0=ALU.mult, op1=ALU.mult),
                     reads=[tile_keys[t], skey, "gbc"], writes=[xskey])
                xs_of[t] = (xs, xskey)

            def stageB(t):
                xs, xskey = xs_of[t]
                for b4 in range(4):
                    bk = bank("all")

                    def fn(e, b4=b4, bk=bk):
                        ins = None
                        for kk in range(4):
                            k = 4 * b4 + kk
                            ins = e.transpose(PS[bk][:, kk * 128:(kk + 1) * 128], xs[:, k * 128:(k + 1) * 128], ident[:])
                        return ins
                    S.op("pe", fn, reads=[xskey, "ident"], writes=[("PS", bk)])
                    o = hTv[:, 4 * b4:4 * b4 + 4, t * 128:(t + 1) * 128]
                    i_ = PS[bk][:, :].rearrange("p (a n) -> p a n", a=4)
                    copy_evac(alt(), o, i_, [("PS", bk)], [(hkey, t, 4 * b4 + kk) for kk in range(4)])

            if loader is not None:
                for t in range(min(preload, nt)):
                    loader(t)
            stageA(0)
            for t in range(nt):
                if t + 1 < nt:
                    stageA(t + 1)
                stageB(t)
                if loader is not None and t + preload < nt:
                    loader(t + preload)

        def hkeys(hkey, ntiles, ks=range(16)):
            return [(hkey, t, k) for t in range(ntiles) for k in ks]

        def load_tiles(src_rows, dst_tiles, dst_keys, tag, eng="pool"):
            for t, (dst, key) in enumerate(zip(dst_tiles, dst_keys)):
                src = src_rows[t * 128:(t + 1) * 128, :]
                S.op(eng, lambda e, dst=dst, src=src: e.dma_start(out=dst, in_=src), writes=[key], dma=f"{tag}{t}")

        def gemm_F(w, c0, ncols, kc, inT, in_reads, ntok, cb, rhs_off=0):
            cpb = SLOTW // kc // 128
            nblk = ncols // (cpb * 128)
            for b in range(nblk):
                blk, wkey = wblock(w[:, c0 + b * cpb * 128: c0 + (b + 1) * cpb * 128], kc, cpb * 128)
                for cc in range(cpb):
                    bk = bank("all")
                    pairs = [(blk[:, k, cc * 128:(cc + 1) * 128], inT[:, k, rhs_off:rhs_off + ntok]) for k in range(kc)]
                    mm_group(bk, PS[bk][:, 0:ntok], pairs, reads=[wkey] + in_reads)
                    cb(bk, b * cpb + cc)

        def gemm_T(w, r0, kc, c0, ncols, inT, in_reads, ntiles, cb, pool="all"):
            kb = min(kc, SLOTW // 512)
            nkb = kc // kb
            for n in range(ncols // 512):
                bks = [bank(pool) for _ in range(ntiles)]
                for j in range(nkb):
                    blk, wkey = wblock(w[r0 + j * kb * 128: r0 + (j + 1) * kb * 128, c0 + n * 512: c0 + (n + 1) * 512], kb, 512)
                    for t in range(ntiles):
                        bk = bks[t]

                        def fn(e, blk=blk, t=t, bk=bk, j=j):
                            ins = None
                            for kk in range(kb):
                                k = j * kb + kk
                                ins = e.matmul(PS[bk][:, :], inT[:, k, t * 128:(t + 1) * 128], blk[:, kk, :],
                                               start=(k == 0), stop=(k == kc - 1))
                            return ins
                        S.op("pe", fn, reads=[wkey] + in_reads, writes=[("PS", bk)])
                        if j == nkb - 1:
                            cb(bk, t, n)

        hT_K = carve(0, 8192, F32R, "p (k n) -> p k n", k=16)
        KT = carve(8192, 8192, BF16, "p (h n) -> p h n", h=8)
        Vt = carve(16384, 8192, BF16, "p (t n) -> p t n", t=16)
        QT = carve(24576, 4096, BF16, "p (h n) -> p h n", h=8)
        QTf = [carve(28672, 512, F32), carve(29184, 512, F32)]
        stage = [carve(t * 2048, 2048, F32, base=ot_addr) for t in range(3)]
        gbcK = carve(6144, 2048, F32, base=ot_addr)
        PTb = None

        scale_att = 1.0 / np.sqrt(128.0)

        def glu_gemm(hTv, hreads, ntok, rhs_off, cb2):
            for c in range(8):
                blkA, kA = wblock(w_in[:, A0 + c * 128: A0 + (c + 1) * 128], 16, 128)
                blkG, kG = wblock(w_in[:, G0 + c * 128: G0 + (c + 1) * 128], 16, 128)
                ba = bank("all")
                mm_group(ba, PS[ba][:, 0:ntok], [(blkA[:, k, :], hTv[:, k, rhs_off:rhs_off + ntok]) for k in range(16)],
                         reads=[kA] + hreads)
                bg = bank("all")
                mm_group(bg, PS[bg][:, 0:ntok], [(blkG[:, k, :], hTv[:, k, rhs_off:rhs_off + ntok]) for k in range(16)],
                         reads=[kG] + hreads)
                cb2(ba, bg, c)

        def gate_ops(h, qf, qk, oc):
            gb = bank("all")

            def fn(e):
                ins = None
                for tt in range(4):
                    ins = e.matmul(PS[gb][:, tt * 8:(tt + 1) * 8], qf[:, tt * 128:(tt + 1) * 128], kmean[:, h, :], start=True, stop=True)
                return ins
            S.op("pe", fn, reads=[qk, "kmean"], writes=[("PS", gb)])
            gm = gtmp[:, 0:32]
            S.op("dve", lambda e: e.tensor_tensor(out=gm, in0=PS[gb][:, 0:32], in1=cp(C_GMASK + oc * 32, 32), op=ALU.add),
                 reads=[("PS", gb), "colp"], writes=["gm"])
            top8 = gtmp[:, 32:64]
            for tt in range(4):
                S.op("dve", lambda e, tt=tt: e.max(out=top8[:, tt * 8:(tt + 1) * 8], in_=gm[:, tt * 8:(tt + 1) * 8]),
                     reads=["gm"], writes=[("top8", tt)])
            thr = gtmp[:, 96:100]
            S.op("dve", lambda e: e.tensor_scalar(out=thr, in0=top8.rearrange("p (t n) -> p t n", n=8)[:, :, 2], scalar1=-1e29, scalar2=None, op0=ALU.max),
                 reads=[("top8", tt) for tt in range(4)], writes=["thr"])
            for tt in range(4):
                t = oc * 4 + tt
                o2 = sbt[:, (t * 8 + h) * 8:(t * 8 + h) * 8 + 8]
                S.op("dve", lambda e, tt=tt, o2=o2: e.tensor_scalar(out=o2, in0=gm[:, tt * 8:(tt + 1) * 8], scalar1=thr[:, tt:tt + 1], scalar2=-1.0, op0=ALU.is_ge, op1=ALU.add),
                     reads=["gm", "thr"], writes=[("sb", t, h)])

        sgt = [None, None]

        for c in range(4):
            S.mark(c)
            src = xp if c < 2 else xo
            r0 = (c % 2) * 512
            skeys = [("stage", t % 3) for t in range(4)]
            srows = src[r0:r0 + 512, :]

            def ld(t, srows=srows):
                dst = stage[t % 3]
                S.op("pool", lambda e: e.dma_start(out=dst, in_=srows[t * 128:(t + 1) * 128, :]), writes=[("stage", t % 3)], dma=f"xs{t % 3}")
            make_hT([stage[t % 3] for t in range(4)], skeys, gmix_d, gbcK, hT_K, "hT", loader=ld, preload=3)
            hr = hkeys("hT", 4)
            S.mark(c + 0.3)

            def cb_k(bk, h, c=c):
                o = KT[:, h, c * 512:(c + 1) * 512]
                bcol = cp(C_BIN + K0 // 128 + h)
                if KSUB <= 11:
                    return
                if not os.environ.get("KSKIPACT"):
                    S.op("act", lambda e: e.activation(out=o, in_=PS[bk][:, :], func=AF.Identity, bias=bcol),
                         reads=[("PS", bk), "colp"], writes=[("KT", h, c)])
                if KSUB <= 12:
                    return
                ks = gtmp[:, 64 + 2 * (h % 8):64 + 2 * (h % 8) + 2]
                S.op("dve", lambda e: e.tensor_reduce(out=ks, in_=PS[bk][:, :].rearrange("p (b t) -> p b t", b=2), axis=AX.X, op=ALU.add),
                     reads=[("PS", bk)], writes=[("ks", h)])
                if KSUB <= 13:
                    return
                S.op("act", lambda e: e.activation(out=kmean[:, h, 2 * c:2 * c + 2], in_=ks, func=AF.Identity, scale=1.0 / 256.0, bias=bcol),
                     reads=[("ks", h), "colp", "kmean"], writes=["kmean"])
            gemm_F(w_in, K0, 1024, 16, hT_K, hr, 512, cb_k)
            S.mark(c + 0.6)

            def cb_v(bk, t, n, c=c):
                o = Vt[:, c * 4 + t, n * 512:(n + 1) * 512]
                copy_evac(alt(), o, PS[bk][:, :], [("PS", bk)], [("V", c * 4 + t, n)])
            gemm_T(w_in, 0, 16, V0, 1024, hT_K, hr, 4, cb_v)

            if c == 1:
                def cb_halo(ba, bg, cch):
                    sg = QTf[0][:, 0:256]
                    S.op("act", lambda e: e.activation(out=sg, in_=PS[bg][:, 0:256], func=AF.Sigmoid, bias=cp(C_BIN + G0 // 128 + cch)),
                         reads=[("PS", bg), "colp"], writes=["halo_sg"])
                    hv = QTf[1][:, 0:256]
                    S.op("dve", lambda e: e.scalar_tensor_tensor(out=hv, in0=PS[ba][:, 0:256], scalar=cp(C_BIN + cch), in1=sg, op0=ALU.add, op1=ALU.mult),
                         reads=[("PS", ba), "halo_sg", "colp"], writes=["halo_v"])
                    S.op("dve", lambda e: e.tensor_scalar(out=vhalo[:, cch, :], in0=hv[:, 224:256], scalar1=cp(C_FLAG), scalar2=None, op0=ALU.mult),
                         reads=["halo_v", "colp"], writes=[("vhalo", cch)])
                glu_gemm(hT_K, hr, 256, 256, cb_halo)

            if c >= 2:
                oc = c - 2
                S.alias({"halo_sg", "halo_v"}, [("QTf", 0), ("QTf", 1)])

                def cb_q(bk, h, oc=oc):
                    o = QT[:, h, oc * 512:(oc + 1) * 512]
                    bcol = cp(C_BIN + Q0 // 128 + h)
                    S.op("act", lambda e: e.activation(out=o, in_=PS[bk][:, :], func=AF.Identity, bias=bcol),
                         reads=[("PS", bk), "colp"], writes=[("QT", h, oc)])
                    qf = QTf[h % 2]
                    qk = ("QTf", h % 2)
                    S.op("dve", lambda e: e.tensor_scalar(out=qf, in0=PS[bk][:, :], scalar1=bcol, scalar2=None, op0=ALU.add),
                         reads=[("PS", bk), "colp"], writes=[qk])
                    prev = pend_gate.pop() if pend_gate else None
                    pend_gate.append(lambda h=h, qf=qf, qk=qk, oc=oc: gate_ops(h, qf, qk, oc))
                    if prev is not None:
                        prev()
                pend_gate = []
                gemm_F(w_in, Q0, 1024, 16, hT_K, hr, 512, cb_q)
                while pend_gate:
                    pend_gate.pop()()
                gemm_F(w_in, Q0, 1024, 16, hT_K, hr, 512, cb_q)

        S.mark(4)
        S.alias({"hT", "stage", "gbc"}, [("PT", i) for i in range(4)] + [("rden", i) for i in range(2)] + [("otmp", i) for i in range(2)])
        PTb = [carve(i * 256, 256, BF16) for i in range(4)]
        rden = [carve(1024 + i * 512, 512, F32) for i in range(2)]
        otmp = [carve(2048 + i * 512, 512, F32) for i in range(2)]
        def prep_sT(h, pr):
            tb = bank("A")

            def fn(e):
                ins = None
                for e2 in range(4):
                    t = 4 * pr + e2
                    ins = e.transpose(PS[tb][0:8, e2 * 128:(e2 + 1) * 128], sbt[:, (t * 8 + h) * 8:(t * 8 + h) * 8 + 8], ident[:])
                return ins
            S.op("pe", fn, reads=[("sb", 4 * pr + e2, h) for e2 in range(4)] + ["ident"], writes=[("PS", tb)])
            copy_evac("dve", sbTb[0:8, 0, :], PS[tb][0:8, 0:512], [("PS", tb)], [("sbT", 0)])

        hj = 0
        ptc = {"n": 0}
        S.alias({"stage", "gbc"}, [("OT", h, p_) for h in range(8) for p_ in range(2)])
        for h in range(8):
            for pr in range(2):
                if hj == 0:
                    prep_sT(h, pr)
                sT = sbTb[:, 0, :]
                sTk = ("sbT", 0)
                ob = 4 + 2 * (hj % 2)
                db = ob + 1
                nblk = 6 + 2 * pr
                qsl = QT[:, h, pr * 512:(pr + 1) * 512]
                qk = ("QT", h, pr)
                visits = [(n, kt) for n in range(nblk) for kt in range(2)]

                def emit_S(v, h=h, pr=pr, qsl=qsl, qk=qk, sT=sT, sTk=sTk):
                    n, kt = v
                    sb_ = bank("A")
                    own0 = 4 + 2 * pr
                    own1 = 5 + 2 * pr

                    def fn(e):
                        o = PS[sb_][:, :]
                        o0 = PS[sb_][:, 0:256]
                        o1 = PS[sb_][:, 256:512]
                        e.matmul(o, KT[:, h, n * 256 + kt * 128: n * 256 + (kt + 1) * 128], qsl, start=True, stop=False)
                        if n < own0:
                            return e.matmul(o, esel[:, n * 128:(n + 1) * 128], sT, start=False, stop=True)
                        if n == own0:
                            e.matmul(o0, identb[:], cmask[:, kt * 256:(kt + 1) * 256], start=False, stop=False)
                            return e.matmul(o1, esel[:, n * 128:(n + 1) * 128], sT[:, 256:512], start=False, stop=True)
                        e.matmul(o0, esel[:, 8 * 128:9 * 128], sT[:, 0:256], start=False, stop=False)
                        return e.matmul(o1, identb[:], cmask[:, kt * 256:(kt + 1) * 256], start=False, stop=True)
                    S.op("pe", fn, reads=[("KT", h, n // 2), qk, "identb", "cmask", "esel", sTk], writes=[("PS", sb_)])
                    pi = ptc["n"] % 4
                    ptc["n"] += 1
                    pt = PTb[pi]
                    S.op("act", lambda e: e.activation(out=pt, in_=PS[sb_][:, :], func=AF.Exp, scale=float(scale_att)),
                         reads=[("PS", sb_)], writes=[("PT", pi)])
                    return pi

                def emit_PV(v, pi, first, last, h=h, ob=ob, db=db):
                    n, kt = v
                    pt = PTb[pi]

                    def fn(e):
                        e.matmul(PS[ob][:, :], Vt[:, n * 2 + kt, h * 128:(h + 1) * 128], pt, start=first, stop=last)
                        return e.matmul(PS[db][:, :], onesb[:], pt, start=first, stop=last)
                    S.op("pe", fn, reads=[("PT", pi), ("V", n * 2 + kt, h // 4), "onesb"], writes=[("PS", ob), ("PS", db)])
                nv = len(visits)
                pis = {}
                for v in range(min(2, nv)):
                    pis[v] = emit_S(visits[v])
                for v in range(nv):
                    if v + 2 < nv:
                        pis[v + 2] = emit_S(visits[v + 2])
                        if v + 2 == nv - 1 and hj + 1 < 16:
                            nh_, npr_ = divmod(hj + 1, 2)
                            prep_sT(nh_, npr_)
                    emit_PV(visits[v], pis[v], v == 0, v == nv - 1)
                rd_ = rden[hj % 2]
                ot_ = otmp[hj % 2]
                S.op("dve", lambda e, rd_=rd_, db=db: e.reciprocal(out=rd_, in_=PS[db][:, :]), reads=[("PS", db)], writes=[("rden", hj % 2)])
                S.op("dve", lambda e, rd_=rd_, ot_=ot_, ob=ob: e.tensor_tensor(out=ot_, in0=PS[ob][:, :], in1=rd_, op=ALU.mult),
                     reads=[("PS", ob), ("rden", hj % 2)], writes=[("otmp", hj % 2)])
                oo = OTr[:, h, pr * 512:(pr + 1) * 512]
                S.op("act", lambda e, oo=oo, ot_=ot_, h=h: e.activation(out=oo, in_=ot_, func=AF.Identity, bias=cp(C_BIN + V0 // 128 + h)),
                     reads=[("otmp", hj % 2), "colp"], writes=[("OT", h, pr)])
                hj += 1
        if DEBUG:
            S.op("pool", lambda e: e.dma_start(out=dbg_d["OT"], in_=OTr[:].bitcast(F32).rearrange("p h n -> p (h n)")),
                 reads=[("OT", h, p_) for h in range(8) for p_ in range(2)], writes=["dbgOT"], dma="dbgOT")
            S.op("pool", lambda e: e.dma_start(out=dbg_d["SB"], in_=sbt[:]),
                 reads=[("sb", t, h) for t in range(8) for h in range(8)], writes=["dbgSB"], dma="dbgSB")

        B1 = 0
        B2 = 8192
        B3 = 16384
        hT_P = carve(B1, 8192, F32R, "p (k n) -> p k n", k=16)
        X = carve(B1, 8192, F32, "p (t n) -> p t n", t=4)
        xstage = [carve(B2 + t * 2048, 2048, F32) for t in range(4)]
        mergedTr = carve(B2, 8192, F32R, "p (k n) -> p k n", k=16)
        ysq = carve(B2, 4096, F32, "p (c n) -> p c n", c=8)
        lnt = [carve(B2 + 4096 + i * 512, 512, F32) for i in range(4)]
        sgtmp = [carve(B2 + 6144 + i * 512, 512, F32) for i in range(2)]
        vglu = carve(B3, 8 * 544 // 2, BF16, "p (c n) -> p c n", c=8)
        dg = [carve(B2 + i * 2048, 1984, BF16, "p (k n) -> p k n", k=31) for i in range(2)]
        ycT = carve(B3 + 4352, 4096, F32, "p (c n) -> p c n", c=8)
        ycTr = carve(B3, 4096, F32R, "p (c n) -> p c n", c=8)
        mtmp = [carve(B3 + 8448 + i * 512, 512, F32) for i in range(2)]
        memst = [carve(B2 + t * 2048, 2048, F32) for t in range(2)]
        mT = carve(B2 + 4096, 4096, F32R, "p (k n) -> p k n", k=16)
        h2T = carve(B2, 8192, F32R, "p (k n) -> p k n", k=16)
        o2T = carve(B2, 8192, F32R, "p (k n) -> p k n", k=16)
        memKT = carve(B3, 2048, BF16, "p (c n) -> p c n", c=16)
        memV = carve(B3 + 2048, 2048, BF16, "p (t n) -> p t n", t=2)
        q2T = carve(B3 + 4096, 4096, BF16, "p (c n) -> p c n", c=16)
        xscr = [carve(B3 + 4096 + i * 2048, 2048, F32) for i in range(2)]
        P2T = [carve(B3 + 8192 + i * 256, 256, BF16) for i in range(2)]
        rden2 = [carve(B3 + 8704 + i * 512, 512, F32) for i in range(1)]
        h3T = carve(B2, 8192, F32R, "p (k n) -> p k n", k=16)
        actT = [carve(B3 + i * 2048, 2048, F32R, "p (c n) -> p c n", c=4) for i in range(2)]
        fsg = [carve(B3 + 4096 + i * 512, 512, F32) for i in range(2)]
        fscr = [carve(B3 + 5120 + i * 2048, 2048, F32) for i in range(2)]
        gfin = carve(B3, 2048, F32)
        obuf = [carve(B3 + 2048 + i * 2048, 2048, F32) for i in range(2)]

        gbcP = carve(25856, 2048, F32)
        scale_x = 1.0 / np.sqrt(512.0)
        ALLB = {"hT", "stage", "gbc", "PT", "rden", "otmp", "KT", "V", "QT", "QTf", "halo_sg", "halo_v"}
        PASSK = {"hTP", "X", "xstage", "mergedT", "ysq", "lnt", "sgtmp", "vglu", "vgh", "ycT", "ycTr", "dg", "dgk", "mtmp", "memst", "mT", "h2T", "o2T",
                 "memKT", "memV", "q2T", "xscr", "P2T", "rden2", "h3T", "actT", "fsg", "fscr", "gfin", "obuf"}

        REGS = {
            "B1": {"hTP", "X"},
            "B2": {"xstage", "mergedT", "ysq", "lnt", "sgtmp", "dg", "dgk", "memst", "mT", "h2T", "o2T", "h3T"},
            "B3": {"vglu", "vgh", "ycT", "ycTr", "mtmp", "memKT", "memV", "q2T", "xscr", "P2T", "rden2", "actT", "fsg", "fscr",
                   "gfin", "obuf"},
            "G": {"gbc", "junkP"},
        }
        REG_OF = {fam: r for r, fams in REGS.items() for fam in fams}
        pass_state = {"p": 0}

        def region_alias(newkeys):
            regs = set()
            for k in newkeys:
                fam = k[0] if isinstance(k, tuple) else k
                regs.add(REG_OF[fam])
            old = set()
            for r in regs:
                old |= REGS[r]
            if pass_state["p"] == 0:
                old |= ALLB
            if os.environ.get("KGLOBAL"):
                old = ALLB | PASSK | {"junkP"}
            S.alias(old, newkeys)

        junkP = carve(27904, 1024, BF16)
        for p in range(2):
            pass_state["p"] = p
            S.mark(5 + 10 * p)
            xk = [("xstage", t) for t in range(4)]
            region_alias(xk)
            load_tiles(xo[p * 512:(p + 1) * 512, :], [t_[:] for t_ in xstage], xk, "xp", eng="sp")
            region_alias(hkeys("hTP", 4))
            region_alias(["gbc", "junkP"])
            make_hT([t_[:] for t_ in xstage], xk, gmix_d, gbcP, hT_P, "hTP", junk=junkP, gload=(p == 0))
            hr = hkeys("hTP", 4)

            region_alias([("vglu", c) for c in range(8)] + [("sgtmp", i) for i in range(2)])
            for c in range(8):
                S.op("pool", lambda e, c=c: e.tensor_copy(out=vglu[:, c, 0:32], in_=vhalo[:, c, :]),
                     reads=[("vhalo", c)], writes=[("vgh", c)])

            def cb_glu(ba, bg, c):
                i = c % 2
                S.op("act", lambda e: e.activation(out=sgtmp[i], in_=PS[bg][:, :], func=AF.Sigmoid, bias=cp(C_BIN + G0 // 128 + c)),
                     reads=[("PS", bg), "colp"], writes=[("sgtmp", i)])
                S.op("dve", lambda e: e.scalar_tensor_tensor(out=vglu[:, c, 32:544], in0=PS[ba][:, :], scalar=cp(C_BIN + c), in1=sgtmp[i], op0=ALU.add, op1=ALU.mult),
                     reads=[("PS", ba), ("sgtmp", i), "colp"], writes=[("vglu", c)])
            glu_gemm(hT_P, hr, 512, 0, cb_glu)
            region_alias([("ycT", c) for c in range(8)] + [("dg", i) for i in range(2)] + [("dgk", i, k) for i in range(2) for k in range(1, 31)])
            for c in range(8):
                d_ = dg[c % 2]
                dk = ("dg", c % 2)
                S.op("dve", lambda e, d_=d_, c=c: e.tensor_scalar(out=d_[:, 0, :], in0=identb[:], scalar1=cp(C_CONVW + c * 31), scalar2=None, op0=ALU.mult),
                     reads=["identb", "colp"], writes=[dk])
                for k in range(1, 31):
                    eng = "dve" if k % 2 == 0 else "pool"
                    S.op(eng, lambda e, d_=d_, c=c, k=k: e.tensor_scalar(out=d_[:, k, :], in0=identb[:], scalar1=cp(C_CONVW + c * 31 + k), scalar2=1.0, op0=ALU.mult, op1=ALU.mult),
                         reads=["identb", "colp"], writes=[("dgk", c % 2, k)])
                bk = bank("all")
                mm_group(bk, PS[bk][:, :], [(d_[:, k, :], vglu[:, c, 2 + k:514 + k]) for k in range(31)],
                         reads=[dk] + [("dgk", c % 2, k) for k in range(1, 31)] + [("vglu", c), ("vgh", c)])
                S.op("act", lambda e, c=c, bk=bk: e.activation(out=ycT[:, c, :], in_=PS[bk][:, :], func=AF.Identity, bias=cp(C_CONVB + c)),
                     reads=[("PS", bk), "colp"], writes=[("ycT", c)])
                if p == 0:
                    S.op("pool", lambda e, c=c: e.tensor_copy(out=vhalo[:, c, :], in_=vglu[:, c, 512:544]),
                         reads=[("vglu", c)], writes=[("vhalo", c)])
            region_alias([("ysq", c) for c in range(8)] + [("lnt", i) for i in range(4)])
            for c in range(8):
                S.op("act", lambda e, c=c: e.activation(out=ysq[:, c, :], in_=ycT[:, c, :], func=AF.Square),
                     reads=[("ycT", c)], writes=[("ysq", c)])
            bm = bank("all")
            mm_group(bm, PS[bm][:, :], [(onesf[:], ycT[:, c, :]) for c in range(8)], reads=["onesf"] + [("ycT", c) for c in range(8)])
            be = bank("all")
            mm_group(be, PS[be][:, :], [(onesf[:], ysq[:, c, :]) for c in range(8)], reads=["onesf"] + [("ysq", c) for c in range(8)])
            mean_sb, msq, var_, rstd_ = lnt
            S.op("act", lambda e, bm=bm: e.activation(out=mean_sb, in_=PS[bm][:, :], func=AF.Copy), reads=[("PS", bm)], writes=[("lnt", 0)])
            S.op("dve", lambda e: e.tensor_tensor(out=msq, in0=mean_sb, in1=mean_sb, op=ALU.mult), reads=[("lnt", 0)], writes=[("lnt", 1)])
            S.op("dve", lambda e, be=be: e.tensor_tensor(out=var_, in0=PS[be][:, :], in1=msq, op=ALU.subtract), reads=[("PS", be), ("lnt", 1)], writes=[("lnt", 2)])
            S.op("act", lambda e: e.activation(out=var_, in_=var_, func=AF.Sqrt, bias=epsc[:]), reads=[("lnt", 2), "epsc"], writes=[("lnt", 2)])
            S.op("dve", lambda e: e.reciprocal(out=rstd_, in_=var_), reads=[("lnt", 2)], writes=[("lnt", 3)])
            region_alias([("ycTr", c) for c in range(8)])
            for c in range(8):
                y = ycT[:, c, :]
                S.op("dve", lambda e, y=y: e.tensor_tensor(out=y, in0=y, in1=mean_sb, op=ALU.subtract), reads=[("ycT", c), ("lnt", 0)], writes=[("ycT", c)])
                S.op("dve", lambda e, y=y: e.tensor_tensor(out=y, in0=y, in1=rstd_, op=ALU.mult), reads=[("ycT", c), ("lnt", 3)], writes=[("ycT", c)])
                S.op("act", lambda e, y=y, c=c: e.activation(out=ycTr[:, c, :], in_=y, func=AF.Silu, scale=cp(C_LNG + c), bias=cp(C_LNB + c)),
                     reads=[("ycT", c), "colp"], writes=[("ycTr", c)])
            if DEBUG and p == 0:
                S.op("pool", lambda e: e.dma_start(out=dbg_d["YC"], in_=ycTr[:].bitcast(F32).rearrange("p c n -> p (c n)")),
                     reads=[("ycTr", c) for c in range(8)], writes=["dbgYC"], dma="dbgYC")

            S.mark(6 + 10 * p)
            region_alias([("mergedT", f) for f in range(16)] + [("mtmp", i) for i in range(2)])
            for sweep in range(2):
                wA = w_conv_out if sweep == 0 else w_att_out
                g0 = GC0 if sweep == 0 else GA0
                for i in range(8):
                    blkY, kY = wblock(wA[:, i * 256:(i + 1) * 256], 8, 256)
                    for cc in range(2):
                        if True:
                            f = 2 * i + cc
                            blkG, kG = wblock(w_in[:, g0 + f * 128: g0 + (f + 1) * 128], 16, 128)
                            by = bank("all")
                            if sweep == 0:
                                pairs = [(blkY[:, k, cc * 128:(cc + 1) * 128], ycTr[:, k, :]) for k in range(8)]
                                rd = [kY] + [("ycTr", k) for k in range(8)]
                            else:
                                pairs = [(blkY[:, k, cc * 128:(cc + 1) * 128], OTr[:, k, p * 512:(p + 1) * 512]) for k in range(8)]
                                rd = [kY] + [("OT", k, p) for k in range(8)]
                            mm_group(by, PS[by][:, :], pairs, reads=rd)
                            bg = bank("all")
                            mm_group(bg, PS[bg][:, :], [(blkG[:, k, :], hT_P[:, k, :]) for k in range(16)], reads=[kG] + hr)
                            mt_ = mtmp[f % 2]
                            mk = ("mtmp", f % 2)
                            S.op("act", lambda e, mt_=mt_, bg=bg, f=f, g0=g0: e.activation(out=mt_, in_=PS[bg][:, :], func=AF.Sigmoid, bias=cp(C_BIN + g0 // 128 + f)),
                                 reads=[("PS", bg), "colp"], writes=[mk])
                            if sweep == 0:
                                S.op("dve", lambda e, mt_=mt_, by=by, f=f: e.tensor_tensor(out=mergedTr[:, f, :], in0=PS[by][:, :], in1=mt_, op=ALU.mult),
                                     reads=[("PS", by), mk], writes=[("mergedT", f)])
                            else:
                                S.op("dve", lambda e, mt_=mt_, by=by: e.tensor_tensor(out=mt_, in0=PS[by][:, :], in1=mt_, op=ALU.mult),
                                     reads=[("PS", by), mk], writes=[mk])
                                S.op("dve", lambda e, mt_=mt_, f=f: e.tensor_tensor(out=mergedTr[:, f, :], in0=mergedTr[:, f, :].bitcast(F32), in1=mt_, op=ALU.add),
                                     reads=[("mergedT", f), mk], writes=[("mergedT", f)])

            S.mark(7 + 10 * p)
            region_alias([("X", t) for t in range(4)])
            Xk = [("X", t) for t in range(4)]
            load_tiles(xo[p * 512:(p + 1) * 512, :], [X[:, t, :] for t in range(4)], Xk, "xr")

            def cb_res(bk, t, n):
                xs_ = X[:, t, n * 512:(n + 1) * 512]
                S.op("dve", lambda e: e.tensor_tensor(out=xs_, in0=PS[bk][:, :], in1=xs_, op=ALU.add),
                     reads=[("PS", bk), ("X", t)], writes=[("X", t)])
            gemm_T(w_out, 0, 16, 0, D, mergedTr, [("mergedT", f) for f in range(16)], 4, cb_res)
            if DEBUG and p == 0:
                S.op("pool", lambda e: e.dma_start(out=dbg_d["X1"], in_=X[:].rearrange("p t n -> p (t n)")),
                     reads=Xk, writes=["dbgX1"], dma="dbgX1")

            S.mark(8 + 10 * p)
            mk_ = [("memst", t) for t in range(2)]
            region_alias(mk_ + hkeys("mT", 2))
            load_tiles(memd[:, :], [t_[:] for t_ in memst], mk_, "ms")
            make_hT([t_[:] for t_ in memst], mk_, gmem_d, gbcP, mT, "mT")
            mr = hkeys("mT", 2)
            region_alias([("memKT", c) for c in range(16)] + [("memV", t, n) for t in range(2) for n in range(4)])

            def cb_mk(bk, c):
                copy_evac(alt(), memKT[:, c, :], PS[bk][:, 0:256], [("PS", bk)], [("memKT", c)])
            gemm_F(w_ckv, 0, D, 16, mT, mr, 256, cb_mk)

            def cb_mv(bk, t, n):
                copy_evac(alt(), memV[:, t, n * 512:(n + 1) * 512], PS[bk][:, :], [("PS", bk)], [("memV", t, n)])
            gemm_T(w_ckv, 0, 16, D, D, mT, mr, 2, cb_mv)

            region_alias(hkeys("h2T", 4) + [("xscr", i) for i in range(2)])
            make_hT([X[:, t, :] for t in range(4)], Xk, gcross_d, gbcP, h2T, "h2T", scratch=[t_[:] for t_ in xscr], scratch_keys=[("xscr", i) for i in range(2)])
            region_alias([("q2T", c) for c in range(16)])

            def cb_q2(bk, c):
                copy_evac(alt(), q2T[:, c, :], PS[bk][:, :], [("PS", bk)], [("q2T", c)])
            gemm_F(w_cq, 0, D, 16, h2T, hkeys("h2T", 4), 512, cb_q2)
            region_alias([("o2T", c) for c in range(16)] + [("P2T", i) for i in range(2)] + [("rden2", 0)])
            for hh in range(4):
                for mt in range(2):
                    sb_ = bank("all")
                    mm_group(sb_, PS[sb_][:, :], [(memKT[:, 4 * hh + dc, mt * 128:(mt + 1) * 128], q2T[:, 4 * hh + dc, :]) for dc in range(4)],
                             reads=[("memKT", 4 * hh + dc) for dc in range(4)] + [("q2T", 4 * hh + dc) for dc in range(4)])
                    S.op("act", lambda e, sb_=sb_, mt=mt: e.activation(out=P2T[mt], in_=PS[sb_][:, :], func=AF.Exp, scale=float(scale_x)),
                         reads=[("PS", sb_)], writes=[("P2T", mt)])
                db_ = bank("all")
                mm_group(db_, PS[db_][:, :], [(onesb[:], P2T[mt]) for mt in range(2)], reads=["onesb", ("P2T", 0), ("P2T", 1)])
                S.op("dve", lambda e, db_=db_: e.reciprocal(out=rden2[0], in_=PS[db_][:, :]), reads=[("PS", db_)], writes=[("rden2", 0)])
                for dc in range(4):
                    c = 4 * hh + dc
                    ob_ = bank("all")
                    mm_group(ob_, PS[ob_][:, :], [(memV[:, mt, c * 128:(c + 1) * 128], P2T[mt]) for mt in range(2)],
                             reads=[("P2T", 0), ("P2T", 1)] + [("memV", mt, c // 4) for mt in range(2)])
                    S.op("dve", lambda e, ob_=ob_, c=c: e.tensor_tensor(out=o2T[:, c, :], in0=PS[ob_][:, :], in1=rden2[0], op=ALU.mult),
                         reads=[("PS", ob_), ("rden2", 0)], writes=[("o2T", c)])
            gemm_T(w_co, 0, 16, 0, D, o2T, [("o2T", c) for c in range(16)], 4, cb_res)
            if DEBUG and p == 0:
                S.op("pool", lambda e: e.dma_start(out=dbg_d["X2"], in_=X[:].rearrange("p t n -> p (t n)")),
                     reads=Xk, writes=["dbgX2"], dma="dbgX2")

            S.mark(9 + 10 * p)
            region_alias(hkeys("h3T", 4) + [("fscr", i) for i in range(2)])
            make_hT([X[:, t, :] for t in range(4)], Xk, gffn_d, gbcP, h3T, "h3T", scratch=[t_[:] for t_ in fscr], scratch_keys=[("fscr", i) for i in range(2)])
            h3r = hkeys("h3T", 4)
            if p == 0:
                S.op("pool", lambda e: e.dma_start(out=gbcP, in_=gmix_d.partition_broadcast(128)), writes=["gbc"], dma="gbc")
            region_alias([("actT", i, fc) for i in range(2) for fc in range(4)] + [("fsg", i) for i in range(2)])
            for fg in range(11):
                ab = fg % 2
                for fc in range(4):
                    if True:
                        blkG, kG = wblock(w_ffn_in[:, fg * 512 + fc * 128: fg * 512 + (fc + 1) * 128], 16, 128)
                        blkU, kU = wblock(w_ffn_in[:, DFF + fg * 512 + fc * 128: DFF + fg * 512 + (fc + 1) * 128], 16, 128)
                        bg = bank("A")
                        mm_group(bg, PS[bg][:, :], [(blkG[:, k, :], h3T[:, k, :]) for k in range(16)], reads=[kG] + h3r)
                        bu = bank("A")
                        mm_group(bu, PS[bu][:, :], [(blkU[:, k, :], h3T[:, k, :]) for k in range(16)], reads=[kU] + h3r)
                        sg_ = fsg[fc % 2]
                        S.op("act", lambda e, sg_=sg_, bg=bg: e.activation(out=sg_, in_=PS[bg][:, :], func=AF.Silu),
                             reads=[("PS", bg)], writes=[("fsg", fc % 2)])
                        S.op("dve", lambda e, sg_=sg_, bu=bu, ab=ab, fc=fc: e.tensor_tensor(out=actT[ab][:, fc, :], in0=PS[bu][:, :], in1=sg_, op=ALU.mult),
                             reads=[("PS", bu), ("fsg", fc % 2)], writes=[("actT", ab, fc)])
                for n4 in range(4):
                    blkO, kO = wblock(w_ffn_out[fg * 512:(fg + 1) * 512, n4 * 512:(n4 + 1) * 512], 4, 512)
                    for t in range(4):
                        bo = bank("B")
                        mm_group(bo, PS[bo][:, :], [(actT[ab][:, fc, t * 128:(t + 1) * 128], blkO[:, fc, :]) for fc in range(4)],
                                 reads=[kO] + [("actT", ab, fc) for fc in range(4)])
                        cb_res(bo, t, n4)

            S.mark(10 + 10 * p)
            region_alias(["gfin"] + [("obuf", i) for i in range(2)])
            S.op("pool", lambda e: e.dma_start(out=gfin[:], in_=gfin_d.partition_broadcast(128)), writes=["gfin"], dma="gfin")
            for t in range(4):
                sc = stat[:, 16 + 2 * t:16 + 2 * t + 1]
                rs = stat[:, 16 + 2 * t + 1:16 + 2 * t + 2]
                skey = ("fstat", t)
                xt = X[:, t, :]
                ob_ = obuf[t % 2]
                S.op("act", lambda e, xt=xt, sc=sc, ob_=ob_: e.activation(out=ob_, in_=xt, func=AF.Square, accum_out=sc),
                     reads=[("X", t)], writes=[("obuf", t % 2), skey])
                S.op("act", lambda e, sc=sc, rs=rs: e.activation(out=rs, in_=sc, func=AF.Sqrt, scale=1.0 / D, bias=epsc[:]),
                     reads=[skey, "epsc"], writes=[skey])
                S.op("dve", lambda e, rs=rs: e.reciprocal(out=rs, in_=rs), reads=[skey], writes=[skey])
                S.op("dve", lambda e, ob_=ob_, xt=xt, rs=rs: e.scalar_tensor_tensor(out=ob_, in0=xt, scalar=rs, in1=gfin[:], op0=ALU.mult, op1=ALU.mult),
                     reads=[("X", t), skey, "gfin"], writes=[("obuf", t % 2)])
                dst = out_d[p * 512 + t * 128: p * 512 + (t + 1) * 128, :]
                S.op("pool", lambda e, ob_=ob_, dst=dst: e.dma_start(out=dst, in_=ob_), reads=[("obuf", t % 2)], writes=[("outd", p, t)], dma=f"out{t % 2}")

        S.stopped = False
        S.final_waits("pool")

        with nc.Block() as block:
            @block.tensor
            def _(e):
                S.replay("pe", e)

            @block.scalar
            def _(e):
                S.replay("act", e)

            @block.vector
            def _(e):
                S.replay("dve", e)

            @block.gpsimd
            def _(e):
                S.replay("pool", e)

            @block.sync
            def _(e):
                S.replay("sp", e)
    return nc


_CACHE = {}


def _consts():
    bf = ml_dtypes.bfloat16
    ident = np.eye(128, dtype=np.float32)
    onesf = np.full((128, 128), 1.0 / 1024.0, dtype=np.float32)
    identb = np.eye(128).astype(bf)
    onesb = np.ones((128, 128)).astype(bf)
    cm = np.zeros((128, 2, 256), dtype=np.float32)
    for kt in range(2):
        key = kt * 128 + np.arange(128)[:, None]
        q = np.arange(256)[None, :]
        cm[:, kt, :] = np.where(key <= q, 0.0, NEG)
    cmask = cm.reshape(128, 512).astype(bf)
    es = np.zeros((128, 9, 128), dtype=np.float32)
    for n in range(9):
        es[n, n, :] = -NEG
    esel = es.reshape(128, 9 * 128).astype(bf)
    return dict(ident=ident, onesf=onesf, identb=identb, onesb=onesb, cmask=cmask, esel=esel)


def kernel(x, mem, norm_mix_g, w_in, b_in, conv_w, conv_b, conv_ln_g, conv_ln_b,
           w_conv_out, w_att_out, w_out, norm_cross_g, norm_mem_g, w_cq, w_ckv,
           w_co, norm_ffn_g, w_ffn_in, w_ffn_out, norm_final_g):
    f = lambda a: np.ascontiguousarray(np.asarray(a, dtype=np.float32))
    x = f(x); mem = f(mem)
    if "nc" not in _CACHE:
        _CACHE["nc"] = build_program()
    nc = _CACHE["nc"]
    consts = _consts()

    def col(v, k):
        return np.asarray(v, np.float32).reshape(k, 128).T

    shared = dict(
        w_in=f(w_in[0]), w_conv_out=f(w_conv_out[0]), w_att_out=f(w_att_out[0]), w_out=f(w_out[0]),
        w_cq=f(w_cq[0]), w_ckv=f(w_ckv[0]), w_co=f(w_co[0]), w_ffn_in=f(w_ffn_in[0]), w_ffn_out=f(w_ffn_out[0]),
        gfin=f(norm_final_g), gmix=f(norm_mix_g[0]), gcross=f(norm_cross_g[0]), gmem=f(norm_mem_g[0]),
        gffn=f(norm_ffn_g[0]), **consts)
    base = np.zeros((128, NCOLP), np.float32)
    base[:, C_BIN:C_BIN + 72] = col(b_in[0], 72)
    base[:, C_GMIX:C_GMIX + 16] = col(norm_mix_g[0], 16)
    base[:, C_GCROSS:C_GCROSS + 16] = col(norm_cross_g[0], 16)
    base[:, C_GMEM:C_GMEM + 16] = col(norm_mem_g[0], 16)
    base[:, C_GFFN:C_GFFN + 16] = col(norm_ffn_g[0], 16)
    cw = np.asarray(conv_w[0], np.float32)
    base[:, C_CONVW:C_CONVW + 248] = cw.reshape(31, 8, 128).transpose(2, 1, 0).reshape(128, 248)
    base[:, C_CONVB:C_CONVB + 8] = col(conv_b[0], 8)
    base[:, C_LNG:C_LNG + 8] = col(conv_ln_g[0], 8)
    base[:, C_LNB:C_LNB + 8] = col(conv_ln_b[0], 8)
    in_maps = []
    for core in range(8):
        b, half = core // 2, core % 2
        cpm = base.copy()
        cpm[:, C_FLAG] = float(half)
        gm = np.zeros((8, 8), np.float32)
        for t in range(8):
            for n in range(8):
                valid = (n < 4 + t // 2) and (n >= 4 or half == 1)
                gm[t, n] = 0.0 if valid else -1e30
        cpm[:, C_GMASK:C_GMASK + 64] = gm.reshape(1, 64)
        m = dict(shared)
        m["xo"] = np.ascontiguousarray(x[b, half * 1024:(half + 1) * 1024])
        m["xp"] = np.ascontiguousarray(x[b, 0:1024])
        m["mem"] = np.ascontiguousarray(mem[b])
        m["colp"] = cpm
        in_maps.append(m)
    declared = set()
    for alloc in nc.allocations:
        if isinstance(alloc, mybir.MemoryLocationSet) and alloc.kind == "ExternalInput":
            declared.add(alloc.memorylocations[0].name)
    in_maps = [{k: v for k, v in m.items() if k in declared} for m in in_maps]
    cores = [int(c) for c in KCORES.split(",")] if KCORES else list(range(8))
    res = run_bass_kernel_spmd(nc, [in_maps[c] for c in cores], core_ids=list(range(len(cores))))
    _CACHE["last"] = res
    out = np.zeros((4, 2048, 2048), np.float32)
    for i, core in enumerate(cores):
        b, half = core // 2, core % 2
        out[b, half * 1024:(half + 1) * 1024] = res.results[i]["out"]
    return out
```

```python
import numpy as np
import ml_dtypes
import concourse.bass as bass
import concourse.mybir as mybir
from concourse.bass_utils import run_bass_kernel_spmd

F32 = mybir.dt.float32
F32R = mybir.dt.float32r
BF16 = mybir.dt.bfloat16
AF = mybir.ActivationFunctionType
ALU = mybir.AluOpType
AX = mybir.AxisListType

D = 2048
S_OWN = 1024
TCH = 512
A0, G0, Q0, K0, V0, GC0, GA0 = 0, 1024, 2048, 3072, 4096, 5120, 7168
DFF = 5632
EPS = 1e-6
NEG = -32768.0

C_BIN = 0
C_GMIX = 72
C_GCROSS = 88
C_GMEM = 104
C_GFFN = 120
C_CONVW = 136
C_CONVB = 384
C_LNG = 392
C_LNB = 400
C_FLAG = 408
C_GMASK = 409
NCOLP = 473

DEBUG = False
NSLOT = 6
SLOTW = 2048
import os
KSTAGE = float(os.environ.get("KSTAGE", "99"))
KCORES = os.environ.get("KCORES", "")
KSUB = int(os.environ.get("KSUB", "99"))


class Sched:
    def __init__(self, nc, sems):
        self.nc = nc
        self.free_sems = list(sems)
        self.engs = ["pe", "act", "dve", "pool", "sp"]
        self.streams = {e: [] for e in self.engs}
        self.esem = {e: self.free_sems.pop() for e in self.engs}
        self.ecnt = {e: 0 for e in self.engs}
        self.waited = {e: {} for e in self.engs}
        self.keys = {}
        self.dsem = {}
        self.dcnt = {}
        self.semobj = {}
        for e in self.engs:
            self.semobj[id(self.esem[e])] = self.esem[e]

    def _dma_sem(self, name):
        if name not in self.dsem:
            s = self.free_sems.pop()
            self.dsem[name] = s
            self.dcnt[name] = 0
            self.semobj[id(s)] = s
        return self.dsem[name]

    def mark(self, n):
        if n >= KSTAGE:
            self.stopped = True

    def op(self, eng, fn, reads=(), writes=(), dma=None):
        if getattr(self, "stopped", False):
            return None
        need = {}

        def merge(d):
            for sid, v in d.items():
                if need.get(sid, 0) < v:
                    need[sid] = v
        for k in reads:
            if k in self.keys:
                if isinstance(k, tuple) and k[0] == "PS" and self.keys[k][1]:
                    merge(self.keys[k][1])
                else:
                    merge(self.keys[k][0])
        for k in writes:
            if k in self.keys:
                merge(self.keys[k][0])
                merge(self.keys[k][1])
        if dma is not None:
            sem = self._dma_sem(dma)
            self.dcnt[dma] += 16
            ev = (id(sem), self.dcnt[dma])
            inc = 16
        else:
            sem = self.esem[eng]
            self.ecnt[eng] += 1
            ev = (id(sem), self.ecnt[eng])
            inc = 1
        waits = []
        own = id(self.esem[eng])
        for sid, v in need.items():
            if eng == "pe" and sid == own:
                continue
            if self.waited[eng].get(sid, 0) < v:
                waits.append((self.semobj[sid], v))
                self.waited[eng][sid] = v
        self.streams[eng].append((fn, waits, (sem, inc)))
        for k in reads:
            ent = self.keys.setdefault(k, ({}, {}))
            if isinstance(k, tuple) and k[0] == "PS":
                ent[1].clear()
            if ent[1].get(ev[0], 0) < ev[1]:
                ent[1][ev[0]] = ev[1]
        for k in writes:
            self.keys[k] = ({ev[0]: ev[1]}, {})
        return ev

    def alias(self, old_prefixes, new_keys):
        merged = {}
        for k, (w, r) in self.keys.items():
            name = k[0] if isinstance(k, tuple) else k
            if name in old_prefixes:
                for d in (w, r):
                    for sid, v in d.items():
                        if merged.get(sid, 0) < v:
                            merged[sid] = v
        for k in new_keys:
            ent = self.keys.get(k)
            if ent is None:
                self.keys[k] = (dict(merged), {})
            else:
                for sid, v in merged.items():
                    if ent[0].get(sid, 0) < v:
                        ent[0][sid] = v

    def final_waits(self, eng):
        need = {}
        for e in self.engs:
            if self.ecnt[e] > 0:
                need[id(self.esem[e])] = self.ecnt[e]
        for name, s in self.dsem.items():
            need[id(s)] = self.dcnt[name]
        waits = [(self.semobj[sid], v) for sid, v in need.items() if sid != id(self.esem[eng])]
        self.streams[eng].append((None, waits, None))

    def replay(self, name, eng):
        for fn, waits, inc in self.streams[name]:
            for s_, v in waits:
                eng.wait_ge(s_, v)
            if fn is None:
                continue
            ins = fn(eng)
            ins.then_inc(inc[0], inc[1])


def build_program():
    nc = bass.Bass("TRN2", target_bir_lowering=False)
    nc.dge_precook = False

    class Lazy:
        def __init__(self, name, shape, dt):
            self.a = (name, list(shape), dt)
            self.v = None

        def ap(self):
            if self.v is None:
                self.v = nc.dram_tensor(self.a[0], self.a[1], self.a[2], kind="ExternalInput").ap()
            return self.v

        def __getitem__(self, k):
            return self.ap()[k]

        def partition_broadcast(self, n):
            return self.ap().partition_broadcast(n)

    def din(name, shape, dt):
        return Lazy(name, shape, dt)

    xo = din("xo", [S_OWN, D], F32)
    xp = din("xp", [S_OWN, D], F32)
    memd = din("mem", [256, D], F32)
    w_in = din("w_in", [D, 9216], F32R)
    w_conv_out = din("w_conv_out", [1024, D], F32R)
    w_att_out = din("w_att_out", [1024, D], F32R)
    w_out = din("w_out", [D, D], F32R)
    w_cq = din("w_cq", [D, D], F32R)
    w_ckv = din("w_ckv", [D, 2 * D], F32R)
    w_co = din("w_co", [D, D], F32R)
    w_ffn_in = din("w_ffn_in", [D, 2 * DFF], F32R)
    w_ffn_out = din("w_ffn_out", [DFF, D], F32R)
    colp_d = din("colp", [128, NCOLP], F32)
    gfin_d = din("gfin", [D], F32)
    gmix_d = din("gmix", [D], F32)
    gcross_d = din("gcross", [D], F32)
    gmem_d = din("gmem", [D], F32)
    gffn_d = din("gffn", [D], F32)
    ident_d = din("ident", [128, 128], F32)
    onesf_d = din("onesf", [128, 128], F32)
    identb_d = din("identb", [128, 128], BF16)
    onesb_d = din("onesb", [128, 128], BF16)
    cmask_d = din("cmask", [128, 512], BF16)
    esel_d = din("esel", [128, 9 * 128], BF16)
    out_d = nc.dram_tensor("out", [S_OWN, D], F32, kind="ExternalOutput").ap()
    dbg_d = {}
    if DEBUG:
        dbg_d["OT"] = nc.dram_tensor("dbg_OT", [128, 8 * 1024], F32, kind="ExternalOutput").ap()
        dbg_d["X1"] = nc.dram_tensor("dbg_X1", [128, 4 * 2048], F32, kind="ExternalOutput").ap()
        dbg_d["X2"] = nc.dram_tensor("dbg_X2", [128, 4 * 2048], F32, kind="ExternalOutput").ap()
        dbg_d["YC"] = nc.dram_tensor("dbg_YC", [128, 8 * 512], F32, kind="ExternalOutput").ap()
        dbg_d["SB"] = nc.dram_tensor("dbg_SB", [128, 512], F32, kind="ExternalOutput").ap()

    import contextlib
    es = contextlib.ExitStack()
    with es:
        REGW = 29696
        reg = es.enter_context(nc.sbuf_tensor("sb_reg", [128, REGW], F32))
        wring = es.enter_context(nc.sbuf_tensor("sb_wring", [128, NSLOT, SLOTW], F32R))
        OTr = es.enter_context(nc.sbuf_tensor("sb_OT", [128, 8, 1024], F32R))
        colp = es.enter_context(nc.sbuf_tensor("sb_colp", [128, NCOLP], F32))
        ident = es.enter_context(nc.sbuf_tensor("sb_ident", [128, 128], F32))
        onesf = es.enter_context(nc.sbuf_tensor("sb_onesf", [128, 128], F32))
        identb = es.enter_context(nc.sbuf_tensor("sb_identb", [128, 128], BF16))
        onesb = es.enter_context(nc.sbuf_tensor("sb_onesb", [128, 128], BF16))
        cmask = es.enter_context(nc.sbuf_tensor("sb_cmask", [128, 512], BF16))
        esel = es.enter_context(nc.sbuf_tensor("sb_esel", [128, 9 * 128], BF16))
        epsc = es.enter_context(nc.sbuf_tensor("sb_epsc", [128, 1], F32))
        stat = es.enter_context(nc.sbuf_tensor("sb_stat", [128, 64], F32))
        kmean = es.enter_context(nc.sbuf_tensor("sb_kmean", [128, 8, 8], F32))
        sbt = es.enter_context(nc.sbuf_tensor("sb_sbt", [128, 8 * 8 * 8], F32))
        gtmp = es.enter_context(nc.sbuf_tensor("sb_gtmp", [128, 128], F32))
        vhalo = es.enter_context(nc.sbuf_tensor("sb_vhalo", [128, 8, 32], F32))
        sbTb = es.enter_context(nc.sbuf_tensor("sb_sbTb", [128, 1, 512], BF16))
        PS = [es.enter_context(nc.psum_tensor(f"ps{i}", [128, 512], F32)) for i in range(8)]
        sems = [es.enter_context(nc.semaphore(f"s{i}")) for i in range(48)]
        S = Sched(nc, sems)
        reg_addr = nc.lookup_mloc(reg).addr
        ot_addr = nc.lookup_mloc(OTr).addr
        cnames = {}

        def carve(off_w, nwords, dt, pat=None, base=None, **kw):
            esz = 2 if dt is BF16 else 4
            nel = nwords * 4 // esz
            if pat:
                (dk, dv), = kw.items()
                shape = [128, dv, nel // dv]
            else:
                shape = [128, nel]
            i = cnames.get("n", 0)
            cnames["n"] = i + 1
            return nc.alloc_sbuf_tensor_at(f"cv{i}", shape, dt, offset=(reg_addr if base is None else base) + off_w * 4)[:]

        def cp(c, n=1):
            return colp[:, c:c + n]

        for nm, t_, d_ in (("colp", colp, colp_d), ("ident", ident, ident_d), ("onesf", onesf, onesf_d),
                           ("identb", identb, identb_d), ("onesb", onesb, onesb_d), ("cmask", cmask, cmask_d),
                           ("esel", esel, esel_d)):
            S.op("pool", (lambda e, t_=t_, d_=d_: e.dma_start(out=t_[:], in_=d_[:, :])), writes=[nm], dma="c_" + nm)
        S.op("dve", lambda e: e.memset(kmean[:], 0.0), writes=["kmean"])
        S.op("dve", lambda e: e.memset(sbTb[:], 0.0), writes=[("sbT", 0)])
        S.op("dve", lambda e: e.memset(sbTb[0:16, :, :], -1.0), writes=[("sbT", 0)])
        S.op("dve", lambda e: e.memset(epsc[:], EPS), writes=["epsc"])

        rot_state = {"all": 0, "A": 0, "B": 0}

        def bank(pool="all"):
            if pool == "all":
                i = rot_state["all"] % 8
            elif pool == "A":
                i = rot_state["A"] % 4
            else:
                i = 4 + rot_state["B"] % 4
            rot_state[pool] += 1
            return i

        wstate = {"n": 0}

        def wblock(w2d, kc, ncols):
            slot = wstate["n"] % NSLOT
            assert kc * ncols <= SLOTW
            wstate["n"] += 1
            view = wring[:, slot, 0:kc * ncols].rearrange("p (k n) -> p k n", k=kc)
            src = w2d.rearrange("(k p) n -> p k n", p=128)
            key = ("W", slot)
            S.op("sp", (lambda e: e.dma_start(out=view, in_=src)), writes=[key], dma=f"w{slot}")
            return view, key

        def mm_group(bk, out_ap, pairs, reads):
            n = len(pairs)

            def fn(e):
                ins = None
                for i, (l, r) in enumerate(pairs):
                    ins = e.matmul(out_ap, l, r, start=(i == 0), stop=(i == n - 1))
                return ins
            S.op("pe", fn, reads=reads, writes=[("PS", bk)])

        flip = {"n": 0}

        def alt():
            flip["n"] += 1
            return "act" if flip["n"] % 2 else "dve"

        def copy_evac(eng, out_ap, in_ap, reads, writes):
            if eng == "act":
                S.op("act", lambda e: e.activation(out=out_ap, in_=in_ap, func=AF.Copy), reads=reads, writes=writes)
            else:
                S.op("dve", lambda e: e.tensor_copy(out=out_ap, in_=in_ap), reads=reads, writes=writes)

        def make_hT(tiles, tile_keys, gvec, gbc, hTv, hkey, scratch=None, scratch_keys=None, loader=None, preload=None, geng="pool", junk=None, gload=True):
            nt = len(tiles)
            if gload:
                S.op(geng, lambda e: e.dma_start(out=gbc, in_=gvec.partition_broadcast(128)), writes=["gbc"], dma="gbc")
            xs_of = {}

            def stageA(t):
                xt = tiles[t]
                sc = stat[:, 2 * (t % 4):2 * (t % 4) + 1]
                rs = stat[:, 2 * (t % 4) + 1:2 * (t % 4) + 2]
                skey = ("stat", t % 4)
                jv = hTv[:, :, t * 128:(t + 1) * 128]
                if junk is None:
                    S.op("act", lambda e: e.activation(out=jv, in_=xt.rearrange("p (k n) -> p k n", k=16), func=AF.Square, accum_out=sc),
                         reads=[tile_keys[t]], writes=[(hkey, t, k) for k in range(16)] + [skey])
                else:
                    S.op("act", lambda e: e.activation(out=junk, in_=xt, func=AF.Square, accum_out=sc),
                         reads=[tile_keys[t]], writes=["junkP", skey])
                S.op("act", lambda e: e.activation(out=rs, in_=sc, func=AF.Sqrt, scale=1.0 / D, bias=epsc[:]),
                     reads=[skey, "epsc"], writes=[skey])
                S.op("dve", lambda e: e.reciprocal(out=rs, in_=rs), reads=[skey], writes=[skey])
                if scratch is None:
                    xs, xskey = xt, tile_keys[t]
                else:
                    xs, xskey = scratch[t % len(scratch)], scratch_keys[t % len(scratch)]
                S.op("dve", lambda e: e.scalar_tensor_tensor(out=xs, in0=xt, scalar=rs, in1=gbc, op0=ALU.mult, op1=ALU.mult),
                     reads=[tile_keys[t], skey, "gbc"], writes=[xskey])
                xs_of[t] = (xs, xskey)

            def stageB(t):
                xs, xskey = xs_of[t]
                for b4 in range(4):
                    bk = bank("all")

                    def fn(e, b4=b4, bk=bk):
                        ins = None
                        for kk in range(4):
                            k = 4 * b4 + kk
                            ins = e.transpose(PS[bk][:, kk * 128:(kk + 1) * 128], xs[:, k * 128:(k + 1) * 128], ident[:])
                        return ins
                    S.op("pe", fn, reads=[xskey, "ident"], writes=[("PS", bk)])
                    o = hTv[:, 4 * b4:4 * b4 + 4, t * 128:(t + 1) * 128]
                    i_ = PS[bk][:, :].rearrange("p (a n) -> p a n", a=4)
                    copy_evac(alt(), o, i_, [("PS", bk)], [(hkey, t, 4 * b4 + kk) for kk in range(4)])

            if loader is not None:
                for t in range(min(preload, nt)):
                    loader(t)
            stageA(0)
            for t in range(nt):
                if t + 1 < nt:
                    stageA(t + 1)
                stageB(t)
                if loader is not None and t + preload < nt:
                    loader(t + preload)

        def hkeys(hkey, ntiles, ks=range(16)):
            return [(hkey, t, k) for t in range(ntiles) for k in ks]

        def load_tiles(src_rows, dst_tiles, dst_keys, tag, eng="pool"):
            for t, (dst, key) in enumerate(zip(dst_tiles, dst_keys)):
                src = src_rows[t * 128:(t + 1) * 128, :]
                S.op(eng, lambda e, dst=dst, src=src: e.dma_start(out=dst, in_=src), writes=[key], dma=f"{tag}{t}")

        def gemm_F(w, c0, ncols, kc, inT, in_reads, ntok, cb, rhs_off=0):
            cpb = SLOTW // kc // 128
            nblk = ncols // (cpb * 128)
            for b in range(nblk):
                blk, wkey = wblock(w[:, c0 + b * cpb * 128: c0 + (b + 1) * cpb * 128], kc, cpb * 128)
                for cc in range(cpb):
                    bk = bank("all")
                    pairs = [(blk[:, k, cc * 128:(cc + 1) * 128], inT[:, k, rhs_off:rhs_off + ntok]) for k in range(kc)]
                    mm_group(bk, PS[bk][:, 0:ntok], pairs, reads=[wkey] + in_reads)
                    cb(bk, b * cpb + cc)

        def gemm_T(w, r0, kc, c0, ncols, inT, in_reads, ntiles, cb, pool="all"):
            kb = min(kc, SLOTW // 512)
            nkb = kc // kb
            for n in range(ncols // 512):
                bks = [bank(pool) for _ in range(ntiles)]
                for j in range(nkb):
                    blk, wkey = wblock(w[r0 + j * kb * 128: r0 + (j + 1) * kb * 128, c0 + n * 512: c0 + (n + 1) * 512], kb, 512)
                    for t in range(ntiles):
                        bk = bks[t]

                        def fn(e, blk=blk, t=t, bk=bk, j=j):
                            ins = None
                            for kk in range(kb):
                                k = j * kb + kk
                                ins = e.matmul(PS[bk][:, :], inT[:, k, t * 128:(t + 1) * 128], blk[:, kk, :],
                                               start=(k == 0), stop=(k == kc - 1))
                            return ins
                        S.op("pe", fn, reads=[wkey] + in_reads, writes=[("PS", bk)])
                        if j == nkb - 1:
                            cb(bk, t, n)

        hT_K = carve(0, 8192, F32R, "p (k n) -> p k n", k=16)
        KT = carve(8192, 8192, BF16, "p (h n) -> p h n", h=8)
        Vt = carve(16384, 8192, BF16, "p (t n) -> p t n", t=16)
        QT = carve(24576, 4096, BF16, "p (h n) -> p h n", h=8)
        QTf = [carve(28672, 512, F32), carve(29184, 512, F32)]
        stage = [carve(t * 2048, 2048, F32, base=ot_addr) for t in range(3)]
        gbcK = carve(6144, 2048, F32, base=ot_addr)
        PTb = None

        scale_att = 1.0 / np.sqrt(128.0)

        def glu_gemm(hTv, hreads, ntok, rhs_off, cb2):
            for c in range(8):
                blkA, kA = wblock(w_in[:, A0 + c * 128: A0 + (c + 1) * 128], 16, 128)
                blkG, kG = wblock(w_in[:, G0 + c * 128: G0 + (c + 1) * 128], 16, 128)
                ba = bank("all")
                mm_group(ba, PS[ba][:, 0:ntok], [(blkA[:, k, :], hTv[:, k, rhs_off:rhs_off + ntok]) for k in range(16)],
                         reads=[kA] + hreads)
                bg = bank("all")
                mm_group(bg, PS[bg][:, 0:ntok], [(blkG[:, k, :], hTv[:, k, rhs_off:rhs_off + ntok]) for k in range(16)],
                         reads=[kG] + hreads)
                cb2(ba, bg, c)

        def gate_ops(h, qf, qk, oc):
            gb = bank("all")

            def fn(e):
                ins = None
                for tt in range(4):
                    ins = e.matmul(PS[gb][:, tt * 8:(tt + 1) * 8], qf[:, tt * 128:(tt + 1) * 128], kmean[:, h, :], start=True, stop=True)
                return ins
            S.op("pe", fn, reads=[qk, "kmean"], writes=[("PS", gb)])
            gm = gtmp[:, 0:32]
            S.op("dve", lambda e: e.tensor_tensor(out=gm, in0=PS[gb][:, 0:32], in1=cp(C_GMASK + oc * 32, 32), op=ALU.add),
                 reads=[("PS", gb), "colp"], writes=["gm"])
            top8 = gtmp[:, 32:64]
            for tt in range(4):
                S.op("dve", lambda e, tt=tt: e.max(out=top8[:, tt * 8:(tt + 1) * 8], in_=gm[:, tt * 8:(tt + 1) * 8]),
                     reads=["gm"], writes=[("top8", tt)])
            thr = gtmp[:, 96:100]
            S.op("dve", lambda e: e.tensor_scalar(out=thr, in0=top8.rearrange("p (t n) -> p t n", n=8)[:, :, 2], scalar1=-1e29, scalar2=None, op0=ALU.max),
                 reads=[("top8", tt) for tt in range(4)], writes=["thr"])
            for tt in range(4):
                t = oc * 4 + tt
                o2 = sbt[:, (t * 8 + h) * 8:(t * 8 + h) * 8 + 8]
                S.op("dve", lambda e, tt=tt, o2=o2: e.tensor_scalar(out=o2, in0=gm[:, tt * 8:(tt + 1) * 8], scalar1=thr[:, tt:tt + 1], scalar2=-1.0, op0=ALU.is_ge, op1=ALU.add),
                     reads=["gm", "thr"], writes=[("sb", t, h)])

        sgt = [None, None]

        for c in range(4):
            S.mark(c)
            src = xp if c < 2 else xo
            r0 = (c % 2) * 512
            skeys = [("stage", t % 3) for t in range(4)]
            srows = src[r0:r0 + 512, :]

            def ld(t, srows=srows):
                dst = stage[t % 3]
                S.op("pool", lambda e: e.dma_start(out=dst, in_=srows[t * 128:(t + 1) * 128, :]), writes=[("stage", t % 3)], dma=f"xs{t % 3}")
            make_hT([stage[t % 3] for t in range(4)], skeys, gmix_d, gbcK, hT_K, "hT", loader=ld, preload=3)
            hr = hkeys("hT", 4)
            S.mark(c + 0.3)

            def cb_k(bk, h, c=c):
                o = KT[:, h, c * 512:(c + 1) * 512]
                bcol = cp(C_BIN + K0 // 128 + h)
                if KSUB <= 11:
                    return
                if not os.environ.get("KSKIPACT"):
                    S.op("act", lambda e: e.activation(out=o, in_=PS[bk][:, :], func=AF.Identity, bias=bcol),
                         reads=[("PS", bk), "colp"], writes=[("KT", h, c)])
                if KSUB <= 12:
                    return
                ks = gtmp[:, 64 + 2 * (h % 8):64 + 2 * (h % 8) + 2]
                S.op("dve", lambda e: e.tensor_reduce(out=ks, in_=PS[bk][:, :].rearrange("p (b t) -> p b t", b=2), axis=AX.X, op=ALU.add),
                     reads=[("PS", bk)], writes=[("ks", h)])
                if KSUB <= 13:
                    return
                S.op("act", lambda e: e.activation(out=kmean[:, h, 2 * c:2 * c + 2], in_=ks, func=AF.Identity, scale=1.0 / 256.0, bias=bcol),
                     reads=[("ks", h), "colp", "kmean"], writes=["kmean"])
            gemm_F(w_in, K0, 1024, 16, hT_K, hr, 512, cb_k)
            S.mark(c + 0.6)

            def cb_v(bk, t, n, c=c):
                o = Vt[:, c * 4 + t, n * 512:(n + 1) * 512]
                copy_evac(alt(), o, PS[bk][:, :], [("PS", bk)], [("V", c * 4 + t, n)])
            gemm_T(w_in, 0, 16, V0, 1024, hT_K, hr, 4, cb_v)

            if c == 1:
                def cb_halo(ba, bg, cch):
                    sg = QTf[0][:, 0:256]
                    S.op("act", lambda e: e.activation(out=sg, in_=PS[bg][:, 0:256], func=AF.Sigmoid, bias=cp(C_BIN + G0 // 128 + cch)),
                         reads=[("PS", bg), "colp"], writes=["halo_sg"])
                    hv = QTf[1][:, 0:256]
                    S.op("dve", lambda e: e.scalar_tensor_tensor(out=hv, in0=PS[ba][:, 0:256], scalar=cp(C_BIN + cch), in1=sg, op0=ALU.add, op1=ALU.mult),
                         reads=[("PS", ba), "halo_sg", "colp"], writes=["halo_v"])
                    S.op("dve", lambda e: e.tensor_scalar(out=vhalo[:, cch, :], in0=hv[:, 224:256], scalar1=cp(C_FLAG), scalar2=None, op0=ALU.mult),
                         reads=["halo_v", "colp"], writes=[("vhalo", cch)])
                glu_gemm(hT_K, hr, 256, 256, cb_halo)

            if c >= 2:
                oc = c - 2
                S.alias({"halo_sg", "halo_v"}, [("QTf", 0), ("QTf", 1)])

                def cb_q(bk, h, oc=oc):
                    o = QT[:, h, oc * 512:(oc + 1) * 512]
                    bcol = cp(C_BIN + Q0 // 128 + h)
                    S.op("act", lambda e: e.activation(out=o, in_=PS[bk][:, :], func=AF.Identity, bias=bcol),
                         reads=[("PS", bk), "colp"], writes=[("QT", h, oc)])
                    qf = QTf[h % 2]
                    qk = ("QTf", h % 2)
                    S.op("dve", lambda e: e.tensor_scalar(out=qf, in0=PS[bk][:, :], scalar1=bcol, scalar2=None, op0=ALU.add),
                         reads=[("PS", bk), "colp"], writes=[qk])
                    prev = pend_gate.pop() if pend_gate else None
                    pend_gate.append(lambda h=h, qf=qf, qk=qk, oc=oc: gate_ops(h, qf, qk, oc))
                    if prev is not None:
                        prev()
                pend_gate = []
                gemm_F(w_in, Q0, 1024, 16, hT_K, hr, 512, cb_q)
                while pend_gate:
                    pend_gate.pop()()

        S.mark(4)
        S.alias({"hT", "stage", "gbc"}, [("PT", i) for i in range(4)] + [("rden", i) for i in range(2)] + [("otmp", i) for i in range(2)])
        PTb = [carve(i * 256, 256, BF16) for i in range(4)]
        rden = [carve(1024 + i * 512, 512, F32) for i in range(2)]
        otmp = [carve(2048 + i * 512, 512, F32) for i in range(2)]
        def prep_sT(h, pr):
            tb = bank("A")

            def fn(e):
                ins = None
                for e2 in range(4):
                    t = 4 * pr + e2
                    ins = e.transpose(PS[tb][0:8, e2 * 128:(e2 + 1) * 128], sbt[:, (t * 8 + h) * 8:(t * 8 + h) * 8 + 8], ident[:])
                return ins
            S.op("pe", fn, reads=[("sb", 4 * pr + e2, h) for e2 in range(4)] + ["ident"], writes=[("PS", tb)])
            copy_evac("dve", sbTb[0:8, 0, :], PS[tb][0:8, 0:512], [("PS", tb)], [("sbT", 0)])

        hj = 0
        ptc = {"n": 0}
        S.alias({"stage", "gbc"}, [("OT", h, p_) for h in range(8) for p_ in range(2)])
        for h in range(8):
            for pr in range(2):
                if hj == 0:
                    prep_sT(h, pr)
                sT = sbTb[:, 0, :]
                sTk = ("sbT", 0)
                ob = 4 + 2 * (hj % 2)
                db = ob + 1
                nblk = 6 + 2 * pr
                qsl = QT[:, h, pr * 512:(pr + 1) * 512]
                qk = ("QT", h, pr)
                visits = [(n, kt) for n in range(nblk) for kt in range(2)]

                def emit_S(v, h=h, pr=pr, qsl=qsl, qk=qk, sT=sT, sTk=sTk):
                    n, kt = v
                    sb_ = bank("A")
                    own0 = 4 + 2 * pr
                    own1 = 5 + 2 * pr

                    def fn(e):
                        o = PS[sb_][:, :]
                        o0 = PS[sb_][:, 0:256]
                        o1 = PS[sb_][:, 256:512]
                        e.matmul(o, KT[:, h, n * 256 + kt * 128: n * 256 + (kt + 1) * 128], qsl, start=True, stop=False)
                        if n < own0:
                            return e.matmul(o, esel[:, n * 128:(n + 1) * 128], sT, start=False, stop=True)
                        if n == own0:
                            e.matmul(o0, identb[:], cmask[:, kt * 256:(kt + 1) * 256], start=False, stop=False)
                            return e.matmul(o1, esel[:, n * 128:(n + 1) * 128], sT[:, 256:512], start=False, stop=True)
                        e.matmul(o0, esel[:, 8 * 128:9 * 128], sT[:, 0:256], start=False, stop=False)
                        return e.matmul(o1, identb[:], cmask[:, kt * 256:(kt + 1) * 256], start=False, stop=True)
                    S.op("pe", fn, reads=[("KT", h, n // 2), qk, "identb", "cmask", "esel", sTk], writes=[("PS", sb_)])
                    pi = ptc["n"] % 4
                    ptc["n"] += 1
                    pt = PTb[pi]
                    S.op("act", lambda e: e.activation(out=pt, in_=PS[sb_][:, :], func=AF.Exp, scale=float(scale_att)),
                         reads=[("PS", sb_)], writes=[("PT", pi)])
                    return pi

                def emit_PV(v, pi, first, last, h=h, ob=ob, db=db):
                    n, kt = v
                    pt = PTb[pi]

                    def fn(e):
                        e.matmul(PS[ob][:, :], Vt[:, n * 2 + kt, h * 128:(h + 1) * 128], pt, start=first, stop=last)
                        return e.matmul(PS[db][:, :], onesb[:], pt, start=first, stop=last)
                    S.op("pe", fn, reads=[("PT", pi), ("V", n * 2 + kt, h // 4), "onesb"], writes=[("PS", ob), ("PS", db)])
                nv = len(visits)
                pis = {}
                for v in range(min(2, nv)):
                    pis[v] = emit_S(visits[v])
                for v in range(nv):
                    if v + 2 < nv:
                        pis[v + 2] = emit_S(visits[v + 2])
                        if v + 2 == nv - 1 and hj + 1 < 16:
                            nh_, npr_ = divmod(hj + 1, 2)
                            prep_sT(nh_, npr_)
                    emit_PV(visits[v], pis[v], v == 0, v == nv - 1)
                rd_ = rden[hj % 2]
                ot_ = otmp[hj % 2]
                S.op("dve", lambda e, rd_=rd_, db=db: e.reciprocal(out=rd_, in_=PS[db][:, :]), reads=[("PS", db)], writes=[("rden", hj % 2)])
                S.op("dve", lambda e, rd_=rd_, ot_=ot_, ob=ob: e.tensor_tensor(out=ot_, in0=PS[ob][:, :], in1=rd_, op=ALU.mult),
                     reads=[("PS", ob), ("rden", hj % 2)], writes=[("otmp", hj % 2)])
                oo = OTr[:, h, pr * 512:(pr + 1) * 512]
                S.op("act", lambda e, oo=oo, ot_=ot_, h=h: e.activation(out=oo, in_=ot_, func=AF.Identity, bias=cp(C_BIN + V0 // 128 + h)),
                     reads=[("otmp", hj % 2), "colp"], writes=[("OT", h, pr)])
                hj += 1
        if DEBUG:
            S.op("pool", lambda e: e.dma_start(out=dbg_d["OT"], in_=OTr[:].bitcast(F32).rearrange("p h n -> p (h n)")),
                 reads=[("OT", h, p_) for h in range(8) for p_ in range(2)], writes=["dbgOT"], dma="dbgOT")
            S.op("pool", lambda e: e.dma_start(out=dbg_d["SB"], in_=sbt[:]),
                 reads=[("sb", t, h) for t in range(8) for h in range(8)], writes=["dbgSB"], dma="dbgSB")

        B1 = 0
        B2 = 8192
        B3 = 16384
        hT_P = carve(B1, 8192, F32R, "p (k n) -> p k n", k=16)
        X = carve(B1, 8192, F32, "p (t n) -> p t n", t=4)
        xstage = [carve(B2 + t * 2048, 2048, F32) for t in range(4)]
        mergedTr = carve(B2, 8192, F32R, "p (k n) -> p k n", k=16)
        ysq = carve(B2, 4096, F32, "p (c n) -> p c n", c=8)
        lnt = [carve(B2 + 4096 + i * 512, 512, F32) for i in range(4)]
        sgtmp = [carve(B2 + 6144 + i * 512, 512, F32) for i in range(2)]
        vglu = carve(B3, 8 * 544 // 2, BF16, "p (c n) -> p c n", c=8)
        dg = [carve(B2 + i * 2048, 1984, BF16, "p (k n) -> p k n", k=31) for i in range(2)]
        ycT = carve(B3 + 4352, 4096, F32, "p (c n) -> p c n", c=8)
        ycTr = carve(B3, 4096, F32R, "p (c n) -> p c n", c=8)
        mtmp = [carve(B3 + 8448 + i * 512, 512, F32) for i in range(2)]
        memst = [carve(B2 + t * 2048, 2048, F32) for t in range(2)]
        mT = carve(B2 + 4096, 4096, F32R, "p (k n) -> p k n", k=16)
        h2T = carve(B2, 8192, F32R, "p (k n) -> p k n", k=16)
        o2T = carve(B2, 8192, F32R, "p (k n) -> p k n", k=16)
        memKT = carve(B3, 2048, BF16, "p (c n) -> p c n", c=16)
        memV = carve(B3 + 2048, 2048, BF16, "p (t n) -> p t n", t=2)
        q2T = carve(B3 + 4096, 4096, BF16, "p (c n) -> p c n", c=16)
        xscr = [carve(B3 + 4096 + i * 2048, 2048, F32) for i in range(2)]
        P2T = [carve(B3 + 8192 + i * 256, 256, BF16) for i in range(2)]
        rden2 = [carve(B3 + 8704 + i * 512, 512, F32) for i in range(1)]
        h3T = carve(B2, 8192, F32R, "p (k n) -> p k n", k=16)
        actT = [carve(B3 + i * 2048, 2048, F32R, "p (c n) -> p c n", c=4) for i in range(2)]
        fsg = [carve(B3 + 4096 + i * 512, 512, F32) for i in range(2)]
        fscr = [carve(B3 + 5120 + i * 2048, 2048, F32) for i in range(2)]
        gfin = carve(B3, 2048, F32)
        obuf = [carve(B3 + 2048 + i * 2048, 2048, F32) for i in range(2)]

        gbcP = carve(25856, 2048, F32)
        scale_x = 1.0 / np.sqrt(512.0)
        ALLB = {"hT", "stage", "gbc", "PT", "rden", "otmp", "KT", "V", "QT", "QTf", "halo_sg", "halo_v"}
        PASSK = {"hTP", "X", "xstage", "mergedT", "ysq", "lnt", "sgtmp", "vglu", "vgh", "ycT", "ycTr", "dg", "dgk", "mtmp", "memst", "mT", "h2T", "o2T",
                 "memKT", "memV", "q2T", "xscr", "P2T", "rden2", "h3T", "actT", "fsg", "fscr", "gfin", "obuf"}

        REGS = {
            "B1": {"hTP", "X"},
            "B2": {"xstage", "mergedT", "ysq", "lnt", "sgtmp", "dg", "dgk", "memst", "mT", "h2T", "o2T", "h3T"},
            "B3": {"vglu", "vgh", "ycT", "ycTr", "mtmp", "memKT", "memV", "q2T", "xscr", "P2T", "rden2", "actT", "fsg", "fscr",
                   "gfin", "obuf"},
            "G": {"gbc", "junkP"},
        }
        REG_OF = {fam: r for r, fams in REGS.items() for fam in fams}
        pass_state = {"p": 0}

        def region_alias(newkeys):
            regs = set()
            for k in newkeys:
                fam = k[0] if isinstance(k, tuple) else k
                regs.add(REG_OF[fam])
            old = set()
            for r in regs:
                old |= REGS[r]
            if pass_state["p"] == 0:
                old |= ALLB
            if os.environ.get("KGLOBAL"):
                old = ALLB | PASSK | {"junkP"}
            S.alias(old, newkeys)

        junkP = carve(27904, 1024, BF16)
        for p in range(2):
            pass_state["p"] = p
            S.mark(5 + 10 * p)
            xk = [("xstage", t) for t in range(4)]
            region_alias(xk)
            load_tiles(xo[p * 512:(p + 1) * 512, :], [t_[:] for t_ in xstage], xk, "xp", eng="sp")
            region_alias(hkeys("hTP", 4))
            region_alias(["gbc", "junkP"])
            make_hT([t_[:] for t_ in xstage], xk, gmix_d, gbcP, hT_P, "hTP", junk=junkP, gload=(p == 0))
            hr = hkeys("hTP", 4)

            region_alias([("vglu", c) for c in range(8)] + [("sgtmp", i) for i in range(2)])
            for c in range(8):
                S.op("pool", lambda e, c=c: e.tensor_copy(out=vglu[:, c, 0:32], in_=vhalo[:, c, :]),
                     reads=[("vhalo", c)], writes=[("vgh", c)])

            def cb_glu(ba, bg, c):
                i = c % 2
                S.op("act", lambda e: e.activation(out=sgtmp[i], in_=PS[bg][:, :], func=AF.Sigmoid, bias=cp(C_BIN + G0 // 128 + c)),
                     reads=[("PS", bg), "colp"], writes=[("sgtmp", i)])
                S.op("dve", lambda e: e.scalar_tensor_tensor(out=vglu[:, c, 32:544], in0=PS[ba][:, :], scalar=cp(C_BIN + c), in1=sgtmp[i], op0=ALU.add, op1=ALU.mult),
                     reads=[("PS", ba), ("sgtmp", i), "colp"], writes=[("vglu", c)])
            glu_gemm(hT_P, hr, 512, 0, cb_glu)
            region_alias([("ycT", c) for c in range(8)] + [("dg", i) for i in range(2)] + [("dgk", i, k) for i in range(2) for k in range(1, 31)])
            for c in range(8):
                d_ = dg[c % 2]
                dk = ("dg", c % 2)
                S.op("dve", lambda e, d_=d_, c=c: e.tensor_scalar(out=d_[:, 0, :], in0=identb[:], scalar1=cp(C_CONVW + c * 31), scalar2=None, op0=ALU.mult),
                     reads=["identb", "colp"], writes=[dk])
                for k in range(1, 31):
                    eng = "dve" if k % 2 == 0 else "pool"
                    S.op(eng, lambda e, d_=d_, c=c, k=k: e.tensor_scalar(out=d_[:, k, :], in0=identb[:], scalar1=cp(C_CONVW + c * 31 + k), scalar2=1.0, op0=ALU.mult, op1=ALU.mult),
                         reads=["identb", "colp"], writes=[("dgk", c % 2, k)])
                bk = bank("all")
                mm_group(bk, PS[bk][:, :], [(d_[:, k, :], vglu[:, c, 2 + k:514 + k]) for k in range(31)],
                         reads=[dk] + [("dgk", c % 2, k) for k in range(1, 31)] + [("vglu", c), ("vgh", c)])
                S.op("act", lambda e, c=c, bk=bk: e.activation(out=ycT[:, c, :], in_=PS[bk][:, :], func=AF.Identity, bias=cp(C_CONVB + c)),
                     reads=[("PS", bk), "colp"], writes=[("ycT", c)])
                if p == 0:
                    S.op("pool", lambda e, c=c: e.tensor_copy(out=vhalo[:, c, :], in_=vglu[:, c, 512:544]),
                         reads=[("vglu", c)], writes=[("vhalo", c)])
            region_alias([("ysq", c) for c in range(8)] + [("lnt", i) for i in range(4)])
            for c in range(8):
                S.op("act", lambda e, c=c: e.activation(out=ysq[:, c, :], in_=ycT[:, c, :], func=AF.Square),
                     reads=[("ycT", c)], writes=[("ysq", c)])
            bm = bank("all")
            mm_group(bm, PS[bm][:, :], [(onesf[:], ycT[:, c, :]) for c in range(8)], reads=["onesf"] + [("ycT", c) for c in range(8)])
            be = bank("all")
            mm_group(be, PS[be][:, :], [(onesf[:], ysq[:, c, :]) for c in range(8)], reads=["onesf"] + [("ysq", c) for c in range(8)])
            mean_sb, msq, var_, rstd_ = lnt
            S.op("act", lambda e, bm=bm: e.activation(out=mean_sb, in_=PS[bm][:, :], func=AF.Copy), reads=[("PS", bm)], writes=[("lnt", 0)])
            S.op("dve", lambda e: e.tensor_tensor(out=msq, in0=mean_sb, in1=mean_sb, op=ALU.mult), reads=[("lnt", 0)], writes=[("lnt", 1)])
            S.op("dve", lambda e, be=be: e.tensor_tensor(out=var_, in0=PS[be][:, :], in1=msq, op=ALU.subtract), reads=[("PS", be), ("lnt", 1)], writes=[("lnt", 2)])
            S.op("act", lambda e: e.activation(out=var_, in_=var_, func=AF.Sqrt, bias=epsc[:]), reads=[("lnt", 2), "epsc"], writes=[("lnt", 2)])
            S.op("dve", lambda e: e.reciprocal(out=rstd_, in_=var_), reads=[("lnt", 2)], writes=[("lnt", 3)])
            region_alias([("ycTr", c) for c in range(8)])
            for c in range(8):
                y = ycT[:, c, :]
                S.op("dve", lambda e, y=y: e.tensor_tensor(out=y, in0=y, in1=mean_sb, op=ALU.subtract), reads=[("ycT", c), ("lnt", 0)], writes=[("ycT", c)])
                S.op("dve", lambda e, y=y: e.tensor_tensor(out=y, in0=y, in1=rstd_, op=ALU.mult), reads=[("ycT", c), ("lnt", 3)], writes=[("ycT", c)])
                S.op("act", lambda e, y=y, c=c: e.activation(out=ycTr[:, c, :], in_=y, func=AF.Silu, scale=cp(C_LNG + c), bias=cp(C_LNB + c)),
                     reads=[("ycT", c), "colp"], writes=[("ycTr", c)])
            if DEBUG and p == 0:
                S.op("pool", lambda e: e.dma_start(out=dbg_d["YC"], in_=ycTr[:].bitcast(F32).rearrange("p c n -> p (c n)")),
                     reads=[("ycTr", c) for c in range(8)], writes=["dbgYC"], dma="dbgYC")

            S.mark(6 + 10 * p)
            region_alias([("mergedT", f) for f in range(16)] + [("mtmp", i) for i in range(2)])
            for sweep in range(2):
                wA = w_conv_out if sweep == 0 else w_att_out
                g0 = GC0 if sweep == 0 else GA0
                for i in range(8):
                    blkY, kY = wblock(wA[:, i * 256:(i + 1) * 256], 8, 256)
                    for cc in range(2):
                        if True:
                            f = 2 * i + cc
                            blkG, kG = wblock(w_in[:, g0 + f * 128: g0 + (f + 1) * 128], 16, 128)
                            by = bank("all")
                            if sweep == 0:
                                pairs = [(blkY[:, k, cc * 128:(cc + 1) * 128], ycTr[:, k, :]) for k in range(8)]
                                rd = [kY] + [("ycTr", k) for k in range(8)]
                            else:
                                pairs = [(blkY[:, k, cc * 128:(cc + 1) * 128], OTr[:, k, p * 512:(p + 1) * 512]) for k in range(8)]
                                rd = [kY] + [("OT", k, p) for k in range(8)]
                            mm_group(by, PS[by][:, :], pairs, reads=rd)
                            bg = bank("all")
                            mm_group(bg, PS[bg][:, :], [(blkG[:, k, :], hT_P[:, k, :]) for k in range(16)], reads=[kG] + hr)
                            mt_ = mtmp[f % 2]
                            mk = ("mtmp", f % 2)
                            S.op("act", lambda e, mt_=mt_, bg=bg, f=f, g0=g0: e.activation(out=mt_, in_=PS[bg][:, :], func=AF.Sigmoid, bias=cp(C_BIN + g0 // 128 + f)),
                                 reads=[("PS", bg), "colp"], writes=[mk])
                            if sweep == 0:
                                S.op("dve", lambda e, mt_=mt_, by=by, f=f: e.tensor_tensor(out=mergedTr[:, f, :], in0=PS[by][:, :], in1=mt_, op=ALU.mult),
                                     reads=[("PS", by), mk], writes=[("mergedT", f)])
                            else:
                                S.op("dve", lambda e, mt_=mt_, by=by: e.tensor_tensor(out=mt_, in0=PS[by][:, :], in1=mt_, op=ALU.mult),
                                     reads=[("PS", by), mk], writes=[mk])
                                S.op("dve", lambda e, mt_=mt_, f=f: e.tensor_tensor(out=mergedTr[:, f, :], in0=mergedTr[:, f, :].bitcast(F32), in1=mt_, op=ALU.add),
                                     reads=[("mergedT", f), mk], writes=[("mergedT", f)])

            S.mark(7 + 10 * p)
            region_alias([("X", t) for t in range(4)])
            Xk = [("X", t) for t in range(4)]
            load_tiles(xo[p * 512:(p + 1) * 512, :], [X[:, t, :] for t in range(4)], Xk, "xr")

            def cb_res(bk, t, n):
                xs_ = X[:, t, n * 512:(n + 1) * 512]
                S.op("dve", lambda e: e.tensor_tensor(out=xs_, in0=PS[bk][:, :], in1=xs_, op=ALU.add),
                     reads=[("PS", bk), ("X", t)], writes=[("X", t)])
            gemm_T(w_out, 0, 16, 0, D, mergedTr, [("mergedT", f) for f in range(16)], 4, cb_res)
            if DEBUG and p == 0:
                S.op("pool", lambda e: e.dma_start(out=dbg_d["X1"], in_=X[:].rearrange("p t n -> p (t n)")),
                     reads=Xk, writes=["dbgX1"], dma="dbgX1")

            S.mark(8 + 10 * p)
            mk_ = [("memst", t) for t in range(2)]
            region_alias(mk_ + hkeys("mT", 2))
            load_tiles(memd[:, :], [t_[:] for t_ in memst], mk_, "ms")
            make_hT([t_[:] for t_ in memst], mk_, gmem_d, gbcP, mT, "mT")
            mr = hkeys("mT", 2)
            region_alias([("memKT", c) for c in range(16)] + [("memV", t, n) for t in range(2) for n in range(4)])

            def cb_mk(bk, c):
                copy_evac(alt(), memKT[:, c, :], PS[bk][:, 0:256], [("PS", bk)], [("memKT", c)])
            gemm_F(w_ckv, 0, D, 16, mT, mr, 256, cb_mk)

            def cb_mv(bk, t, n):
                copy_evac(alt(), memV[:, t, n * 512:(n + 1) * 512], PS[bk][:, :], [("PS", bk)], [("memV", t, n)])
            gemm_T(w_ckv, 0, 16, D, D, mT, mr, 2, cb_mv)

            region_alias(hkeys("h2T", 4) + [("xscr", i) for i in range(2)])
            make_hT([X[:, t, :] for t in range(4)], Xk, gcross_d, gbcP, h2T, "h2T", scratch=[t_[:] for t_ in xscr], scratch_keys=[("xscr", i) for i in range(2)])
            region_alias([("q2T", c) for c in range(16)])

            def cb_q2(bk, c):
                copy_evac(alt(), q2T[:, c, :], PS[bk][:, :], [("PS", bk)], [("q2T", c)])
            gemm_F(w_cq, 0, D, 16, h2T, hkeys("h2T", 4), 512, cb_q2)
            region_alias([("o2T", c) for c in range(16)] + [("P2T", i) for i in range(2)] + [("rden2", 0)])
            for hh in range(4):
                for mt in range(2):
                    sb_ = bank("all")
                    mm_group(sb_, PS[sb_][:, :], [(memKT[:, 4 * hh + dc, mt * 128:(mt + 1) * 128], q2T[:, 4 * hh + dc, :]) for dc in range(4)],
                             reads=[("memKT", 4 * hh + dc) for dc in range(4)] + [("q2T", 4 * hh + dc) for dc in range(4)])
                    S.op("act", lambda e, sb_=sb_, mt=mt: e.activation(out=P2T[mt], in_=PS[sb_][:, :], func=AF.Exp, scale=float(scale_x)),
                         reads=[("PS", sb_)], writes=[("P2T", mt)])
                db_ = bank("all")
                mm_group(db_, PS[db_][:, :], [(onesb[:], P2T[mt]) for mt in range(2)], reads=["onesb", ("P2T", 0), ("P2T", 1)])
                S.op("dve", lambda e, db_=db_: e.reciprocal(out=rden2[0], in_=PS[db_][:, :]), reads=[("PS", db_)], writes=[("rden2", 0)])
                for dc in range(4):
                    c = 4 * hh + dc
                    ob_ = bank("all")
                    mm_group(ob_, PS[ob_][:, :], [(memV[:, mt, c * 128:(c + 1) * 128], P2T[mt]) for mt in range(2)],
                             reads=[("P2T", 0), ("P2T", 1)] + [("memV", mt, c // 4) for mt in range(2)])
                    S.op("dve", lambda e, ob_=ob_, c=c: e.tensor_tensor(out=o2T[:, c, :], in0=PS[ob_][:, :], in1=rden2[0], op=ALU.mult),
                         reads=[("PS", ob_), ("rden2", 0)], writes=[("o2T", c)])
            gemm_T(w_co, 0, 16, 0, D, o2T, [("o2T", c) for c in range(16)], 4, cb_res)
            if DEBUG and p == 0:
                S.op("pool", lambda e: e.dma_start(out=dbg_d["X2"], in_=X[:].rearrange("p t n -> p (t n)")),
                     reads=Xk, writes=["dbgX2"], dma="dbgX2")

            S.mark(9 + 10 * p)
            region_alias(hkeys("h3T", 4) + [("fscr", i) for i in range(2)])
            make_hT([X[:, t, :] for t in range(4)], Xk, gffn_d, gbcP, h3T, "h3T", scratch=[t_[:] for t_ in fscr], scratch_keys=[("fscr", i) for i in range(2)])
            h3r = hkeys("h3T", 4)
            if p == 0:
                S.op("pool", lambda e: e.dma_start(out=gbcP, in_=gmix_d.partition_broadcast(128)), writes=["gbc"], dma="gbc")
            region_alias([("actT", i, fc) for i in range(2) for fc in range(4)] + [("fsg", i) for i in range(2)])
            for fg in range(11):
                ab = fg % 2
                for fc in range(4):
                    if True:
                        blkG, kG = wblock(w_ffn_in[:, fg * 512 + fc * 128: fg * 512 + (fc + 1) * 128], 16, 128)
                        blkU, kU = wblock(w_ffn_in[:, DFF + fg * 512 + fc * 128: DFF + fg * 512 + (fc + 1) * 128], 16, 128)
                        bg = bank("A")
                        mm_group(bg, PS[bg][:, :], [(blkG[:, k, :], h3T[:, k, :]) for k in range(16)], reads=[kG] + h3r)
                        bu = bank("A")
                        mm_group(bu, PS[bu][:, :], [(blkU[:, k, :], h3T[:, k, :]) for k in range(16)], reads=[kU] + h3r)
                        sg_ = fsg[fc % 2]
                        S.op("act", lambda e, sg_=sg_, bg=bg: e.activation(out=sg_, in_=PS[bg][:, :], func=AF.Silu),
                             reads=[("PS", bg)], writes=[("fsg", fc % 2)])
                        S.op("dve", lambda e, sg_=sg_, bu=bu, ab=ab, fc=fc: e.tensor_tensor(out=actT[ab][:, fc, :], in0=PS[bu][:, :], in1=sg_, op=ALU.mult),
                             reads=[("PS", bu), ("fsg", fc % 2)], writes=[("actT", ab, fc)])
                for n4 in range(4):
                    blkO, kO = wblock(w_ffn_out[fg * 512:(fg + 1) * 512, n4 * 512:(n4 + 1) * 512], 4, 512)
                    for t in range(4):
                        bo = bank("B")
                        mm_group(bo, PS[bo][:, :], [(actT[ab][:, fc, t * 128:(t + 1) * 128], blkO[:, fc, :]) for fc in range(4)],
                                 reads=[kO] + [("actT", ab, fc) for fc in range(4)])
                        cb_res(bo, t, n4)

            S.mark(10 + 10 * p)
            region_alias(["gfin"] + [("obuf", i) for i in range(2)])
            S.op("pool", lambda e: e.dma_start(out=gfin[:], in_=gfin_d.partition_broadcast(128)), writes=["gfin"], dma="gfin")
            for t in range(4):
                sc = stat[:, 16 + 2 * t:16 + 2 * t + 1]
                rs = stat[:, 16 + 2 * t + 1:16 + 2 * t + 2]
                skey = ("fstat", t)
                xt = X[:, t, :]
                ob_ = obuf[t % 2]
                S.op("act", lambda e, xt=xt, sc=sc, ob_=ob_: e.activation(out=ob_, in_=xt, func=AF.Square, accum_out=sc),
                     reads=[("X", t)], writes=[("obuf", t % 2), skey])
                S.op("act", lambda e, sc=sc, rs=rs: e.activation(out=rs, in_=sc, func=AF.Sqrt, scale=1.0 / D, bias=epsc[:]),
                     reads=[skey, "epsc"], writes=[skey])
                S.op("dve", lambda e, rs=rs: e.reciprocal(out=rs, in_=rs), reads=[skey], writes=[skey])
                S.op("dve", lambda e, ob_=ob_, xt=xt, rs=rs: e.scalar_tensor_tensor(out=ob_, in0=xt, scalar=rs, in1=gfin[:], op0=ALU.mult, op1=ALU.mult),
                     reads=[("X", t), skey, "gfin"], writes=[("obuf", t % 2)])
                dst = out_d[p * 512 + t * 128: p * 512 + (t + 1) * 128, :]
                S.op("pool", lambda e, ob_=ob_, dst=dst: e.dma_start(out=dst, in_=ob_), reads=[("obuf", t % 2)], writes=[("outd", p, t)], dma=f"out{t % 2}")

        S.stopped = False
        S.final_waits("pool")

        with nc.Block() as block:
            @block.tensor
            def _(e):
                S.replay("pe", e)

            @block.scalar
            def _(e):
                S.replay("act", e)

            @block.vector
            def _(e):
                S.replay("dve", e)

            @block.gpsimd
            def _(e):
                S.replay("pool", e)

            @block.sync
            def _(e):
                S.replay("sp", e)
    return nc


_CACHE = {}


def _consts():
    bf = ml_dtypes.bfloat16
    ident = np.eye(128, dtype=np.float32)
    onesf = np.full((128, 128), 1.0 / 1024.0, dtype=np.float32)
    identb = np.eye(128).astype(bf)
    onesb = np.ones((128, 128)).astype(bf)
    cm = np.zeros((128, 2, 256), dtype=np.float32)
    for kt in range(2):
        key = kt * 128 + np.arange(128)[:, None]
        q = np.arange(256)[None, :]
        cm[:, kt, :] = np.where(key <= q, 0.0, NEG)
    cmask = cm.reshape(128, 512).astype(bf)
    es = np.zeros((128, 9, 128), dtype=np.float32)
    for n in range(9):
        es[n, n, :] = -NEG
    esel = es.reshape(128, 9 * 128).astype(bf)
    return dict(ident=ident, onesf=onesf, identb=identb, onesb=onesb, cmask=cmask, esel=esel)


def kernel(x, mem, norm_mix_g, w_in, b_in, conv_w, conv_b, conv_ln_g, conv_ln_b,
           w_conv_out, w_att_out, w_out, norm_cross_g, norm_mem_g, w_cq, w_ckv,
           w_co, norm_ffn_g, w_ffn_in, w_ffn_out, norm_final_g):
    f = lambda a: np.ascontiguousarray(np.asarray(a, dtype=np.float32))
    x = f(x); mem = f(mem)
    if "nc" not in _CACHE:
        _CACHE["nc"] = build_program()
    nc = _CACHE["nc"]
    consts = _consts()

    def col(v, k):
        return np.asarray(v, np.float32).reshape(k, 128).T

    shared = dict(
        w_in=f(w_in[0]), w_conv_out=f(w_conv_out[0]), w_att_out=f(w_att_out[0]), w_out=f(w_out[0]),
        w_cq=f(w_cq[0]), w_ckv=f(w_ckv[0]), w_co=f(w_co[0]), w_ffn_in=f(w_ffn_in[0]), w_ffn_out=f(w_ffn_out[0]),
        gfin=f(norm_final_g), gmix=f(norm_mix_g[0]), gcross=f(norm_cross_g[0]), gmem=f(norm_mem_g[0]),
        gffn=f(norm_ffn_g[0]), **consts)
    base = np.zeros((128, NCOLP), np.float32)
    base[:, C_BIN:C_BIN + 72] = col(b_in[0], 72)
    base[:, C_GMIX:C_GMIX + 16] = col(norm_mix_g[0], 16)
    base[:, C_GCROSS:C_GCROSS + 16] = col(norm_cross_g[0], 16)
    base[:, C_GMEM:C_GMEM + 16] = col(norm_mem_g[0], 16)
    base[:, C_GFFN:C_GFFN + 16] = col(norm_ffn_g[0], 16)
    cw = np.asarray(conv_w[0], np.float32)
    base[:, C_CONVW:C_CONVW + 248] = cw.reshape(31, 8, 128).transpose(2, 1, 0).reshape(128, 248)
    base[:, C_CONVB:C_CONVB + 8] = col(conv_b[0], 8)
    base[:, C_LNG:C_LNG + 8] = col(conv_ln_g[0], 8)
    base[:, C_LNB:C_LNB + 8] = col(conv_ln_b[0], 8)
    in_maps = []
    for core in range(8):
        b, half = core // 2, core % 2
        cpm = base.copy()
        cpm[:, C_FLAG] = float(half)
        gm = np.zeros((8, 8), np.float32)
        for t in range(8):
            for n in range(8):
                valid = (n < 4 + t // 2) and (n >= 4 or half == 1)
                gm[t, n] = 0.0 if valid else -1e30
        cpm[:, C_GMASK:C_GMASK + 64] = gm.reshape(1, 64)
        m = dict(shared)
        m["xo"] = np.ascontiguousarray(x[b, half * 1024:(half + 1) * 1024])
        m["xp"] = np.ascontiguousarray(x[b, 0:1024])
        m["mem"] = np.ascontiguousarray(mem[b])
        m["colp"] = cpm
        in_maps.append(m)
    declared = set()
    for alloc in nc.allocations:
        if isinstance(alloc, mybir.MemoryLocationSet) and alloc.kind == "ExternalInput":
            declared.add(alloc.memorylocations[0].name)
    in_maps = [{k: v for k, v in m.items() if k in declared} for m in in_maps]
    cores = [int(c) for c in KCORES.split(",")] if KCORES else list(range(8))
    res = run_bass_kernel_spmd(nc, [in_maps[c] for c in cores], core_ids=list(range(len(cores))))
    _CACHE["last"] = res
    out = np.zeros((4, 2048, 2048), np.float32)
    for i, core in enumerate(cores):
        b, half = core // 2, core % 2
        out[b, half * 1024:(half + 1) * 1024] = res.results[i]["out"]
    return out
```

```python
import numpy as np
import ml_dtypes
import concourse.bass as bass
import concourse.mybir as mybir
from concourse.bass_utils import run_bass_kernel_spmd

F32 = mybir.dt.float32
F32R = mybir.dt.float32r
BF16 = mybir.dt.bfloat16
AF = mybir.ActivationFunctionType
ALU = mybir.AluOpType
AX = mybir.AxisListType

D = 2048
S_OWN = 1024
TCH = 512
A0, G0, Q0, K0, V0, GC0, GA0 = 0, 1024, 2048, 3072, 4096, 5120, 7168
DFF = 5632
EPS = 1e-6
NEG = -32768.0

C_BIN = 0
C_GMIX = 72
C_GCROSS = 88
C_GMEM = 104
C_GFFN = 120
C_CONVW = 136
C_CONVB = 384
C_LNG = 392
C_LNB = 400
C_FLAG = 408
C_GMASK = 409
NCOLP = 473

DEBUG = False
NSLOT = 6
SLOTW = 2048
import os
KSTAGE = float(os.environ.get("KSTAGE", "99"))
KCORES = os.environ.get("KCORES", "")
KSUB = int(os.environ.get("KSUB", "99"))


class Sched:
    def __init__(self, nc, sems):
        self.nc = nc
        self.free_sems = list(sems)
        self.engs = ["pe", "act", "dve", "pool", "sp"]
        self.streams = {e: [] for e in self.engs}
        self.esem = {e: self.free_sems.pop() for e in self.engs}
        self.ecnt = {e: 0 for e in self.engs}
        self.waited = {e: {} for e in self.engs}
        self.keys = {}
        self.dsem = {}
        self.dcnt = {}
        self.semobj = {}
        for e in self.engs:
            self.semobj[id(self.esem[e])] = self.esem[e]

    def _dma_sem(self, name):
        if name not in self.dsem:
            s = self.free_sems.pop()
            self.dsem[name] = s
            self.dcnt[name] = 0
            self.semobj[id(s)] = s
        return self.dsem[name]

    def mark(self, n):
        if n >= KSTAGE:
            self.stopped = True

    def op(self, eng, fn, reads=(), writes=(), dma=None):
        if getattr(self, "stopped", False):
            return None
        need = {}

        def merge(d):
            for sid, v in d.items():
                if need.get(sid, 0) < v:
                    need[sid] = v
        for k in reads:
            if k in self.keys:
                if isinstance(k, tuple) and k[0] == "PS" and self.keys[k][1]:
                    merge(self.keys[k][1])
                else:
                    merge(self.keys[k][0])
        for k in writes:
            if k in self.keys:
                merge(self.keys[k][0])
                merge(self.keys[k][1])
        if dma is not None:
            sem = self._dma_sem(dma)
            self.dcnt[dma] += 16
            ev = (id(sem), self.dcnt[dma])
            inc = 16
        else:
            sem = self.esem[eng]
            self.ecnt[eng] += 1
            ev = (id(sem), self.ecnt[eng])
            inc = 1
        waits = []
        own = id(self.esem[eng])
        for sid, v in need.items():
            if eng == "pe" and sid == own:
                continue
            if self.waited[eng].get(sid, 0) < v:
                waits.append((self.semobj[sid], v))
                self.waited[eng][sid] = v
        self.streams[eng].append((fn, waits, (sem, inc)))
        for k in reads:
            ent = self.keys.setdefault(k, ({}, {}))
            if isinstance(k, tuple) and k[0] == "PS":
                ent[1].clear()
            if ent[1].get(ev[0], 0) < ev[1]:
                ent[1][ev[0]] = ev[1]
        for k in writes:
            self.keys[k] = ({ev[0]: ev[1]}, {})
        return ev

    def alias(self, old_prefixes, new_keys):
        merged = {}
        for k, (w, r) in self.keys.items():
            name = k[0] if isinstance(k, tuple) else k
            if name in old_prefixes:
                for d in (w, r):
                    for sid, v in d.items():
                        if merged.get(sid, 0) < v:
                            merged[sid] = v
        for k in new_keys:
            ent = self.keys.get(k)
            if ent is None:
                self.keys[k] = (dict(merged), {})
            else:
                for sid, v in merged.items():
                    if ent[0].get(sid, 0) < v:
                        ent[0][sid] = v

    def final_waits(self, eng):
        need = {}
        for e in self.engs:
            if self.ecnt[e] > 0:
                need[id(self.esem[e])] = self.ecnt[e]
        for name, s in self.dsem.items():
            need[id(s)] = self.dcnt[name]
        waits = [(self.semobj[sid], v) for sid, v in need.items() if sid != id(self.esem[eng])]
        self.streams[eng].append((None, waits, None))

    def replay(self, name, eng):
        for fn, waits, inc in self.streams[name]:
            for s_, v in waits:
                eng.wait_ge(s_, v)
            if fn is None:
                continue
            ins = fn(eng)
            ins.then_inc(inc[0], inc[1])


def build_program():
    nc = bass.Bass("TRN2", target_bir_lowering=False)
    nc.dge_precook = False

    class Lazy:
        def __init__(self, name, shape, dt):
            self.a = (name, list(shape), dt)
            self.v = None

        def ap(self):
            if self.v is None:
                self.v = nc.dram_tensor(self.a[0], self.a[1], self.a[2], kind="ExternalInput").ap()
            return self.v

        def __getitem__(self, k):
            return self.ap()[k]

        def partition_broadcast(self, n):
            return self.ap().partition_broadcast(n)

    def din(name, shape, dt):
        return Lazy(name, shape, dt)

    xo = din("xo", [S_OWN, D], F32)
    xp = din("xp", [S_OWN, D], F32)
    memd = din("mem", [256, D], F32)
    w_in = din("w_in", [D, 9216], F32R)
    w_conv_out = din("w_conv_out", [1024, D], F32R)
    w_att_out = din("w_att_out", [1024, D], F32R)
    w_out = din("w_out", [D, D], F32R)
    w_cq = din("w_cq", [D, D], F32R)
    w_ckv = din("w_ckv", [D, 2 * D], F32R)
    w_co = din("w_co", [D, D], F32R)
    w_ffn_in = din("w_ffn_in", [D, 2 * DFF], F32R)
    w_ffn_out = din("w_ffn_out", [DFF, D], F32R)
    colp_d = din("colp", [128, NCOLP], F32)
    gfin_d = din("gfin", [D], F32)
    gmix_d = din("gmix", [D], F32)
    gcross_d = din("gcross", [D], F32)
    gmem_d = din("gmem", [D], F32)
    gffn_d = din("gffn", [D], F32)
    ident_d = din("ident", [128, 128], F32)
    onesf_d = din("onesf", [128, 128], F32)
    identb_d = din("identb", [128, 128], BF16)
    onesb_d = din("onesb", [128, 128], BF16)
    cmask_d = din("cmask", [128, 512], BF16)
    esel_d = din("esel", [128, 9 * 128], BF16)
    out_d = nc.dram_tensor("out", [S_OWN, D], F32, kind="ExternalOutput").ap()
    dbg_d = {}
    if DEBUG:
        dbg_d["OT"] = nc.dram_tensor("dbg_OT", [128, 8 * 1024], F32, kind="ExternalOutput").ap()
        dbg_d["X1"] = nc.dram_tensor("dbg_X1", [128, 4 * 2048], F32, kind="ExternalOutput").ap()
        dbg_d["X2"] = nc.dram_tensor("dbg_X2", [128, 4 * 2048], F32, kind="ExternalOutput").ap()
        dbg_d["YC"] = nc.dram_tensor("dbg_YC", [128, 8 * 512], F32, kind="ExternalOutput").ap()
        dbg_d["SB"] = nc.dram_tensor("dbg_SB", [128, 512], F32, kind="ExternalOutput").ap()

    import contextlib
    es = contextlib.ExitStack()
    with es:
        REGW = 29696
        reg = es.enter_context(nc.sbuf_tensor("sb_reg", [128, REGW], F32))
        wring = es.enter_context(nc.sbuf_tensor("sb_wring", [128, NSLOT, SLOTW], F32R))
        OTr = es.enter_context(nc.sbuf_tensor("sb_OT", [128, 8, 1024], F32R))
        colp = es.enter_context(nc.sbuf_tensor("sb_colp", [128, NCOLP], F32))
        ident = es.enter_context(nc.sbuf_tensor("sb_ident", [128, 128], F32))
        onesf = es.enter_context(nc.sbuf_tensor("sb_onesf", [128, 128], F32))
        identb = es.enter_context(nc.sbuf_tensor("sb_identb", [128, 128], BF16))
        onesb = es.enter_context(nc.sbuf_tensor("sb_onesb", [128, 128], BF16))
        cmask = es.enter_context(nc.sbuf_tensor("sb_cmask", [128, 512], BF16))
        esel = es.enter_context(nc.sbuf_tensor("sb_esel", [128, 9 * 128], BF16))
        epsc = es.enter_context(nc.sbuf_tensor("sb_epsc", [128, 1], F32))
        stat = es.enter_context(nc.sbuf_tensor("sb_stat", [128, 64], F32))
        kmean = es.enter_context(nc.sbuf_tensor("sb_kmean", [128, 8, 8], F32))
        sbt = es.enter_context(nc.sbuf_tensor("sb_sbt", [128, 8 * 8 * 8], F32))
        gtmp = es.enter_context(nc.sbuf_tensor("sb_gtmp", [128, 128], F32))
        vhalo = es.enter_context(nc.sbuf_tensor("sb_vhalo", [128, 8, 32], F32))
        sbTb = es.enter_context(nc.sbuf_tensor("sb_sbTb", [128, 1, 512], BF16))
        PS = [es.enter_context(nc.psum_tensor(f"ps{i}", [128, 512], F32)) for i in range(8)]
        sems = [es.enter_context(nc.semaphore(f"s{i}")) for i in range(48)]
        S = Sched(nc, sems)
        reg_addr = nc.lookup_mloc(reg).addr
        ot_addr = nc.lookup_mloc(OTr).addr
        cnames = {}

        def carve(off_w, nwords, dt, pat=None, base=None, **kw):
            esz = 2 if dt is BF16 else 4
            nel = nwords * 4 // esz
            if pat:
                (dk, dv), = kw.items()
                shape = [128, dv, nel // dv]
            else:
                shape = [128, nel]
            i = cnames.get("n", 0)
            cnames["n"] = i + 1
            return nc.alloc_sbuf_tensor_at(f"cv{i}", shape, dt, offset=(reg_addr if base is None else base) + off_w * 4)[:]

        def cp(c, n=1):
            return colp[:, c:c + n]

        for nm, t_, d_ in (("colp", colp, colp_d), ("ident", ident, ident_d), ("onesf", onesf, onesf_d),
                           ("identb", identb, identb_d), ("onesb", onesb, onesb_d), ("cmask", cmask, cmask_d),
                           ("esel", esel, esel_d)):
            S.op("pool", (lambda e, t_=t_, d_=d_: e.dma_start(out=t_[:], in_=d_[:, :])), writes=[nm], dma="c_" + nm)
        S.op("dve", lambda e: e.memset(kmean[:], 0.0), writes=["kmean"])
        S.op("dve", lambda e: e.memset(sbTb[:], 0.0), writes=[("sbT", 0)])
        S.op("dve", lambda e: e.memset(sbTb[0:16, :, :], -1.0), writes=[("sbT", 0)])
        S.op("dve", lambda e: e.memset(epsc[:], EPS), writes=["epsc"])

        rot_state = {"all": 0, "A": 0, "B": 0}

        def bank(pool="all"):
            if pool == "all":
                i = rot_state["all"] % 8
            elif pool == "A":
                i = rot_state["A"] % 4
            else:
                i = 4 + rot_state["B"] % 4
            rot_state[pool] += 1
            return i

        wstate = {"n": 0}

        def wblock(w2d, kc, ncols):
            slot = wstate["n"] % NSLOT
            assert kc * ncols <= SLOTW
            wstate["n"] += 1
            view = wring[:, slot, 0:kc * ncols].rearrange("p (k n) -> p k n", k=kc)
            src = w2d.rearrange("(k p) n -> p k n", p=128)
            key = ("W", slot)
            S.op("sp", (lambda e: e.dma_start(out=view, in_=src)), writes=[key], dma=f"w{slot}")
            return view, key

        def mm_group(bk, out_ap, pairs, reads):
            n = len(pairs)

            def fn(e):
                ins = None
                for i, (l, r) in enumerate(pairs):
                    ins = e.matmul(out_ap, l, r, start=(i == 0), stop=(i == n - 1))
                return ins
            S.op("pe", fn, reads=reads, writes=[("PS", bk)])

        flip = {"n": 0}

        def alt():
            flip["n"] += 1
            return "act" if flip["n"] % 2 else "dve"

        def copy_evac(eng, out_ap, in_ap, reads, writes):
            if eng == "act":
                S.op("act", lambda e: e.activation(out=out_ap, in_=in_ap, func=AF.Copy), reads=reads, writes=writes)
            else:
                S.op("dve", lambda e: e.tensor_copy(out=out_ap, in_=in_ap), reads=reads, writes=writes)

        def make_hT(tiles, tile_keys, gvec, gbc, hTv, hkey, scratch=None, scratch_keys=None, loader=None, preload=None, geng="pool", junk=None, gload=True, defer_B=False):
            nt = len(tiles)
            if gload:
                S.op(geng, lambda e: e.dma_start(out=gbc, in_=gvec.partition_broadcast(128)), writes=["gbc"], dma="gbc")
            xs_of = {}

            def stageA(t):
                xt = tiles[t]
                sc = stat[:, 2 * (t % 4):2 * (t % 4) + 1]
                rs = stat[:, 2 * (t % 4) + 1:2 * (t % 4) + 2]
                skey = ("stat", t % 4)
                jv = hTv[:, :, t * 128:(t + 1) * 128]
                if junk is None:
                    S.op("act", lambda e: e.activation(out=jv, in_=xt.rearrange("p (k n) -> p k n", k=16), func=AF.Square, accum_out=sc),
                         reads=[tile_keys[t]], writes=[(hkey, t, k) for k in range(16)] + [skey])
                else:
                    S.op("act", lambda e: e.activation(out=junk, in_=xt, func=AF.Square, accum_out=sc),
                         reads=[tile_keys[t]], writes=["junkP", skey])
                S.op("act", lambda e: e.activation(out=rs, in_=sc, func=AF.Sqrt, scale=1.0 / D, bias=epsc[:]),
                     reads=[skey, "epsc"], writes=[skey])
                S.op("dve", lambda e: e.reciprocal(out=rs, in_=rs), reads=[skey], writes=[skey])
                if scratch is None:
                    xs, xskey = xt, tile_keys[t]
                else:
                    xs, xskey = scratch[t % len(scratch)], scratch_keys[t % len(scratch)]
                S.op("dve", lambda e: e.scalar_tensor_tensor(out=xs, in0=xt, scalar=rs, in1=gbc, op0=ALU.mult, op1=ALU.mult),
                     reads=[tile_keys[t], skey, "gbc"], writes=[xskey])
                xs_of[t] = (xs, xskey)

            def stageB(t):
                xs, xskey = xs_of[t]
                for b4 in range(4):
                    bk = bank("all")

                    def fn(e, b4=b4, bk=bk):
                        ins = None
                        for kk in range(4):
                            k = 4 * b4 + kk
                            ins = e.transpose(PS[bk][:, kk * 128:(kk + 1) * 128], xs[:, k * 128:(k + 1) * 128], ident[:])
                        return ins
                    S.op("pe", fn, reads=[xskey, "ident"], writes=[("PS", bk)])
                    o = hTv[:, 4 * b4:4 * b4 + 4, t * 128:(t + 1) * 128]
                    i_ = PS[bk][:, :].rearrange("p (a n) -> p a n", a=4)
                    copy_evac(alt(), o, i_, [("PS", bk)], [(hkey, t, 4 * b4 + kk) for kk in range(4)])

            if defer_B:
                for t in range(nt):
                    stageA(t)
                return lambda: [stageB(t) for t in range(nt)]
            if loader is not None:
                for t in range(min(preload, nt)):
                    loader(t)
            stageA(0)
            for t in range(nt):
                if t + 1 < nt:
                    stageA(t + 1)
                stageB(t)
                if loader is not None and t + preload < nt:
                    loader(t + preload)

        def hkeys(hkey, ntiles, ks=range(16)):
            return [(hkey, t, k) for t in range(ntiles) for k in ks]

        def load_tiles(src_rows, dst_tiles, dst_keys, tag, eng="pool"):
            for t, (dst, key) in enumerate(zip(dst_tiles, dst_keys)):
                src = src_rows[t * 128:(t + 1) * 128, :]
                S.op(eng, lambda e, dst=dst, src=src: e.dma_start(out=dst, in_=src), writes=[key], dma=f"{tag}{t}")

        def gemm_F(w, c0, ncols, kc, inT, in_reads, ntok, cb, rhs_off=0):
            cpb = SLOTW // kc // 128
            nblk = ncols // (cpb * 128)
            for b in range(nblk):
                blk, wkey = wblock(w[:, c0 + b * cpb * 128: c0 + (b + 1) * cpb * 128], kc, cpb * 128)
                for cc in range(cpb):
                    bk = bank("all")
                    pairs = [(blk[:, k, cc * 128:(cc + 1) * 128], inT[:, k, rhs_off:rhs_off + ntok]) for k in range(kc)]
                    mm_group(bk, PS[bk][:, 0:ntok], pairs, reads=[wkey] + in_reads)
                    cb(bk, b * cpb + cc)

        def gemm_T(w, r0, kc, c0, ncols, inT, in_reads, ntiles, cb, pool="all"):
            kb = min(kc, SLOTW // 512)
            nkb = kc // kb
            for n in range(ncols // 512):
                bks = [bank(pool) for _ in range(ntiles)]
                for j in range(nkb):
                    blk, wkey = wblock(w[r0 + j * kb * 128: r0 + (j + 1) * kb * 128, c0 + n * 512: c0 + (n + 1) * 512], kb, 512)
                    for t in range(ntiles):
                        bk = bks[t]

                        def fn(e, blk=blk, t=t, bk=bk, j=j):
                            ins = None
                            for kk in range(kb):
                                k = j * kb + kk
                                ins = e.matmul(PS[bk][:, :], inT[:, k, t * 128:(t + 1) * 128], blk[:, kk, :],
                                               start=(k == 0), stop=(k == kc - 1))
                            return ins
                        S.op("pe", fn, reads=[wkey] + in_reads, writes=[("PS", bk)])
                        if j == nkb - 1:
                            cb(bk, t, n)

        hT_K = carve(0, 8192, F32R, "p (k n) -> p k n", k=16)
        KT = carve(8192, 8192, BF16, "p (h n) -> p h n", h=8)
        Vt = carve(16384, 8192, BF16, "p (t n) -> p t n", t=16)
        QT = carve(24576, 4096, BF16, "p (h n) -> p h n", h=8)
        QTf = [carve(28672, 512, F32), carve(29184, 512, F32)]
        stage = [carve(t * 2048, 2048, F32, base=ot_addr) for t in range(3)]
        gbcK = carve(6144, 2048, F32, base=ot_addr)
        PTb = None

        scale_att = 1.0 / np.sqrt(128.0)

        def glu_gemm(hTv, hreads, ntok, rhs_off, cb2):
            for c in range(8):
                blkA, kA = wblock(w_in[:, A0 + c * 128: A0 + (c + 1) * 128], 16, 128)
                blkG, kG = wblock(w_in[:, G0 + c * 128: G0 + (c + 1) * 128], 16, 128)
                ba = bank("all")
                mm_group(ba, PS[ba][:, 0:ntok], [(blkA[:, k, :], hTv[:, k, rhs_off:rhs_off + ntok]) for k in range(16)],
                         reads=[kA] + hreads)
                bg = bank("all")
                mm_group(bg, PS[bg][:, 0:ntok], [(blkG[:, k, :], hTv[:, k, rhs_off:rhs_off + ntok]) for k in range(16)],
                         reads=[kG] + hreads)
                cb2(ba, bg, c)

        def gate_ops(h, qf, qk, oc):
            gb = bank("all")

            def fn(e):
                ins = None
                for tt in range(4):
                    ins = e.matmul(PS[gb][:, tt * 8:(tt + 1) * 8], qf[:, tt * 128:(tt + 1) * 128], kmean[:, h, :], start=True, stop=True)
                return ins
            S.op("pe", fn, reads=[qk, "kmean"], writes=[("PS", gb)])
            gm = gtmp[:, 0:32]
            S.op("dve", lambda e: e.tensor_tensor(out=gm, in0=PS[gb][:, 0:32], in1=cp(C_GMASK + oc * 32, 32), op=ALU.add),
                 reads=[("PS", gb), "colp"], writes=["gm"])
            top8 = gtmp[:, 32:64]
            for tt in range(4):
                S.op("dve", lambda e, tt=tt: e.max(out=top8[:, tt * 8:(tt + 1) * 8], in_=gm[:, tt * 8:(tt + 1) * 8]),
                     reads=["gm"], writes=[("top8", tt)])
            thr = gtmp[:, 96:100]
            S.op("dve", lambda e: e.tensor_scalar(out=thr, in0=top8.rearrange("p (t n) -> p t n", n=8)[:, :, 2], scalar1=-1e29, scalar2=None, op0=ALU.max),
                 reads=[("top8", tt) for tt in range(4)], writes=["thr"])
            for tt in range(4):
                t = oc * 4 + tt
                o2 = sbt[:, (t * 8 + h) * 8:(t * 8 + h) * 8 + 8]
                S.op("dve", lambda e, tt=tt, o2=o2: e.tensor_scalar(out=o2, in0=gm[:, tt * 8:(tt + 1) * 8], scalar1=thr[:, tt:tt + 1], scalar2=-1.0, op0=ALU.is_ge, op1=ALU.add),
                     reads=["gm", "thr"], writes=[("sb", t, h)])

        sgt = [None, None]

        for c in range(4):
            S.mark(c)
            src = xp if c < 2 else xo
            r0 = (c % 2) * 512
            skeys = [("stage", t % 3) for t in range(4)]
            srows = src[r0:r0 + 512, :]

            def ld(t, srows=srows):
                dst = stage[t % 3]
                S.op("pool", lambda e: e.dma_start(out=dst, in_=srows[t * 128:(t + 1) * 128, :]), writes=[("stage", t % 3)], dma=f"xs{t % 3}")
            make_hT([stage[t % 3] for t in range(4)], skeys, gmix_d, gbcK, hT_K, "hT", loader=ld, preload=3)
            hr = hkeys("hT", 4)
            S.mark(c + 0.3)

            def cb_k(bk, h, c=c):
                o = KT[:, h, c * 512:(c + 1) * 512]
                bcol = cp(C_BIN + K0 // 128 + h)
                if KSUB <= 11:
                    return
                if not os.environ.get("KSKIPACT"):
                    S.op("act", lambda e: e.activation(out=o, in_=PS[bk][:, :], func=AF.Identity, bias=bcol),
                         reads=[("PS", bk), "colp"], writes=[("KT", h, c)])
                if KSUB <= 12:
                    return
                ks = gtmp[:, 64 + 2 * (h % 8):64 + 2 * (h % 8) + 2]
                S.op("dve", lambda e: e.tensor_reduce(out=ks, in_=PS[bk][:, :].rearrange("p (b t) -> p b t", b=2), axis=AX.X, op=ALU.add),
                     reads=[("PS", bk)], writes=[("ks", h)])
                if KSUB <= 13:
                    return
                S.op("act", lambda e: e.activation(out=kmean[:, h, 2 * c:2 * c + 2], in_=ks, func=AF.Identity, scale=1.0 / 256.0, bias=bcol),
                     reads=[("ks", h), "colp", "kmean"], writes=["kmean"])
            gemm_F(w_in, K0, 1024, 16, hT_K, hr, 512, cb_k)
            S.mark(c + 0.6)

            def cb_v(bk, t, n, c=c):
                o = Vt[:, c * 4 + t, n * 512:(n + 1) * 512]
                copy_evac(alt(), o, PS[bk][:, :], [("PS", bk)], [("V", c * 4 + t, n)])
            gemm_T(w_in, 0, 16, V0, 1024, hT_K, hr, 4, cb_v)

            if c == 1:
                def cb_halo(ba, bg, cch):
                    sg = QTf[0][:, 0:256]
                    S.op("act", lambda e: e.activation(out=sg, in_=PS[bg][:, 0:256], func=AF.Sigmoid, bias=cp(C_BIN + G0 // 128 + cch)),
                         reads=[("PS", bg), "colp"], writes=["halo_sg"])
                    hv = QTf[1][:, 0:256]
                    S.op("dve", lambda e: e.scalar_tensor_tensor(out=hv, in0=PS[ba][:, 0:256], scalar=cp(C_BIN + cch), in1=sg, op0=ALU.add, op1=ALU.mult),
                         reads=[("PS", ba), "halo_sg", "colp"], writes=["halo_v"])
                    S.op("dve", lambda e: e.tensor_scalar(out=vhalo[:, cch, :], in0=hv[:, 224:256], scalar1=cp(C_FLAG), scalar2=None, op0=ALU.mult),
                         reads=["halo_v", "colp"], writes=[("vhalo", cch)])
                glu_gemm(hT_K, hr, 256, 256, cb_halo)

            if c >= 2:
                oc = c - 2
                S.alias({"halo_sg", "halo_v"}, [("QTf", 0), ("QTf", 1)])

                def cb_q(bk, h, oc=oc):
                    o = QT[:, h, oc * 512:(oc + 1) * 512]
                    bcol = cp(C_BIN + Q0 // 128 + h)
                    S.op("act", lambda e: e.activation(out=o, in_=PS[bk][:, :], func=AF.Identity, bias=bcol),
                         reads=[("PS", bk), "colp"], writes=[("QT", h, oc)])
                    qf = QTf[h % 2]
                    qk = ("QTf", h % 2)
                    S.op("dve", lambda e: e.tensor_scalar(out=qf, in0=PS[bk][:, :], scalar1=bcol, scalar2=None, op0=ALU.add),
                         reads=[("PS", bk), "colp"], writes=[qk])
                    prev = pend_gate.pop() if pend_gate else None
                    pend_gate.append(lambda h=h, qf=qf, qk=qk, oc=oc: gate_ops(h, qf, qk, oc))
                    if prev is not None:
                        prev()
                pend_gate = []
                gemm_F(w_in, Q0, 1024, 16, hT_K, hr, 512, cb_q)
                while pend_gate:
                    pend_gate.pop()()

        S.mark(4)
        S.alias({"hT", "stage", "gbc"}, [("PT", i) for i in range(4)] + [("rden", i) for i in range(2)] + [("otmp", i) for i in range(2)])
        PTb = [carve(i * 256, 256, BF16) for i in range(4)]
        rden = [carve(1024 + i * 512, 512, F32) for i in range(2)]
        otmp = [carve(2048 + i * 512, 512, F32) for i in range(2)]
        def prep_sT(h, pr):
            tb = bank("A")

            def fn(e):
                ins = None
                for e2 in range(4):
                    t = 4 * pr + e2
                    ins = e.transpose(PS[tb][0:8, e2 * 128:(e2 + 1) * 128], sbt[:, (t * 8 + h) * 8:(t * 8 + h) * 8 + 8], ident[:])
                return ins
            S.op("pe", fn, reads=[("sb", 4 * pr + e2, h) for e2 in range(4)] + ["ident"], writes=[("PS", tb)])
            copy_evac("dve", sbTb[0:8, 0, :], PS[tb][0:8, 0:512], [("PS", tb)], [("sbT", 0)])

        hj = 0
        ptc = {"n": 0}
        S.alias({"stage", "gbc"}, [("OT", h, p_) for h in range(8) for p_ in range(2)])
        for h in range(8):
            for pr in range(2):
                if hj == 0:
                    prep_sT(h, pr)
                sT = sbTb[:, 0, :]
                sTk = ("sbT", 0)
                ob = 4 + 2 * (hj % 2)
                db = ob + 1
                nblk = 6 + 2 * pr
                qsl = QT[:, h, pr * 512:(pr + 1) * 512]
                qk = ("QT", h, pr)
                visits = [(n, kt) for n in range(nblk) for kt in range(2)]

                def emit_S(v, h=h, pr=pr, qsl=qsl, qk=qk, sT=sT, sTk=sTk):
                    n, kt = v
                    sb_ = bank("A")
                    own0 = 4 + 2 * pr
                    own1 = 5 + 2 * pr

                    def fn(e):
                        o = PS[sb_][:, :]
                        o0 = PS[sb_][:, 0:256]
                        o1 = PS[sb_][:, 256:512]
                        e.matmul(o, KT[:, h, n * 256 + kt * 128: n * 256 + (kt + 1) * 128], qsl, start=True, stop=False)
                        if n < own0:
                            return e.matmul(o, esel[:, n * 128:(n + 1) * 128], sT, start=False, stop=True)
                        if n == own0:
                            e.matmul(o0, identb[:], cmask[:, kt * 256:(kt + 1) * 256], start=False, stop=False)
                            return e.matmul(o1, esel[:, n * 128:(n + 1) * 128], sT[:, 256:512], start=False, stop=True)
                        e.matmul(o0, esel[:, 8 * 128:9 * 128], sT[:, 0:256], start=False, stop=False)
                        return e.matmul(o1, identb[:], cmask[:, kt * 256:(kt + 1) * 256], start=False, stop=True)
                    S.op("pe", fn, reads=[("KT", h, n // 2), qk, "identb", "cmask", "esel", sTk], writes=[("PS", sb_)])
                    pi = ptc["n"] % 4
                    ptc["n"] += 1
                    pt = PTb[pi]
                    S.op("act", lambda e: e.activation(out=pt, in_=PS[sb_][:, :], func=AF.Exp, scale=float(scale_att)),
                         reads=[("PS", sb_)], writes=[("PT", pi)])
                    return pi

                def emit_PV(v, pi, first, last, h=h, ob=ob, db=db):
                    n, kt = v
                    pt = PTb[pi]

                    def fn(e):
                        e.matmul(PS[ob][:, :], Vt[:, n * 2 + kt, h * 128:(h + 1) * 128], pt, start=first, stop=last)
                        return e.matmul(PS[db][:, :], onesb[:], pt, start=first, stop=last)
                    S.op("pe", fn, reads=[("PT", pi), ("V", n * 2 + kt, h // 4), "onesb"], writes=[("PS", ob), ("PS", db)])
                nv = len(visits)
                pis = {}
                for v in range(min(2, nv)):
                    pis[v] = emit_S(visits[v])
                for v in range(nv):
                    if v + 2 < nv:
                        pis[v + 2] = emit_S(visits[v + 2])
                        if v + 2 == nv - 1 and hj + 1 < 16:
                            nh_, npr_ = divmod(hj + 1, 2)
                            prep_sT(nh_, npr_)
                    emit_PV(visits[v], pis[v], v == 0, v == nv - 1)
                rd_ = rden[hj % 2]
                ot_ = otmp[hj % 2]
                S.op("dve", lambda e, rd_=rd_, db=db: e.reciprocal(out=rd_, in_=PS[db][:, :]), reads=[("PS", db)], writes=[("rden", hj % 2)])
                S.op("dve", lambda e, rd_=rd_, ot_=ot_, ob=ob: e.tensor_tensor(out=ot_, in0=PS[ob][:, :], in1=rd_, op=ALU.mult),
                     reads=[("PS", ob), ("rden", hj % 2)], writes=[("otmp", hj % 2)])
                oo = OTr[:, h, pr * 512:(pr + 1) * 512]
                S.op("act", lambda e, oo=oo, ot_=ot_, h=h: e.activation(out=oo, in_=ot_, func=AF.Identity, bias=cp(C_BIN + V0 // 128 + h)),
                     reads=[("otmp", hj % 2), "colp"], writes=[("OT", h, pr)])
                hj += 1
        if DEBUG:
            S.op("pool", lambda e: e.dma_start(out=dbg_d["OT"], in_=OTr[:].bitcast(F32).rearrange("p h n -> p (h n)")),
                 reads=[("OT", h, p_) for h in range(8) for p_ in range(2)], writes=["dbgOT"], dma="dbgOT")
            S.op("pool", lambda e: e.dma_start(out=dbg_d["SB"], in_=sbt[:]),
                 reads=[("sb", t, h) for t in range(8) for h in range(8)], writes=["dbgSB"], dma="dbgSB")

        B1 = 0
        B2 = 8192
        B3 = 16384
        hT_P = carve(B1, 8192, F32R, "p (k n) -> p k n", k=16)
        X = carve(B1, 8192, F32, "p (t n) -> p t n", t=4)
        xstage = [carve(B2 + t * 2048, 2048, F32) for t in range(4)]
        mergedTr = carve(B2, 8192, F32R, "p (k n) -> p k n", k=16)
        ysq = carve(B2, 4096, F32, "p (c n) -> p c n", c=8)
        lnt = [carve(B2 + 4096 + i * 512, 512, F32) for i in range(4)]
        sgtmp = [carve(B2 + 6144 + i * 512, 512, F32) for i in range(2)]
        vglu = carve(B3, 8 * 544 // 2, BF16, "p (c n) -> p c n", c=8)
        dg = [carve(B2 + i * 2048, 1984, BF16, "p (k n) -> p k n", k=31) for i in range(2)]
        ycT = carve(B3 + 4352, 4096, F32, "p (c n) -> p c n", c=8)
        ycTr = carve(B3, 4096, F32R, "p (c n) -> p c n", c=8)
        mtmp = [carve(B3 + 8448 + i * 512, 512, F32) for i in range(2)]
        memst = [carve(B3 + t * 2048, 2048, F32) for t in range(2)]
        mT = carve(B2 + 4096, 4096, F32R, "p (k n) -> p k n", k=16)
        h2T = carve(B2, 8192, F32R, "p (k n) -> p k n", k=16)
        o2T = carve(B2, 8192, F32R, "p (k n) -> p k n", k=16)
        memKT = carve(B3, 2048, BF16, "p (c n) -> p c n", c=16)
        memV = carve(B3 + 2048, 2048, BF16, "p (t n) -> p t n", t=2)
        q2T = carve(B3 + 4096, 4096, BF16, "p (c n) -> p c n", c=16)
        xscr = [carve(B3 + 4096 + i * 2048, 2048, F32) for i in range(2)]
        P2T = [carve(B3 + 8192 + i * 256, 256, BF16) for i in range(2)]
        rden2 = [carve(B3 + 8704 + i * 512, 512, F32) for i in range(1)]
        h3T = carve(B2, 8192, F32R, "p (k n) -> p k n", k=16)
        actT = [carve(B3 + i * 2048, 2048, F32R, "p (c n) -> p c n", c=4) for i in range(2)]
        fsg = [carve(B3 + 4096 + i * 512, 512, F32) for i in range(2)]
        fscr = [carve(B3 + 5120 + i * 2048, 2048, F32) for i in range(2)]
        gfin = carve(B3, 2048, F32)
        obuf = [carve(B3 + 2048 + i * 2048, 2048, F32) for i in range(2)]

        gbcP = carve(25856, 2048, F32)
        scale_x = 1.0 / np.sqrt(512.0)
        ALLB = {"hT", "stage", "gbc", "PT", "rden", "otmp", "KT", "V", "QT", "QTf", "halo_sg", "halo_v"}
        PASSK = {"hTP", "X", "xstage", "mergedT", "ysq", "lnt", "sgtmp", "vglu", "vgh", "ycT", "ycTr", "dg", "dgk", "mtmp", "memst", "mT", "h2T", "o2T",
                 "memKT", "memV", "q2T", "xscr", "P2T", "rden2", "h3T", "actT", "fsg", "fscr", "gfin", "obuf"}

        REGS = {
            "B1": {"hTP", "X"},
            "B2": {"xstage", "mergedT", "ysq", "lnt", "sgtmp", "dg", "dgk", "mT", "h2T", "o2T", "h3T"},
            "B3": {"vglu", "vgh", "ycT", "ycTr", "mtmp", "memst", "memKT", "memV", "q2T", "xscr", "P2T", "rden2", "actT", "fsg", "fscr",
                   "gfin", "obuf"},
            "G": {"gbc", "junkP"},
        }
        REG_OF = {fam: r for r, fams in REGS.items() for fam in fams}
        pass_state = {"p": 0}

        def region_alias(newkeys):
            regs = set()
            for k in newkeys:
                fam = k[0] if isinstance(k, tuple) else k
                regs.add(REG_OF[fam])
            old = set()
            for r in regs:
                old |= REGS[r]
            if pass_state["p"] == 0:
                old |= ALLB
            if os.environ.get("KGLOBAL"):
                old = ALLB | PASSK | {"junkP"}
            S.alias(old, newkeys)

        junkP = carve(27904, 1024, BF16)
        for p in range(2):
            pass_state["p"] = p
            S.mark(5 + 10 * p)
            xk = [("xstage", t) for t in range(4)]
            region_alias(xk)
            load_tiles(xo[p * 512:(p + 1) * 512, :], [t_[:] for t_ in xstage], xk, "xp", eng="sp")
            region_alias(hkeys("hTP", 4))
            region_alias(["gbc", "junkP"])
            make_hT([t_[:] for t_ in xstage], xk, gmix_d, gbcP, hT_P, "hTP", junk=junkP, gload=(p == 0))
            hr = hkeys("hTP", 4)

            region_alias([("vglu", c) for c in range(8)] + [("sgtmp", i) for i in range(2)])
            for c in range(8):
                S.op("pool", lambda e, c=c: e.tensor_copy(out=vglu[:, c, 0:32], in_=vhalo[:, c, :]),
                     reads=[("vhalo", c)], writes=[("vgh", c)])

            def cb_glu(ba, bg, c):
                i = c % 2
                S.op("act", lambda e: e.activation(out=sgtmp[i], in_=PS[bg][:, :], func=AF.Sigmoid, bias=cp(C_BIN + G0 // 128 + c)),
                     reads=[("PS", bg), "colp"], writes=[("sgtmp", i)])
                S.op("dve", lambda e: e.scalar_tensor_tensor(out=vglu[:, c, 32:544], in0=PS[ba][:, :], scalar=cp(C_BIN + c), in1=sgtmp[i], op0=ALU.add, op1=ALU.mult),
                     reads=[("PS", ba), ("sgtmp", i), "colp"], writes=[("vglu", c)])
            glu_gemm(hT_P, hr, 512, 0, cb_glu)
            region_alias([("ycT", c) for c in range(8)] + [("dg", i) for i in range(2)] + [("dgk", i, k) for i in range(2) for k in range(1, 31)])
            for c in range(8):
                d_ = dg[c % 2]
                dk = ("dg", c % 2)
                S.op("dve", lambda e, d_=d_, c=c: e.tensor_scalar(out=d_[:, 0, :], in0=identb[:], scalar1=cp(C_CONVW + c * 31), scalar2=None, op0=ALU.mult),
                     reads=["identb", "colp"], writes=[dk])
                for k in range(1, 31):
                    eng = "dve" if k % 2 == 0 else "pool"
                    S.op(eng, lambda e, d_=d_, c=c, k=k: e.tensor_scalar(out=d_[:, k, :], in0=identb[:], scalar1=cp(C_CONVW + c * 31 + k), scalar2=1.0, op0=ALU.mult, op1=ALU.mult),
                         reads=["identb", "colp"], writes=[("dgk", c % 2, k)])
                bk = bank("all")
                mm_group(bk, PS[bk][:, :], [(d_[:, k, :], vglu[:, c, 2 + k:514 + k]) for k in range(31)],
                         reads=[dk] + [("dgk", c % 2, k) for k in range(1, 31)] + [("vglu", c), ("vgh", c)])
                S.op("act", lambda e, c=c, bk=bk: e.activation(out=ycT[:, c, :], in_=PS[bk][:, :], func=AF.Identity, bias=cp(C_CONVB + c)),
                     reads=[("PS", bk), "colp"], writes=[("ycT", c)])
                if p == 0:
                    S.op("pool", lambda e, c=c: e.tensor_copy(out=vhalo[:, c, :], in_=vglu[:, c, 512:544]),
                         reads=[("vglu", c)], writes=[("vhalo", c)])
            region_alias([("ysq", c) for c in range(8)] + [("lnt", i) for i in range(4)])
            for c in range(8):
                S.op("act", lambda e, c=c: e.activation(out=ysq[:, c, :], in_=ycT[:, c, :], func=AF.Square),
                     reads=[("ycT", c)], writes=[("ysq", c)])
            bm = bank("all")
            mm_group(bm, PS[bm][:, :], [(onesf[:], ycT[:, c, :]) for c in range(8)], reads=["onesf"] + [("ycT", c) for c in range(8)])
            be = bank("all")
            mm_group(be, PS[be][:, :], [(onesf[:], ysq[:, c, :]) for c in range(8)], reads=["onesf"] + [("ysq", c) for c in range(8)])
            mean_sb, msq, var_, rstd_ = lnt
            S.op("act", lambda e, bm=bm: e.activation(out=mean_sb, in_=PS[bm][:, :], func=AF.Copy), reads=[("PS", bm)], writes=[("lnt", 0)])
            S.op("dve", lambda e: e.tensor_tensor(out=msq, in0=mean_sb, in1=mean_sb, op=ALU.mult), reads=[("lnt", 0)], writes=[("lnt", 1)])
            S.op("dve", lambda e, be=be: e.tensor_tensor(out=var_, in0=PS[be][:, :], in1=msq, op=ALU.subtract), reads=[("PS", be), ("lnt", 1)], writes=[("lnt", 2)])
            S.op("act", lambda e: e.activation(out=var_, in_=var_, func=AF.Sqrt, bias=epsc[:]), reads=[("lnt", 2), "epsc"], writes=[("lnt", 2)])
            S.op("dve", lambda e: e.reciprocal(out=rstd_, in_=var_), reads=[("lnt", 2)], writes=[("lnt", 3)])
            region_alias([("ycTr", c) for c in range(8)])
            for c in range(8):
                y = ycT[:, c, :]
                S.op("dve", lambda e, y=y: e.tensor_tensor(out=y, in0=y, in1=mean_sb, op=ALU.subtract), reads=[("ycT", c), ("lnt", 0)], writes=[("ycT", c)])
                S.op("dve", lambda e, y=y: e.tensor_tensor(out=y, in0=y, in1=rstd_, op=ALU.mult), reads=[("ycT", c), ("lnt", 3)], writes=[("ycT", c)])
                S.op("act", lambda e, y=y, c=c: e.activation(out=ycTr[:, c, :], in_=y, func=AF.Silu, scale=cp(C_LNG + c), bias=cp(C_LNB + c)),
                     reads=[("ycT", c), "colp"], writes=[("ycTr", c)])
            if DEBUG and p == 0:
                S.op("pool", lambda e: e.dma_start(out=dbg_d["YC"], in_=ycTr[:].bitcast(F32).rearrange("p c n -> p (c n)")),
                     reads=[("ycTr", c) for c in range(8)], writes=["dbgYC"], dma="dbgYC")

            S.mark(6 + 10 * p)
            region_alias([("mergedT", f) for f in range(16)] + [("mtmp", i) for i in range(2)])
            for sweep in range(2):
                wA = w_conv_out if sweep == 0 else w_att_out
                g0 = GC0 if sweep == 0 else GA0
                for i in range(8):
                    blkY, kY = wblock(wA[:, i * 256:(i + 1) * 256], 8, 256)
                    for cc in range(2):
                        if True:
                            f = 2 * i + cc
                            blkG, kG = wblock(w_in[:, g0 + f * 128: g0 + (f + 1) * 128], 16, 128)
                            by = bank("all")
                            if sweep == 0:
                                pairs = [(blkY[:, k, cc * 128:(cc + 1) * 128], ycTr[:, k, :]) for k in range(8)]
                                rd = [kY] + [("ycTr", k) for k in range(8)]
                            else:
                                pairs = [(blkY[:, k, cc * 128:(cc + 1) * 128], OTr[:, k, p * 512:(p + 1) * 512]) for k in range(8)]
                                rd = [kY] + [("OT", k, p) for k in range(8)]
                            mm_group(by, PS[by][:, :], pairs, reads=rd)
                            bg = bank("all")
                            mm_group(bg, PS[bg][:, :], [(blkG[:, k, :], hT_P[:, k, :]) for k in range(16)], reads=[kG] + hr)
                            mt_ = mtmp[f % 2]
                            mk = ("mtmp", f % 2)
                            S.op("act", lambda e, mt_=mt_, bg=bg, f=f, g0=g0: e.activation(out=mt_, in_=PS[bg][:, :], func=AF.Sigmoid, bias=cp(C_BIN + g0 // 128 + f)),
                                 reads=[("PS", bg), "colp"], writes=[mk])
                            if sweep == 0:
                                S.op("dve", lambda e, mt_=mt_, by=by, f=f: e.tensor_tensor(out=mergedTr[:, f, :], in0=PS[by][:, :], in1=mt_, op=ALU.mult),
                                     reads=[("PS", by), mk], writes=[("mergedT", f)])
                            else:
                                S.op("dve", lambda e, mt_=mt_, by=by: e.tensor_tensor(out=mt_, in0=PS[by][:, :], in1=mt_, op=ALU.mult),
                                     reads=[("PS", by), mk], writes=[mk])
                                S.op("dve", lambda e, mt_=mt_, f=f: e.tensor_tensor(out=mergedTr[:, f, :], in0=mergedTr[:, f, :].bitcast(F32), in1=mt_, op=ALU.add),
                                     reads=[("mergedT", f), mk], writes=[("mergedT", f)])

            S.mark(7 + 10 * p)
            mk_ = [("memst", t) for t in range(2)]
            region_alias(mk_)
            load_tiles(memd[:, :], [t_[:] for t_ in memst], mk_, "ms")
            mT_stageB = make_hT([t_[:] for t_ in memst], mk_, gmem_d, gbcP, mT, "mT", junk=junkP, defer_B=True)
            region_alias([("X", t) for t in range(4)])
            Xk = [("X", t) for t in range(4)]
            load_tiles(xo[p * 512:(p + 1) * 512, :], [X[:, t, :] for t in range(4)], Xk, "xr")

            def cb_res(bk, t, n):
                xs_ = X[:, t, n * 512:(n + 1) * 512]
                S.op("dve", lambda e: e.tensor_tensor(out=xs_, in0=PS[bk][:, :], in1=xs_, op=ALU.add),
                     reads=[("PS", bk), ("X", t)], writes=[("X", t)])
            gemm_T(w_out, 0, 16, 0, D, mergedTr, [("mergedT", f) for f in range(16)], 4, cb_res)
            if DEBUG and p == 0:
                S.op("pool", lambda e: e.dma_start(out=dbg_d["X1"], in_=X[:].rearrange("p t n -> p (t n)")),
                     reads=Xk, writes=["dbgX1"], dma="dbgX1")

            S.mark(8 + 10 * p)
            region_alias(hkeys("mT", 2))
            mT_stageB()
            mr = hkeys("mT", 2)
            region_alias([("memKT", c) for c in range(16)] + [("memV", t, n) for t in range(2) for n in range(4)])

            def cb_mk(bk, c):
                copy_evac(alt(), memKT[:, c, :], PS[bk][:, 0:256], [("PS", bk)], [("memKT", c)])
            gemm_F(w_ckv, 0, D, 16, mT, mr, 256, cb_mk)

            def cb_mv(bk, t, n):
                copy_evac(alt(), memV[:, t, n * 512:(n + 1) * 512], PS[bk][:, :], [("PS", bk)], [("memV", t, n)])
            gemm_T(w_ckv, 0, 16, D, D, mT, mr, 2, cb_mv)

            region_alias(hkeys("h2T", 4) + [("xscr", i) for i in range(2)])
            make_hT([X[:, t, :] for t in range(4)], Xk, gcross_d, gbcP, h2T, "h2T", scratch=[t_[:] for t_ in xscr], scratch_keys=[("xscr", i) for i in range(2)])
            region_alias([("q2T", c) for c in range(16)])

            def cb_q2(bk, c):
                copy_evac(alt(), q2T[:, c, :], PS[bk][:, :], [("PS", bk)], [("q2T", c)])
            gemm_F(w_cq, 0, D, 16, h2T, hkeys("h2T", 4), 512, cb_q2)
            region_alias([("o2T", c) for c in range(16)] + [("P2T", i) for i in range(2)] + [("rden2", 0)])
            for hh in range(4):
                for mt in range(2):
                    sb_ = bank("all")
                    mm_group(sb_, PS[sb_][:, :], [(memKT[:, 4 * hh + dc, mt * 128:(mt + 1) * 128], q2T[:, 4 * hh + dc, :]) for dc in range(4)],
                             reads=[("memKT", 4 * hh + dc) for dc in range(4)] + [("q2T", 4 * hh + dc) for dc in range(4)])
                    S.op("act", lambda e, sb_=sb_, mt=mt: e.activation(out=P2T[mt], in_=PS[sb_][:, :], func=AF.Exp, scale=float(scale_x)),
                         reads=[("PS", sb_)], writes=[("P2T", mt)])
                db_ = bank("all")
                mm_group(db_, PS[db_][:, :], [(onesb[:], P2T[mt]) for mt in range(2)], reads=["onesb", ("P2T", 0), ("P2T", 1)])
                S.op("dve", lambda e, db_=db_: e.reciprocal(out=rden2[0], in_=PS[db_][:, :]), reads=[("PS", db_)], writes=[("rden2", 0)])
                for dc in range(4):
                    c = 4 * hh + dc
                    ob_ = bank("all")
                    mm_group(ob_, PS[ob_][:, :], [(memV[:, mt, c * 128:(c + 1) * 128], P2T[mt]) for mt in range(2)],
                             reads=[("P2T", 0), ("P2T", 1)] + [("memV", mt, c // 4) for mt in range(2)])
                    S.op("dve", lambda e, ob_=ob_, c=c: e.tensor_tensor(out=o2T[:, c, :], in0=PS[ob_][:, :], in1=rden2[0], op=ALU.mult),
                         reads=[("PS", ob_), ("rden2", 0)], writes=[("o2T", c)])
            gemm_T(w_co, 0, 16, 0, D, o2T, [("o2T", c) for c in range(16)], 4, cb_res)
            if DEBUG and p == 0:
                S.op("pool", lambda e: e.dma_start(out=dbg_d["X2"], in_=X[:].rearrange("p t n -> p (t n)")),
                     reads=Xk, writes=["dbgX2"], dma="dbgX2")

            S.mark(9 + 10 * p)
            region_alias(hkeys("h3T", 4) + [("fscr", i) for i in range(2)])
            make_hT([X[:, t, :] for t in range(4)], Xk, gffn_d, gbcP, h3T, "h3T", scratch=[t_[:] for t_ in fscr], scratch_keys=[("fscr", i) for i in range(2)])
            h3r = hkeys("h3T", 4)
            if p == 0:
                S.op("pool", lambda e: e.dma_start(out=gbcP, in_=gmix_d.partition_broadcast(128)), writes=["gbc"], dma="gbc")
            region_alias([("actT", i, fc) for i in range(2) for fc in range(4)] + [("fsg", i) for i in range(2)])
            for fg in range(11):
                ab = fg % 2
                for fc in range(4):
                    if True:
                        blkG, kG = wblock(w_ffn_in[:, fg * 512 + fc * 128: fg * 512 + (fc + 1) * 128], 16, 128)
                        blkU, kU = wblock(w_ffn_in[:, DFF + fg * 512 + fc * 128: DFF + fg * 512 + (fc + 1) * 128], 16, 128)
                        bg = bank("A")
                        mm_group(bg, PS[bg][:, :], [(blkG[:, k, :], h3T[:, k, :]) for k in range(16)], reads=[kG] + h3r)
                        bu = bank("A")
                        mm_group(bu, PS[bu][:, :], [(blkU[:, k, :], h3T[:, k, :]) for k in range(16)], reads=[kU] + h3r)
                        sg_ = fsg[fc % 2]
                        S.op("act", lambda e, sg_=sg_, bg=bg: e.activation(out=sg_, in_=PS[bg][:, :], func=AF.Silu),
                             reads=[("PS", bg)], writes=[("fsg", fc % 2)])
                        S.op("dve", lambda e, sg_=sg_, bu=bu, ab=ab, fc=fc: e.tensor_tensor(out=actT[ab][:, fc, :], in0=PS[bu][:, :], in1=sg_, op=ALU.mult),
                             reads=[("PS", bu), ("fsg", fc % 2)], writes=[("actT", ab, fc)])
                for n4 in range(4):
                    blkO, kO = wblock(w_ffn_out[fg * 512:(fg + 1) * 512, n4 * 512:(n4 + 1) * 512], 4, 512)
                    for t in range(4):
                        bo = bank("B")
                        mm_group(bo, PS[bo][:, :], [(actT[ab][:, fc, t * 128:(t + 1) * 128], blkO[:, fc, :]) for fc in range(4)],
                                 reads=[kO] + [("actT", ab, fc) for fc in range(4)])
                        cb_res(bo, t, n4)

            S.mark(10 + 10 * p)
            region_alias(["gfin"] + [("obuf", i) for i in range(2)])
            S.op("pool", lambda e: e.dma_start(out=gfin[:], in_=gfin_d.partition_broadcast(128)), writes=["gfin"], dma="gfin")
            for t in range(4):
                sc = stat[:, 16 + 2 * t:16 + 2 * t + 1]
                rs = stat[:, 16 + 2 * t + 1:16 + 2 * t + 2]
                skey = ("fstat", t)
                xt = X[:, t, :]
                ob_ = obuf[t % 2]
                S.op("act", lambda e, xt=xt, sc=sc, ob_=ob_: e.activation(out=ob_, in_=xt, func=AF.Square, accum_out=sc),
                     reads=[("X", t)], writes=[("obuf", t % 2), skey])
                S.op("act", lambda e, sc=sc, rs=rs: e.activation(out=rs, in_=sc, func=AF.Sqrt, scale=1.0 / D, bias=epsc[:]),
                     reads=[skey, "epsc"], writes=[skey])
                S.op("dve", lambda e, rs=rs: e.reciprocal(out=rs, in_=rs), reads=[skey], writes=[skey])
                S.op("dve", lambda e, ob_=ob_, xt=xt, rs=rs: e.scalar_tensor_tensor(out=ob_, in0=xt, scalar=rs, in1=gfin[:], op0=ALU.mult, op1=ALU.mult),
                     reads=[("X", t), skey, "gfin"], writes=[("obuf", t % 2)])
                dst = out_d[p * 512 + t * 128: p * 512 + (t + 1) * 128, :]
                S.op("pool", lambda e, ob_=ob_, dst=dst: e.dma_start(out=dst, in_=ob_), reads=[("obuf", t % 2)], writes=[("outd", p, t)], dma=f"out{t % 2}")

        S.stopped = False
        S.final_waits("pool")

        with nc.Block() as block:
            @block.tensor
            def _(e):
                S.replay("pe", e)

            @block.scalar
            def _(e):
                S.replay("act", e)

            @block.vector
            def _(e):
                S.replay("dve", e)

            @block.gpsimd
            def _(e):
                S.replay("pool", e)

            @block.sync
            def _(e):
                S.replay("sp", e)
    return nc


_CACHE = {}


def _consts():
    bf = ml_dtypes.bfloat16
    ident = np.eye(128, dtype=np.float32)
    onesf = np.full((128, 128), 1.0 / 1024.0, dtype=np.float32)
    identb = np.eye(128).astype(bf)
    onesb = np.ones((128, 128)).astype(bf)
    cm = np.zeros((128, 2, 256), dtype=np.float32)
    for kt in range(2):
        key = kt * 128 + np.arange(128)[:, None]
        q = np.arange(256)[None, :]
        cm[:, kt, :] = np.where(key <= q, 0.0, NEG)
    cmask = cm.reshape(128, 512).astype(bf)
    es = np.zeros((128, 9, 128), dtype=np.float32)
    for n in range(9):
        es[n, n, :] = -NEG
    esel = es.reshape(128, 9 * 128).astype(bf)
    return dict(ident=ident, onesf=onesf, identb=identb, onesb=onesb, cmask=cmask, esel=esel)


def kernel(x, mem, norm_mix_g, w_in, b_in, conv_w, conv_b, conv_ln_g, conv_ln_b,
           w_conv_out, w_att_out, w_out, norm_cross_g, norm_mem_g, w_cq, w_ckv,
           w_co, norm_ffn_g, w_ffn_in, w_ffn_out, norm_final_g):
    f = lambda a: np.ascontiguousarray(np.asarray(a, dtype=np.float32))
    x = f(x); mem = f(mem)
    if "nc" not in _CACHE:
        _CACHE["nc"] = build_program()
    nc = _CACHE["nc"]
    consts = _consts()

    def col(v, k):
        return np.asarray(v, np.float32).reshape(k, 128).T

    shared = dict(
        w_in=f(w_in[0]), w_conv_out=f(w_conv_out[0]), w_att_out=f(w_att_out[0]), w_out=f(w_out[0]),
        w_cq=f(w_cq[0]), w_ckv=f(w_ckv[0]), w_co=f(w_co[0]), w_ffn_in=f(w_ffn_in[0]), w_ffn_out=f(w_ffn_out[0]),
        gfin=f(norm_final_g), gmix=f(norm_mix_g[0]), gcross=f(norm_cross_g[0]), gmem=f(norm_mem_g[0]),
        gffn=f(norm_ffn_g[0]), **consts)
    base = np.zeros((128, NCOLP), np.float32)
    base[:, C_BIN:C_BIN + 72] = col(b_in[0], 72)
    base[:, C_GMIX:C_GMIX + 16] = col(norm_mix_g[0], 16)
    base[:, C_GCROSS:C_GCROSS + 16] = col(norm_cross_g[0], 16)
    base[:, C_GMEM:C_GMEM + 16] = col(norm_mem_g[0], 16)
    base[:, C_GFFN:C_GFFN + 16] = col(norm_ffn_g[0], 16)
    cw = np.asarray(conv_w[0], np.float32)
    base[:, C_CONVW:C_CONVW + 248] = cw.reshape(31, 8, 128).transpose(2, 1, 0).reshape(128, 248)
    base[:, C_CONVB:C_CONVB + 8] = col(conv_b[0], 8)
    base[:, C_LNG:C_LNG + 8] = col(conv_ln_g[0], 8)
    base[:, C_LNB:C_LNB + 8] = col(conv_ln_b[0], 8)
    in_maps = []
    for core in range(8):
        b, half = core // 2, core % 2
        cpm = base.copy()
        cpm[:, C_FLAG] = float(half)
        gm = np.zeros((8, 8), np.float32)
        for t in range(8):
            for n in range(8):
                valid = (n < 4 + t // 2) and (n >= 4 or half == 1)
                gm[t, n] = 0.0 if valid else -1e30
        cpm[:, C_GMASK:C_GMASK + 64] = gm.reshape(1, 64)
        m = dict(shared)
        m["xo"] = np.ascontiguousarray(x[b, half * 1024:(half + 1) * 1024])
        m["xp"] = np.ascontiguousarray(x[b, 0:1024])
        m["mem"] = np.ascontiguousarray(mem[b])
        m["colp"] = cpm
        in_maps.append(m)
    declared = set()
    for alloc in nc.allocations:
        if isinstance(alloc, mybir.MemoryLocationSet) and alloc.kind == "ExternalInput":
            declared.add(alloc.memorylocations[0].name)
    in_maps = [{k: v for k, v in m.items() if k in declared} for m in in_maps]
    cores = [int(c) for c in KCORES.split(",")] if KCORES else list(range(8))
    res = run_bass_kernel_spmd(nc, [in_maps[c] for c in cores], core_ids=list(range(len(cores))))
    _CACHE["last"] = res
    out = np.zeros((4, 2048, 2048), np.float32)
    for i, core in enumerate(cores):
        b, half = core // 2, core % 2
        out[b, half * 1024:(half + 1) * 1024] = res.results[i]["out"]
    return out
```

```python
import numpy as np
import ml_dtypes
import concourse.bass as bass
import concourse.mybir as mybir
from concourse.bass_utils import run_bass_kernel_spmd

F32 = mybir.dt.float32
F32R = mybir.dt.float32r
BF16 = mybir.dt.bfloat16
AF = mybir.ActivationFunctionType
ALU = mybir.AluOpType
AX = mybir.AxisListType

D = 2048
S_OWN = 1024
TCH = 512
A0, G0, Q0, K0, V0, GC0, GA0 = 0, 1024, 2048, 3072, 4096, 5120, 7168
DFF = 5632
EPS = 1e-6
NEG = -32768.0

C_BIN = 0
C_GMIX = 72
C_GCROSS = 88
C_GMEM = 104
C_GFFN = 120
C_CONVW = 136
C_CONVB = 384
C_LNG = 392
C_LNB = 400
C_FLAG = 408
C_GMASK = 409
NCOLP = 473

DEBUG = False
NSLOT = 6
SLOTW = 2048
import os
KSTAGE = float(os.environ.get("KSTAGE", "99"))
KCORES = os.environ.get("KCORES", "")
KSUB = int(os.environ.get("KSUB", "99"))


class Sched:
    def __init__(self, nc, sems):
        self.nc = nc
        self.free_sems = list(sems)
        self.engs = ["pe", "act", "dve", "pool", "sp"]
        self.streams = {e: [] for e in self.engs}
        self.esem = {e: self.free_sems.pop() for e in self.engs}
        self.ecnt = {e: 0 for e in self.engs}
        self.waited = {e: {} for e in self.engs}
        self.keys = {}
        self.dsem = {}
        self.dcnt = {}
        self.semobj = {}
        for e in self.engs:
            self.semobj[id(self.esem[e])] = self.esem[e]

    def _dma_sem(self, name):
        if name not in self.dsem:
            s = self.free_sems.pop()
            self.dsem[name] = s
            self.dcnt[name] = 0
            self.semobj[id(s)] = s
        return self.dsem[name]

    def mark(self, n):
        if n >= KSTAGE:
            self.stopped = True

    def op(self, eng, fn, reads=(), writes=(), dma=None):
        if getattr(self, "stopped", False):
            return None
        need = {}

        def merge(d):
            for sid, v in d.items():
                if need.get(sid, 0) < v:
                    need[sid] = v
        for k in reads:
            if k in self.keys:
                if isinstance(k, tuple) and k[0] == "PS" and self.keys[k][1]:
                    merge(self.keys[k][1])
                else:
                    merge(self.keys[k][0])
        for k in writes:
            if k in self.keys:
                merge(self.keys[k][0])
                merge(self.keys[k][1])
        if dma is not None:
            sem = self._dma_sem(dma)
            self.dcnt[dma] += 16
            ev = (id(sem), self.dcnt[dma])
            inc = 16
        else:
            sem = self.esem[eng]
            self.ecnt[eng] += 1
            ev = (id(sem), self.ecnt[eng])
            inc = 1
        waits = []
        own = id(self.esem[eng])
        for sid, v in need.items():
            if eng == "pe" and sid == own:
                continue
            if self.waited[eng].get(sid, 0) < v:
                waits.append((self.semobj[sid], v))
                self.waited[eng][sid] = v
        self.streams[eng].append((fn, waits, (sem, inc)))
        for k in reads:
            ent = self.keys.setdefault(k, ({}, {}))
            if isinstance(k, tuple) and k[0] == "PS":
                ent[1].clear()
            if ent[1].get(ev[0], 0) < ev[1]:
                ent[1][ev[0]] = ev[1]
        for k in writes:
            self.keys[k] = ({ev[0]: ev[1]}, {})
        return ev

    def alias(self, old_prefixes, new_keys):
        merged = {}
        for k, (w, r) in self.keys.items():
            name = k[0] if isinstance(k, tuple) else k
            if name in old_prefixes:
                for d in (w, r):
                    for sid, v in d.items():
                        if merged.get(sid, 0) < v:
                            merged[sid] = v
        for k in new_keys:
            ent = self.keys.get(k)
            if ent is None:
                self.keys[k] = (dict(merged), {})
            else:
                for sid, v in merged.items():
                    if ent[0].get(sid, 0) < v:
                        ent[0][sid] = v

    def final_waits(self, eng):
        need = {}
        for e in self.engs:
            if self.ecnt[e] > 0:
                need[id(self.esem[e])] = self.ecnt[e]
        for name, s in self.dsem.items():
            need[id(s)] = self.dcnt[name]
        waits = [(self.semobj[sid], v) for sid, v in need.items() if sid != id(self.esem[eng])]
        self.streams[eng].append((None, waits, None))

    def replay(self, name, eng):
        for fn, waits, inc in self.streams[name]:
            for s_, v in waits:
                eng.wait_ge(s_, v)
            if fn is None:
                continue
            ins = fn(eng)
            ins.then_inc(inc[0], inc[1])


def build_program():
    nc = bass.Bass("TRN2", target_bir_lowering=False)
    nc.dge_precook = False

    class Lazy:
        def __init__(self, name, shape, dt):
            self.a = (name, list(shape), dt)
            self.v = None

        def ap(self):
            if self.v is None:
                self.v = nc.dram_tensor(self.a[0], self.a[1], self.a[2], kind="ExternalInput").ap()
            return self.v

        def __getitem__(self, k):
            return self.ap()[k]

        def partition_broadcast(self, n):
            return self.ap().partition_broadcast(n)

    def din(name, shape, dt):
        return Lazy(name, shape, dt)

    xo = din("xo", [S_OWN, D], F32)
    xp = din("xp", [S_OWN, D], F32)
    memd = din("mem", [256, D], F32)
    w_in = din("w_in", [D, 9216], F32R)
    w_conv_out = din("w_conv_out", [1024, D], F32R)
    w_att_out = din("w_att_out", [1024, D], F32R)
    w_out = din("w_out", [D, D], F32R)
    w_cq = din("w_cq", [D, D], F32R)
    w_ckv = din("w_ckv", [D, 2 * D], F32R)
    w_co = din("w_co", [D, D], F32R)
    w_ffn_in = din("w_ffn_in", [D, 2 * DFF], F32R)
    w_ffn_out = din("w_ffn_out", [DFF, D], F32R)
    colp_d = din("colp", [128, NCOLP], F32)
    gfin_d = din("gfin", [D], F32)
    gmix_d = din("gmix", [D], F32)
    gcross_d = din("gcross", [D], F32)
    gmem_d = din("gmem", [D], F32)
    gffn_d = din("gffn", [D], F32)
    ident_d = din("ident", [128, 128], F32)
    onesf_d = din("onesf", [128, 128], F32)
    identb_d = din("identb", [128, 128], BF16)
    onesb_d = din("onesb", [128, 128], BF16)
    cmask_d = din("cmask", [128, 512], BF16)
    esel_d = din("esel", [128, 9 * 128], BF16)
    out_d = nc.dram_tensor("out", [S_OWN, D], F32, kind="ExternalOutput").ap()
    dbg_d = {}
    if DEBUG:
        dbg_d["OT"] = nc.dram_tensor("dbg_OT", [128, 8 * 1024], F32, kind="ExternalOutput").ap()
        dbg_d["X1"] = nc.dram_tensor("dbg_X1", [128, 4 * 2048], F32, kind="ExternalOutput").ap()
        dbg_d["X2"] = nc.dram_tensor("dbg_X2", [128, 4 * 2048], F32, kind="ExternalOutput").ap()
        dbg_d["YC"] = nc.dram_tensor("dbg_YC", [128, 8 * 512], F32, kind="ExternalOutput").ap()
        dbg_d["SB"] = nc.dram_tensor("dbg_SB", [128, 512], F32, kind="ExternalOutput").ap()

    import contextlib
    es = contextlib.ExitStack()
    with es:
        REGW = 29696
        reg = es.enter_context(nc.sbuf_tensor("sb_reg", [128, REGW], F32))
        wring = es.enter_context(nc.sbuf_tensor("sb_wring", [128, NSLOT, SLOTW], F32R))
        OTr = es.enter_context(nc.sbuf_tensor("sb_OT", [128, 8, 1024], F32R))
        colp = es.enter_context(nc.sbuf_tensor("sb_colp", [128, NCOLP], F32))
        ident = es.enter_context(nc.sbuf_tensor("sb_ident", [128, 128], F32))
        onesf = es.enter_context(nc.sbuf_tensor("sb_onesf", [128, 128], F32))
        identb = es.enter_context(nc.sbuf_tensor("sb_identb", [128, 128], BF16))
        onesb = es.enter_context(nc.sbuf_tensor("sb_onesb", [128, 128], BF16))
        cmask = es.enter_context(nc.sbuf_tensor("sb_cmask", [128, 512], BF16))
        esel = es.enter_context(nc.sbuf_tensor("sb_esel", [128, 9 * 128], BF16))
        epsc = es.enter_context(nc.sbuf_tensor("sb_epsc", [128, 1], F32))
        stat = es.enter_context(nc.sbuf_tensor("sb_stat", [128, 64], F32))
        kmean = es.enter_context(nc.sbuf_tensor("sb_kmean", [128, 8, 8], F32))
        sbt = es.enter_context(nc.sbuf_tensor("sb_sbt", [128, 8 * 8 * 8], F32))
        gtmp = es.enter_context(nc.sbuf_tensor("sb_gtmp", [128, 128], F32))
        vhalo = es.enter_context(nc.sbuf_tensor("sb_vhalo", [128, 8, 32], F32))
        sbTb = es.enter_context(nc.sbuf_tensor("sb_sbTb", [128, 1, 512], BF16))
        PS = [es.enter_context(nc.psum_tensor(f"ps{i}", [128, 512], F32)) for i in range(8)]
        sems = [es.enter_context(nc.semaphore(f"s{i}")) for i in range(48)]
        S = Sched(nc, sems)
        reg_addr = nc.lookup_mloc(reg).addr
        ot_addr = nc.lookup_mloc(OTr).addr
        cnames = {}

        def carve(off_w, nwords, dt, pat=None, base=None, **kw):
            esz = 2 if dt is BF16 else 4
            nel = nwords * 4 // esz
            if pat:
                (dk, dv), = kw.items()
                shape = [128, dv, nel // dv]
            else:
                shape = [128, nel]
            i = cnames.get("n", 0)
            cnames["n"] = i + 1
            return nc.alloc_sbuf_tensor_at(f"cv{i}", shape, dt, offset=(reg_addr if base is None else base) + off_w * 4)[:]

        def cp(c, n=1):
            return colp[:, c:c + n]

        for nm, t_, d_ in (("colp", colp, colp_d), ("ident", ident, ident_d), ("onesf", onesf, onesf_d),
                           ("identb", identb, identb_d), ("onesb", onesb, onesb_d), ("cmask", cmask, cmask_d),
                           ("esel", esel, esel_d)):
            S.op("pool", (lambda e, t_=t_, d_=d_: e.dma_start(out=t_[:], in_=d_[:, :])), writes=[nm], dma="c_" + nm)
        S.op("dve", lambda e: e.memset(kmean[:], 0.0), writes=["kmean"])
        S.op("dve", lambda e: e.memset(sbTb[:], 0.0), writes=[("sbT", 0)])
        S.op("dve", lambda e: e.memset(sbTb[0:16, :, :], -1.0), writes=[("sbT", 0)])
        S.op("dve", lambda e: e.memset(epsc[:], EPS), writes=["epsc"])

        rot_state = {"all": 0, "A": 0, "B": 0}

        def bank(pool="all"):
            if pool == "all":
                i = rot_state["all"] % 8
            elif pool == "A":
                i = rot_state["A"] % 4
            else:
                i = 4 + rot_state["B"] % 4
            rot_state[pool] += 1
            return i

        wstate = {"n": 0}

        def wblock(w2d, kc, ncols):
            slot = wstate["n"] % NSLOT
            assert kc * ncols <= SLOTW
            wstate["n"] += 1
            view = wring[:, slot, 0:kc * ncols].rearrange("p (k n) -> p k n", k=kc)
            src = w2d.rearrange("(k p) n -> p k n", p=128)
            key = ("W", slot)
            S.op("sp", (lambda e: e.dma_start(out=view, in_=src)), writes=[key], dma=f"w{slot}")
            return view, key

        def mm_group(bk, out_ap, pairs, reads):
            n = len(pairs)

            def fn(e):
                ins = None
                for i, (l, r) in enumerate(pairs):
                    ins = e.matmul(out_ap, l, r, start=(i == 0), stop=(i == n - 1))
                return ins
            S.op("pe", fn, reads=reads, writes=[("PS", bk)])

        flip = {"n": 0}

        def alt():
            flip["n"] += 1
            return "act" if flip["n"] % 2 else "dve"

        def copy_evac(eng, out_ap, in_ap, reads, writes):
            if eng == "act":
                S.op("act", lambda e: e.activation(out=out_ap, in_=in_ap, func=AF.Copy), reads=reads, writes=writes)
            else:
                S.op("dve", lambda e: e.tensor_copy(out=out_ap, in_=in_ap), reads=reads, writes=writes)

        def make_hT(tiles, tile_keys, gvec, gbc, hTv, hkey, scratch=None, scratch_keys=None, loader=None, preload=None, geng="pool", junk=None, gload=True, defer_B=False):
            nt = len(tiles)
            if gload:
                S.op(geng, lambda e: e.dma_start(out=gbc, in_=gvec.partition_broadcast(128)), writes=["gbc"], dma="gbc")
            xs_of = {}

            def stageA(t):
                xt = tiles[t]
                sc = stat[:, 2 * (t % 4):2 * (t % 4) + 1]
                rs = stat[:, 2 * (t % 4) + 1:2 * (t % 4) + 2]
                skey = ("stat", t % 4)
                jv = hTv[:, :, t * 128:(t + 1) * 128]
                if junk is None:
                    S.op("act", lambda e: e.activation(out=jv, in_=xt.rearrange("p (k n) -> p k n", k=16), func=AF.Square, accum_out=sc),
                         reads=[tile_keys[t]], writes=[(hkey, t, k) for k in range(16)] + [skey])
                else:
                    S.op("act", lambda e: e.activation(out=junk, in_=xt, func=AF.Square, accum_out=sc),
                         reads=[tile_keys[t]], writes=["junkP", skey])
                S.op("act", lambda e: e.activation(out=rs, in_=sc, func=AF.Sqrt, scale=1.0 / D, bias=epsc[:]),
                     reads=[skey, "epsc"], writes=[skey])
                S.op("dve", lambda e: e.reciprocal(out=rs, in_=rs), reads=[skey], writes=[skey])
                if scratch is None:
                    xs, xskey = xt, tile_keys[t]
                else:
                    xs, xskey = scratch[t % len(scratch)], scratch_keys[t % len(scratch)]
                S.op("dve", lambda e: e.scalar_tensor_tensor(out=xs, in0=xt, scalar=rs, in1=gbc, op0=ALU.mult, op1=ALU.mult),
                     reads=[tile_keys[t], skey, "gbc"], writes=[xskey])
                xs_of[t] = (xs, xskey)

            def stageB(t):
                xs, xskey = xs_of[t]
                for b4 in range(4):
                    bk = bank("all")

                    def fn(e, b4=b4, bk=bk):
                        ins = None
                        for kk in range(4):
                            k = 4 * b4 + kk
                            ins = e.transpose(PS[bk][:, kk * 128:(kk + 1) * 128], xs[:, k * 128:(k + 1) * 128], ident[:])
                        return ins
                    S.op("pe", fn, reads=[xskey, "ident"], writes=[("PS", bk)])
                    o = hTv[:, 4 * b4:4 * b4 + 4, t * 128:(t + 1) * 128]
                    i_ = PS[bk][:, :].rearrange("p (a n) -> p a n", a=4)
                    copy_evac(alt(), o, i_, [("PS", bk)], [(hkey, t, 4 * b4 + kk) for kk in range(4)])

            if defer_B:
                for t in range(nt):
                    stageA(t)
                return lambda: [stageB(t) for t in range(nt)]
            if loader is not None:
                for t in range(min(preload, nt)):
                    loader(t)
            stageA(0)
            for t in range(nt):
                if t + 1 < nt:
                    stageA(t + 1)
                stageB(t)
                if loader is not None and t + preload < nt:
                    loader(t + preload)

        def hkeys(hkey, ntiles, ks=range(16)):
            return [(hkey, t, k) for t in range(ntiles) for k in ks]

        def load_tiles(src_rows, dst_tiles, dst_keys, tag, eng="pool"):
            for t, (dst, key) in enumerate(zip(dst_tiles, dst_keys)):
                src = src_rows[t * 128:(t + 1) * 128, :]
                S.op(eng, lambda e, dst=dst, src=src: e.dma_start(out=dst, in_=src), writes=[key], dma=f"{tag}{t}")

        def gemm_F(w, c0, ncols, kc, inT, in_reads, ntok, cb, rhs_off=0):
            cpb = SLOTW // kc // 128
            nblk = ncols // (cpb * 128)
            for b in range(nblk):
                blk, wkey = wblock(w[:, c0 + b * cpb * 128: c0 + (b + 1) * cpb * 128], kc, cpb * 128)
                for cc in range(cpb):
                    bk = bank("all")
                    pairs = [(blk[:, k, cc * 128:(cc + 1) * 128], inT[:, k, rhs_off:rhs_off + ntok]) for k in range(kc)]
                    mm_group(bk, PS[bk][:, 0:ntok], pairs, reads=[wkey] + in_reads)
                    cb(bk, b * cpb + cc)

        def gemm_T(w, r0, kc, c0, ncols, inT, in_reads, ntiles, cb, pool="all", after_first=None):
            kb = min(kc, SLOTW // 512)
            nkb = kc // kb
            for n in range(ncols // 512):
                bks = [bank(pool) for _ in range(ntiles)]
                for j in range(nkb):
                    blk, wkey = wblock(w[r0 + j * kb * 128: r0 + (j + 1) * kb * 128, c0 + n * 512: c0 + (n + 1) * 512], kb, 512)
                    for t in range(ntiles):
                        bk = bks[t]

                        def fn(e, blk=blk, t=t, bk=bk, j=j):
                            ins = None
                            for kk in range(kb):
                                k = j * kb + kk
                                ins = e.matmul(PS[bk][:, :], inT[:, k, t * 128:(t + 1) * 128], blk[:, kk, :],
                                               start=(k == 0), stop=(k == kc - 1))
                            return ins
                        S.op("pe", fn, reads=[wkey] + in_reads, writes=[("PS", bk)])
                        if j == nkb - 1:
                            cb(bk, t, n)
                if n == 0 and after_first is not None:
                    after_first()

        hT_K = carve(0, 8192, F32R, "p (k n) -> p k n", k=16)
        KT = carve(8192, 8192, BF16, "p (h n) -> p h n", h=8)
        Vt = carve(16384, 8192, BF16, "p (t n) -> p t n", t=16)
        QT = carve(24576, 4096, BF16, "p (h n) -> p h n", h=8)
        QTf = [carve(28672, 512, F32), carve(29184, 512, F32)]
        stage = [carve(t * 2048, 2048, F32, base=ot_addr) for t in range(3)]
        gbcK = carve(6144, 2048, F32, base=ot_addr)
        PTb = None

        scale_att = 1.0 / np.sqrt(128.0)

        def glu_gemm(hTv, hreads, ntok, rhs_off, cb2):
            for c in range(8):
                blkA, kA = wblock(w_in[:, A0 + c * 128: A0 + (c + 1) * 128], 16, 128)
                blkG, kG = wblock(w_in[:, G0 + c * 128: G0 + (c + 1) * 128], 16, 128)
                ba = bank("all")
                mm_group(ba, PS[ba][:, 0:ntok], [(blkA[:, k, :], hTv[:, k, rhs_off:rhs_off + ntok]) for k in range(16)],
                         reads=[kA] + hreads)
                bg = bank("all")
                mm_group(bg, PS[bg][:, 0:ntok], [(blkG[:, k, :], hTv[:, k, rhs_off:rhs_off + ntok]) for k in range(16)],
                         reads=[kG] + hreads)
                cb2(ba, bg, c)

        def gate_ops(h, qf, qk, oc):
            gb = bank("all")

            def fn(e):
                ins = None
                for tt in range(4):
                    ins = e.matmul(PS[gb][:, tt * 8:(tt + 1) * 8], qf[:, tt * 128:(tt + 1) * 128], kmean[:, h, :], start=True, stop=True)
                return ins
            S.op("pe", fn, reads=[qk, "kmean"], writes=[("PS", gb)])
            gm = gtmp[:, 0:32]
            S.op("dve", lambda e: e.tensor_tensor(out=gm, in0=PS[gb][:, 0:32], in1=cp(C_GMASK + oc * 32, 32), op=ALU.add),
                 reads=[("PS", gb), "colp"], writes=["gm"])
            top8 = gtmp[:, 32:64]
            for tt in range(4):
                S.op("dve", lambda e, tt=tt: e.max(out=top8[:, tt * 8:(tt + 1) * 8], in_=gm[:, tt * 8:(tt + 1) * 8]),
                     reads=["gm"], writes=[("top8", tt)])
            thr = gtmp[:, 96:100]
            S.op("dve", lambda e: e.tensor_scalar(out=thr, in0=top8.rearrange("p (t n) -> p t n", n=8)[:, :, 2], scalar1=-1e29, scalar2=None, op0=ALU.max),
                 reads=[("top8", tt) for tt in range(4)], writes=["thr"])
            for tt in range(4):
                t = oc * 4 + tt
                o2 = sbt[:, (t * 8 + h) * 8:(t * 8 + h) * 8 + 8]
                S.op("dve", lambda e, tt=tt, o2=o2: e.tensor_scalar(out=o2, in0=gm[:, tt * 8:(tt + 1) * 8], scalar1=thr[:, tt:tt + 1], scalar2=-1.0, op0=ALU.is_ge, op1=ALU.add),
                     reads=["gm", "thr"], writes=[("sb", t, h)])

        sgt = [None, None]

        for c in range(4):
            S.mark(c)
            src = xp if c < 2 else xo
            r0 = (c % 2) * 512
            skeys = [("stage", t % 3) for t in range(4)]
            srows = src[r0:r0 + 512, :]

            def ld(t, srows=srows):
                dst = stage[t % 3]
                S.op("pool", lambda e: e.dma_start(out=dst, in_=srows[t * 128:(t + 1) * 128, :]), writes=[("stage", t % 3)], dma=f"xs{t % 3}")
            make_hT([stage[t % 3] for t in range(4)], skeys, gmix_d, gbcK, hT_K, "hT", loader=ld, preload=3)
            hr = hkeys("hT", 4)
            S.mark(c + 0.3)

            def cb_k(bk, h, c=c):
                o = KT[:, h, c * 512:(c + 1) * 512]
                bcol = cp(C_BIN + K0 // 128 + h)
                if KSUB <= 11:
                    return
                if not os.environ.get("KSKIPACT"):
                    S.op("act", lambda e: e.activation(out=o, in_=PS[bk][:, :], func=AF.Identity, bias=bcol),
                         reads=[("PS", bk), "colp"], writes=[("KT", h, c)])
                if KSUB <= 12:
                    return
                ks = gtmp[:, 64 + 2 * (h % 8):64 + 2 * (h % 8) + 2]
                S.op("dve", lambda e: e.tensor_reduce(out=ks, in_=PS[bk][:, :].rearrange("p (b t) -> p b t", b=2), axis=AX.X, op=ALU.add),
                     reads=[("PS", bk)], writes=[("ks", h)])
                if KSUB <= 13:
                    return
                S.op("act", lambda e: e.activation(out=kmean[:, h, 2 * c:2 * c + 2], in_=ks, func=AF.Identity, scale=1.0 / 256.0, bias=bcol),
                     reads=[("ks", h), "colp", "kmean"], writes=["kmean"])
            gemm_F(w_in, K0, 1024, 16, hT_K, hr, 512, cb_k)
            S.mark(c + 0.6)

            def cb_v(bk, t, n, c=c):
                o = Vt[:, c * 4 + t, n * 512:(n + 1) * 512]
                copy_evac(alt(), o, PS[bk][:, :], [("PS", bk)], [("V", c * 4 + t, n)])
            gemm_T(w_in, 0, 16, V0, 1024, hT_K, hr, 4, cb_v)

            if c == 1:
                def cb_halo(ba, bg, cch):
                    sg = QTf[0][:, 0:256]
                    S.op("act", lambda e: e.activation(out=sg, in_=PS[bg][:, 0:256], func=AF.Sigmoid, bias=cp(C_BIN + G0 // 128 + cch)),
                         reads=[("PS", bg), "colp"], writes=["halo_sg"])
                    hv = QTf[1][:, 0:256]
                    S.op("dve", lambda e: e.scalar_tensor_tensor(out=hv, in0=PS[ba][:, 0:256], scalar=cp(C_BIN + cch), in1=sg, op0=ALU.add, op1=ALU.mult),
                         reads=[("PS", ba), "halo_sg", "colp"], writes=["halo_v"])
                    S.op("dve", lambda e: e.tensor_scalar(out=vhalo[:, cch, :], in0=hv[:, 224:256], scalar1=cp(C_FLAG), scalar2=None, op0=ALU.mult),
                         reads=["halo_v", "colp"], writes=[("vhalo", cch)])
                glu_gemm(hT_K, hr, 256, 256, cb_halo)

            if c >= 2:
                oc = c - 2
                S.alias({"halo_sg", "halo_v"}, [("QTf", 0), ("QTf", 1)])

                def cb_q(bk, h, oc=oc):
                    o = QT[:, h, oc * 512:(oc + 1) * 512]
                    bcol = cp(C_BIN + Q0 // 128 + h)
                    S.op("act", lambda e: e.activation(out=o, in_=PS[bk][:, :], func=AF.Identity, bias=bcol),
                         reads=[("PS", bk), "colp"], writes=[("QT", h, oc)])
                    qf = QTf[h % 2]
                    qk = ("QTf", h % 2)
                    S.op("dve", lambda e: e.tensor_scalar(out=qf, in0=PS[bk][:, :], scalar1=bcol, scalar2=None, op0=ALU.add),
                         reads=[("PS", bk), "colp"], writes=[qk])
                    prev = pend_gate.pop() if pend_gate else None
                    pend_gate.append(lambda h=h, qf=qf, qk=qk, oc=oc: gate_ops(h, qf, qk, oc))
                    if prev is not None:
                        prev()
                pend_gate = []
                gemm_F(w_in, Q0, 1024, 16, hT_K, hr, 512, cb_q)
                while pend_gate:
                    pend_gate.pop()()

        S.mark(4)
        S.alias({"hT", "stage", "gbc"}, [("PT", i) for i in range(4)] + [("rden", i) for i in range(2)] + [("otmp", i) for i in range(2)])
        PTb = [carve(i * 256, 256, BF16) for i in range(4)]
        rden = [carve(1024 + i * 512, 512, F32) for i in range(2)]
        otmp = [carve(2048 + i * 512, 512, F32) for i in range(2)]
        def prep_sT(h, pr):
            tb = bank("A")

            def fn(e):
                ins = None
                for e2 in range(4):
                    t = 4 * pr + e2
                    ins = e.transpose(PS[tb][0:8, e2 * 128:(e2 + 1) * 128], sbt[:, (t * 8 + h) * 8:(t * 8 + h) * 8 + 8], ident[:])
                return ins
            S.op("pe", fn, reads=[("sb", 4 * pr + e2, h) for e2 in range(4)] + ["ident"], writes=[("PS", tb)])
            copy_evac("dve", sbTb[0:8, 0, :], PS[tb][0:8, 0:512], [("PS", tb)], [("sbT", 0)])

        hj = 0
        ptc = {"n": 0}
        S.alias({"stage", "gbc"}, [("OT", h, p_) for h in range(8) for p_ in range(2)])
        for h in range(8):
            for pr in range(2):
                if hj == 0:
                    prep_sT(h, pr)
                sT = sbTb[:, 0, :]
                sTk = ("sbT", 0)
                ob = 4 + 2 * (hj % 2)
                db = ob + 1
                nblk = 6 + 2 * pr
                qsl = QT[:, h, pr * 512:(pr + 1) * 512]
                qk = ("QT", h, pr)
                visits = [(n, kt) for n in range(nblk) for kt in range(2)]

                def emit_S(v, h=h, pr=pr, qsl=qsl, qk=qk, sT=sT, sTk=sTk):
                    n, kt = v
                    sb_ = bank("A")
                    own0 = 4 + 2 * pr
                    own1 = 5 + 2 * pr

                    def fn(e):
                        o = PS[sb_][:, :]
                        o0 = PS[sb_][:, 0:256]
                        o1 = PS[sb_][:, 256:512]
                        e.matmul(o, KT[:, h, n * 256 + kt * 128: n * 256 + (kt + 1) * 128], qsl, start=True, stop=False)
                        if n < own0:
                            return e.matmul(o, esel[:, n * 128:(n + 1) * 128], sT, start=False, stop=True)
                        if n == own0:
                            e.matmul(o0, identb[:], cmask[:, kt * 256:(kt + 1) * 256], start=False, stop=False)
                            return e.matmul(o1, esel[:, n * 128:(n + 1) * 128], sT[:, 256:512], start=False, stop=True)
                        e.matmul(o0, esel[:, 8 * 128:9 * 128], sT[:, 0:256], start=False, stop=False)
                        return e.matmul(o1, identb[:], cmask[:, kt * 256:(kt + 1) * 256], start=False, stop=True)
                    S.op("pe", fn, reads=[("KT", h, n // 2), qk, "identb", "cmask", "esel", sTk], writes=[("PS", sb_)])
                    pi = ptc["n"] % 4
                    ptc["n"] += 1
                    pt = PTb[pi]
                    S.op("act", lambda e: e.activation(out=pt, in_=PS[sb_][:, :], func=AF.Exp, scale=float(scale_att)),
                         reads=[("PS", sb_)], writes=[("PT", pi)])
                    return pi

                def emit_PV(v, pi, first, last, h=h, ob=ob, db=db):
                    n, kt = v
                    pt = PTb[pi]

                    def fn(e):
                        e.matmul(PS[ob][:, :], Vt[:, n * 2 + kt, h * 128:(h + 1) * 128], pt, start=first, stop=last)
                        return e.matmul(PS[db][:, :], onesb[:], pt, start=first, stop=last)
                    S.op("pe", fn, reads=[("PT", pi), ("V", n * 2 + kt, h // 4), "onesb"], writes=[("PS", ob), ("PS", db)])
                nv = len(visits)
                pis = {}
                for v in range(min(2, nv)):
                    pis[v] = emit_S(visits[v])
                for v in range(nv):
                    if v + 2 < nv:
                        pis[v + 2] = emit_S(visits[v + 2])
                        if v + 2 == nv - 1 and hj + 1 < 16:
                            nh_, npr_ = divmod(hj + 1, 2)
                            prep_sT(nh_, npr_)
                    emit_PV(visits[v], pis[v], v == 0, v == nv - 1)
                rd_ = rden[hj % 2]
                ot_ = otmp[hj % 2]
                S.op("dve", lambda e, rd_=rd_, db=db: e.reciprocal(out=rd_, in_=PS[db][:, :]), reads=[("PS", db)], writes=[("rden", hj % 2)])
                S.op("dve", lambda e, rd_=rd_, ot_=ot_, ob=ob: e.tensor_tensor(out=ot_, in0=PS[ob][:, :], in1=rd_, op=ALU.mult),
                     reads=[("PS", ob), ("rden", hj % 2)], writes=[("otmp", hj % 2)])
                oo = OTr[:, h, pr * 512:(pr + 1) * 512]
                S.op("act", lambda e, oo=oo, ot_=ot_, h=h: e.activation(out=oo, in_=ot_, func=AF.Identity, bias=cp(C_BIN + V0 // 128 + h)),
                     reads=[("otmp", hj % 2), "colp"], writes=[("OT", h, pr)])
                hj += 1
        if DEBUG:
            S.op("pool", lambda e: e.dma_start(out=dbg_d["OT"], in_=OTr[:].bitcast(F32).rearrange("p h n -> p (h n)")),
                 reads=[("OT", h, p_) for h in range(8) for p_ in range(2)], writes=["dbgOT"], dma="dbgOT")
            S.op("pool", lambda e: e.dma_start(out=dbg_d["SB"], in_=sbt[:]),
                 reads=[("sb", t, h) for t in range(8) for h in range(8)], writes=["dbgSB"], dma="dbgSB")

        B1 = 0
        B2 = 8192
        B3 = 16384
        hT_P = carve(B1, 8192, F32R, "p (k n) -> p k n", k=16)
        X = carve(B1, 8192, F32, "p (t n) -> p t n", t=4)
        xstage = [carve(B2 + t * 2048, 2048, F32) for t in range(4)]
        mergedTr = carve(B2, 8192, F32R, "p (k n) -> p k n", k=16)
        ysq = carve(B2, 4096, F32, "p (c n) -> p c n", c=8)
        lnt = [carve(B2 + 4096 + i * 512, 512, F32) for i in range(4)]
        sgtmp = [carve(B2 + 6144 + i * 512, 512, F32) for i in range(2)]
        vglu = carve(B3, 8 * 544 // 2, BF16, "p (c n) -> p c n", c=8)
        dg = [carve(B2 + i * 2048, 1984, BF16, "p (k n) -> p k n", k=31) for i in range(2)]
        ycT = carve(B3 + 4352, 4096, F32, "p (c n) -> p c n", c=8)
        ycTr = carve(B3, 4096, F32R, "p (c n) -> p c n", c=8)
        mtmp = [carve(B3 + 8448 + i * 512, 512, F32) for i in range(2)]
        memst = [carve(B3 + t * 2048, 2048, F32) for t in range(2)]
        mT = carve(B2 + 4096, 4096, F32R, "p (k n) -> p k n", k=16)
        h2T = carve(B2, 8192, F32R, "p (k n) -> p k n", k=16)
        o2T = carve(B2, 8192, F32R, "p (k n) -> p k n", k=16)
        memKT = carve(B3, 2048, BF16, "p (c n) -> p c n", c=16)
        memV = carve(B3 + 2048, 2048, BF16, "p (t n) -> p t n", t=2)
        q2T = carve(B3 + 4096, 4096, BF16, "p (c n) -> p c n", c=16)
        xscr = [carve(B3 + 4096 + i * 2048, 2048, F32) for i in range(2)]
        P2T = [carve(B3 + 8192 + i * 256, 256, BF16) for i in range(2)]
        rden2 = [carve(B3 + 8704 + i * 512, 512, F32) for i in range(1)]
        h3T = carve(B2, 8192, F32R, "p (k n) -> p k n", k=16)
        actT = [carve(B3 + i * 2048, 2048, F32R, "p (c n) -> p c n", c=4) for i in range(2)]
        fsg = [carve(B3 + 4096 + i * 512, 512, F32) for i in range(2)]
        fscr = [carve(B3 + 5120 + i * 2048, 2048, F32) for i in range(2)]
        gfin = carve(B3, 2048, F32)
        obuf = [carve(B3 + 2048 + i * 2048, 2048, F32) for i in range(2)]

        gbcP = carve(25856, 2048, F32)
        scale_x = 1.0 / np.sqrt(512.0)
        ALLB = {"hT", "stage", "gbc", "PT", "rden", "otmp", "KT", "V", "QT", "QTf", "halo_sg", "halo_v"}
        PASSK = {"hTP", "X", "xstage", "mergedT", "ysq", "lnt", "sgtmp", "vglu", "vgh", "ycT", "ycTr", "dg", "dgk", "mtmp", "memst", "mT", "h2T", "o2T",
                 "memKT", "memV", "q2T", "xscr", "P2T", "rden2", "h3T", "actT", "fsg", "fscr", "gfin", "obuf"}

        REGS = {
            "B1": {"hTP", "X"},
            "B2": {"xstage", "mergedT", "ysq", "lnt", "sgtmp", "dg", "dgk", "mT", "h2T", "o2T", "h3T"},
            "B3": {"vglu", "vgh", "ycT", "ycTr", "mtmp", "memst", "memKT", "memV", "q2T", "xscr", "P2T", "rden2", "actT", "fsg", "fscr",
                   "gfin", "obuf"},
            "G": {"gbc", "junkP"},
        }
        REG_OF = {fam: r for r, fams in REGS.items() for fam in fams}
        pass_state = {"p": 0}

        def region_alias(newkeys):
            regs = set()
            for k in newkeys:
                fam = k[0] if isinstance(k, tuple) else k
                regs.add(REG_OF[fam])
            old = set()
            for r in regs:
                old |= REGS[r]
            if pass_state["p"] == 0:
                old |= ALLB
            if os.environ.get("KGLOBAL"):
                old = ALLB | PASSK | {"junkP"}
            S.alias(old, newkeys)

        junkP = carve(27904, 1024, BF16)
        for p in range(2):
            pass_state["p"] = p
            S.mark(5 + 10 * p)
            xk = [("xstage", t) for t in range(4)]
            region_alias(xk)
            load_tiles(xo[p * 512:(p + 1) * 512, :], [t_[:] for t_ in xstage], xk, "xp", eng="sp")
            region_alias(hkeys("hTP", 4))
            region_alias(["gbc", "junkP"])
            make_hT([t_[:] for t_ in xstage], xk, gmix_d, gbcP, hT_P, "hTP", junk=junkP, gload=(p == 0))
            hr = hkeys("hTP", 4)

            region_alias([("vglu", c) for c in range(8)] + [("sgtmp", i) for i in range(2)])
            for c in range(8):
                S.op("pool", lambda e, c=c: e.tensor_copy(out=vglu[:, c, 0:32], in_=vhalo[:, c, :]),
                     reads=[("vhalo", c)], writes=[("vgh", c)])

            def cb_glu(ba, bg, c):
                i = c % 2
                S.op("act", lambda e: e.activation(out=sgtmp[i], in_=PS[bg][:, :], func=AF.Sigmoid, bias=cp(C_BIN + G0 // 128 + c)),
                     reads=[("PS", bg), "colp"], writes=[("sgtmp", i)])
                S.op("dve", lambda e: e.scalar_tensor_tensor(out=vglu[:, c, 32:544], in0=PS[ba][:, :], scalar=cp(C_BIN + c), in1=sgtmp[i], op0=ALU.add, op1=ALU.mult),
                     reads=[("PS", ba), ("sgtmp", i), "colp"], writes=[("vglu", c)])
            glu_gemm(hT_P, hr, 512, 0, cb_glu)
            region_alias([("ycT", c) for c in range(8)] + [("dg", i) for i in range(2)] + [("dgk", i, k) for i in range(2) for k in range(1, 31)])
            for c in range(8):
                d_ = dg[c % 2]
                dk = ("dg", c % 2)
                S.op("dve", lambda e, d_=d_, c=c: e.tensor_scalar(out=d_[:, 0, :], in0=identb[:], scalar1=cp(C_CONVW + c * 31), scalar2=None, op0=ALU.mult),
                     reads=["identb", "colp"], writes=[dk])
                for k in range(1, 31):
                    eng = "dve" if k % 2 == 0 else "pool"
                    S.op(eng, lambda e, d_=d_, c=c, k=k: e.tensor_scalar(out=d_[:, k, :], in0=identb[:], scalar1=cp(C_CONVW + c * 31 + k), scalar2=1.0, op0=ALU.mult, op1=ALU.mult),
                         reads=["identb", "colp"], writes=[("dgk", c % 2, k)])
                bk = bank("all")
                mm_group(bk, PS[bk][:, :], [(d_[:, k, :], vglu[:, c, 2 + k:514 + k]) for k in range(31)],
                         reads=[dk] + [("dgk", c % 2, k) for k in range(1, 31)] + [("vglu", c), ("vgh", c)])
                S.op("act", lambda e, c=c, bk=bk: e.activation(out=ycT[:, c, :], in_=PS[bk][:, :], func=AF.Identity, bias=cp(C_CONVB + c)),
                     reads=[("PS", bk), "colp"], writes=[("ycT", c)])
                if p == 0:
                    S.op("pool", lambda e, c=c: e.tensor_copy(out=vhalo[:, c, :], in_=vglu[:, c, 512:544]),
                         reads=[("vglu", c)], writes=[("vhalo", c)])
            region_alias([("ysq", c) for c in range(8)] + [("lnt", i) for i in range(4)])
            for c in range(8):
                S.op("act", lambda e, c=c: e.activation(out=ysq[:, c, :], in_=ycT[:, c, :], func=AF.Square),
                     reads=[("ycT", c)], writes=[("ysq", c)])
            bm = bank("all")
            mm_group(bm, PS[bm][:, :], [(onesf[:], ycT[:, c, :]) for c in range(8)], reads=["onesf"] + [("ycT", c) for c in range(8)])
            be = bank("all")
            mm_group(be, PS[be][:, :], [(onesf[:], ysq[:, c, :]) for c in range(8)], reads=["onesf"] + [("ysq", c) for c in range(8)])
            mean_sb, msq, var_, rstd_ = lnt
            S.op("act", lambda e, bm=bm: e.activation(out=mean_sb, in_=PS[bm][:, :], func=AF.Copy), reads=[("PS", bm)], writes=[("lnt", 0)])
            S.op("dve", lambda e: e.tensor_tensor(out=msq, in0=mean_sb, in1=mean_sb, op=ALU.mult), reads=[("lnt", 0)], writes=[("lnt", 1)])
            S.op("dve", lambda e, be=be: e.tensor_tensor(out=var_, in0=PS[be][:, :], in1=msq, op=ALU.subtract), reads=[("PS", be), ("lnt", 1)], writes=[("lnt", 2)])
            S.op("act", lambda e: e.activation(out=var_, in_=var_, func=AF.Sqrt, bias=epsc[:]), reads=[("lnt", 2), "epsc"], writes=[("lnt", 2)])
            S.op("dve", lambda e: e.reciprocal(out=rstd_, in_=var_), reads=[("lnt", 2)], writes=[("lnt", 3)])
            region_alias([("ycTr", c) for c in range(8)])
            for c in range(8):
                y = ycT[:, c, :]
                S.op("dve", lambda e, y=y: e.tensor_tensor(out=y, in0=y, in1=mean_sb, op=ALU.subtract), reads=[("ycT", c), ("lnt", 0)], writes=[("ycT", c)])
                S.op("dve", lambda e, y=y: e.tensor_tensor(out=y, in0=y, in1=rstd_, op=ALU.mult), reads=[("ycT", c), ("lnt", 3)], writes=[("ycT", c)])
                S.op("act", lambda e, y=y, c=c: e.activation(out=ycTr[:, c, :], in_=y, func=AF.Silu, scale=cp(C_LNG + c), bias=cp(C_LNB + c)),
                     reads=[("ycT", c), "colp"], writes=[("ycTr", c)])
            if DEBUG and p == 0:
                S.op("pool", lambda e: e.dma_start(out=dbg_d["YC"], in_=ycTr[:].bitcast(F32).rearrange("p c n -> p (c n)")),
                     reads=[("ycTr", c) for c in range(8)], writes=["dbgYC"], dma="dbgYC")

            S.mark(6 + 10 * p)
            region_alias([("mergedT", f) for f in range(16)] + [("mtmp", i) for i in range(2)])
            for sweep in range(2):
                wA = w_conv_out if sweep == 0 else w_att_out
                g0 = GC0 if sweep == 0 else GA0
                for i in range(8):
                    blkY, kY = wblock(wA[:, i * 256:(i + 1) * 256], 8, 256)
                    for cc in range(2):
                        if True:
                            f = 2 * i + cc
                            blkG, kG = wblock(w_in[:, g0 + f * 128: g0 + (f + 1) * 128], 16, 128)
                            by = bank("all")
                            if sweep == 0:
                                pairs = [(blkY[:, k, cc * 128:(cc + 1) * 128], ycTr[:, k, :]) for k in range(8)]
                                rd = [kY] + [("ycTr", k) for k in range(8)]
                            else:
                                pairs = [(blkY[:, k, cc * 128:(cc + 1) * 128], OTr[:, k, p * 512:(p + 1) * 512]) for k in range(8)]
                                rd = [kY] + [("OT", k, p) for k in range(8)]
                            mm_group(by, PS[by][:, :], pairs, reads=rd)
                            bg = bank("all")
                            mm_group(bg, PS[bg][:, :], [(blkG[:, k, :], hT_P[:, k, :]) for k in range(16)], reads=[kG] + hr)
                            mt_ = mtmp[f % 2]
                            mk = ("mtmp", f % 2)
                            S.op("act", lambda e, mt_=mt_, bg=bg, f=f, g0=g0: e.activation(out=mt_, in_=PS[bg][:, :], func=AF.Sigmoid, bias=cp(C_BIN + g0 // 128 + f)),
                                 reads=[("PS", bg), "colp"], writes=[mk])
                            if sweep == 0:
                                S.op("dve", lambda e, mt_=mt_, by=by, f=f: e.tensor_tensor(out=mergedTr[:, f, :], in0=PS[by][:, :], in1=mt_, op=ALU.mult),
                                     reads=[("PS", by), mk], writes=[("mergedT", f)])
                            else:
                                S.op("dve", lambda e, mt_=mt_, by=by: e.tensor_tensor(out=mt_, in0=PS[by][:, :], in1=mt_, op=ALU.mult),
                                     reads=[("PS", by), mk], writes=[mk])
                                S.op("dve", lambda e, mt_=mt_, f=f: e.tensor_tensor(out=mergedTr[:, f, :], in0=mergedTr[:, f, :].bitcast(F32), in1=mt_, op=ALU.add),
                                     reads=[("mergedT", f), mk], writes=[("mergedT", f)])

            S.mark(7 + 10 * p)
            mk_ = [("memst", t) for t in range(2)]
            region_alias(mk_)
            load_tiles(memd[:, :], [t_[:] for t_ in memst], mk_, "ms")
            S.op("pool", lambda e: e.dma_start(out=gbcP, in_=gmem_d.partition_broadcast(128)), writes=["gbc"], dma="gbc")
            mT_cont = {}

            def mem_stageA(mk_=mk_, mT_cont=mT_cont):
                mT_cont["B"] = make_hT([t_[:] for t_ in memst], mk_, gmem_d, gbcP, mT, "mT", junk=junkP, defer_B=True, gload=False)
            region_alias([("X", t) for t in range(4)])
            Xk = [("X", t) for t in range(4)]
            load_tiles(xo[p * 512:(p + 1) * 512, :], [X[:, t, :] for t in range(4)], Xk, "xr")

            def cb_res(bk, t, n):
                xs_ = X[:, t, n * 512:(n + 1) * 512]
                S.op("dve", lambda e: e.tensor_tensor(out=xs_, in0=PS[bk][:, :], in1=xs_, op=ALU.add),
                     reads=[("PS", bk), ("X", t)], writes=[("X", t)])
            gemm_T(w_out, 0, 16, 0, D, mergedTr, [("mergedT", f) for f in range(16)], 4, cb_res, after_first=mem_stageA)
            if DEBUG and p == 0:
                S.op("pool", lambda e: e.dma_start(out=dbg_d["X1"], in_=X[:].rearrange("p t n -> p (t n)")),
                     reads=Xk, writes=["dbgX1"], dma="dbgX1")

            S.mark(8 + 10 * p)
            region_alias(hkeys("mT", 2))
            mT_cont["B"]()
            mr = hkeys("mT", 2)
            region_alias([("memKT", c) for c in range(16)] + [("memV", t, n) for t in range(2) for n in range(4)])

            def cb_mk(bk, c):
                copy_evac(alt(), memKT[:, c, :], PS[bk][:, 0:256], [("PS", bk)], [("memKT", c)])
            gemm_F(w_ckv, 0, D, 16, mT, mr, 256, cb_mk)

            def cb_mv(bk, t, n):
                copy_evac(alt(), memV[:, t, n * 512:(n + 1) * 512], PS[bk][:, :], [("PS", bk)], [("memV", t, n)])
            gemm_T(w_ckv, 0, 16, D, D, mT, mr, 2, cb_mv)

            region_alias(hkeys("h2T", 4) + [("xscr", i) for i in range(2)])
            make_hT([X[:, t, :] for t in range(4)], Xk, gcross_d, gbcP, h2T, "h2T", scratch=[t_[:] for t_ in xscr], scratch_keys=[("xscr", i) for i in range(2)])
            region_alias([("q2T", c) for c in range(16)])

            def cb_q2(bk, c):
                copy_evac(alt(), q2T[:, c, :], PS[bk][:, :], [("PS", bk)], [("q2T", c)])
            gemm_F(w_cq, 0, D, 16, h2T, hkeys("h2T", 4), 512, cb_q2)
            region_alias([("o2T", c) for c in range(16)] + [("P2T", i) for i in range(2)] + [("rden2", 0)])
            for hh in range(4):
                for mt in range(2):
                    sb_ = bank("all")
                    mm_group(sb_, PS[sb_][:, :], [(memKT[:, 4 * hh + dc, mt * 128:(mt + 1) * 128], q2T[:, 4 * hh + dc, :]) for dc in range(4)],
                             reads=[("memKT", 4 * hh + dc) for dc in range(4)] + [("q2T", 4 * hh + dc) for dc in range(4)])
                    S.op("act", lambda e, sb_=sb_, mt=mt: e.activation(out=P2T[mt], in_=PS[sb_][:, :], func=AF.Exp, scale=float(scale_x)),
                         reads=[("PS", sb_)], writes=[("P2T", mt)])
                db_ = bank("all")
                mm_group(db_, PS[db_][:, :], [(onesb[:], P2T[mt]) for mt in range(2)], reads=["onesb", ("P2T", 0), ("P2T", 1)])
                S.op("dve", lambda e, db_=db_: e.reciprocal(out=rden2[0], in_=PS[db_][:, :]), reads=[("PS", db_)], writes=[("rden2", 0)])
                for dc in range(4):
                    c = 4 * hh + dc
                    ob_ = bank("all")
                    mm_group(ob_, PS[ob_][:, :], [(memV[:, mt, c * 128:(c + 1) * 128], P2T[mt]) for mt in range(2)],
                             reads=[("P2T", 0), ("P2T", 1)] + [("memV", mt, c // 4) for mt in range(2)])
                    S.op("dve", lambda e, ob_=ob_, c=c: e.tensor_tensor(out=o2T[:, c, :], in0=PS[ob_][:, :], in1=rden2[0], op=ALU.mult),
                         reads=[("PS", ob_), ("rden2", 0)], writes=[("o2T", c)])
            gemm_T(w_co, 0, 16, 0, D, o2T, [("o2T", c) for c in range(16)], 4, cb_res)
            if DEBUG and p == 0:
                S.op("pool", lambda e: e.dma_start(out=dbg_d["X2"], in_=X[:].rearrange("p t n -> p (t n)")),
                     reads=Xk, writes=["dbgX2"], dma="dbgX2")

            S.mark(9 + 10 * p)
            region_alias(hkeys("h3T", 4) + [("fscr", i) for i in range(2)])
            make_hT([X[:, t, :] for t in range(4)], Xk, gffn_d, gbcP, h3T, "h3T", scratch=[t_[:] for t_ in fscr], scratch_keys=[("fscr", i) for i in range(2)])
            h3r = hkeys("h3T", 4)
            if p == 0:
                S.op("pool", lambda e: e.dma_start(out=gbcP, in_=gmix_d.partition_broadcast(128)), writes=["gbc"], dma="gbc")
            region_alias([("actT", i, fc) for i in range(2) for fc in range(4)] + [("fsg", i) for i in range(2)])
            for fg in range(11):
                ab = fg % 2
                for fc in range(4):
                    if True:
                        blkG, kG = wblock(w_ffn_in[:, fg * 512 + fc * 128: fg * 512 + (fc + 1) * 128], 16, 128)
                        blkU, kU = wblock(w_ffn_in[:, DFF + fg * 512 + fc * 128: DFF + fg * 512 + (fc + 1) * 128], 16, 128)
                        bg = bank("A")
                        mm_group(bg, PS[bg][:, :], [(blkG[:, k, :], h3T[:, k, :]) for k in range(16)], reads=[kG] + h3r)
                        bu = bank("A")
                        mm_group(bu, PS[bu][:, :], [(blkU[:, k, :], h3T[:, k, :]) for k in range(16)], reads=[kU] + h3r)
                        sg_ = fsg[fc % 2]
                        S.op("act", lambda e, sg_=sg_, bg=bg: e.activation(out=sg_, in_=PS[bg][:, :], func=AF.Silu),
                             reads=[("PS", bg)], writes=[("fsg", fc % 2)])
                        S.op("dve", lambda e, sg_=sg_, bu=bu, ab=ab, fc=fc: e.tensor_tensor(out=actT[ab][:, fc, :], in0=PS[bu][:, :], in1=sg_, op=ALU.mult),
                             reads=[("PS", bu), ("fsg", fc % 2)], writes=[("actT", ab, fc)])
                for n4 in range(4):
                    blkO, kO = wblock(w_ffn_out[fg * 512:(fg + 1) * 512, n4 * 512:(n4 + 1) * 512], 4, 512)
                    for t in range(4):
                        bo = bank("B")
                        mm_group(bo, PS[bo][:, :], [(actT[ab][:, fc, t * 128:(t + 1) * 128], blkO[:, fc, :]) for fc in range(4)],
                                 reads=[kO] + [("actT", ab, fc) for fc in range(4)])
                        cb_res(bo, t, n4)

            S.mark(10 + 10 * p)
            region_alias(["gfin"] + [("obuf", i) for i in range(2)])
            S.op("pool", lambda e: e.dma_start(out=gfin[:], in_=gfin_d.partition_broadcast(128)), writes=["gfin"], dma="gfin")
            for t in range(4):
                sc = stat[:, 16 + 2 * t:16 + 2 * t + 1]
                rs = stat[:, 16 + 2 * t + 1:16 + 2 * t + 2]
                skey = ("fstat", t)
                xt = X[:, t, :]
                ob_ = obuf[t % 2]
                S.op("act", lambda e, xt=xt, sc=sc, ob_=ob_: e.activation(out=ob_, in_=xt, func=AF.Square, accum_out=sc),
                     reads=[("X", t)], writes=[("obuf", t % 2), skey])
                S.op("act", lambda e, sc=sc, rs=rs: e.activation(out=rs, in_=sc, func=AF.Sqrt, scale=1.0 / D, bias=epsc[:]),
                     reads=[skey, "epsc"], writes=[skey])
                S.op("dve", lambda e, rs=rs: e.reciprocal(out=rs, in_=rs), reads=[skey], writes=[skey])
                S.op("dve", lambda e, ob_=ob_, xt=xt, rs=rs: e.scalar_tensor_tensor(out=ob_, in0=xt, scalar=rs, in1=gfin[:], op0=ALU.mult, op1=ALU.mult),
                     reads=[("X", t), skey, "gfin"], writes=[("obuf", t % 2)])
                dst = out_d[p * 512 + t * 128: p * 512 + (t + 1) * 128, :]
                S.op("pool", lambda e, ob_=ob_, dst=dst: e.dma_start(out=dst, in_=ob_), reads=[("obuf", t % 2)], writes=[("outd", p, t)], dma=f"out{t % 2}")

        S.stopped = False
        S.final_waits("pool")

        with nc.Block() as block:
            @block.tensor
            def _(e):
                S.replay("pe", e)

            @block.scalar
            def _(e):
                S.replay("act", e)

            @block.vector
            def _(e):
                S.replay("dve", e)

            @block.gpsimd
            def _(e):
                S.replay("pool", e)

            @block.sync
            def _(e):
                S.replay("sp", e)
    return nc


_CACHE = {}


def _consts():
    bf = ml_dtypes.bfloat16
    ident = np.eye(128, dtype=np.float32)
    onesf = np.full((128, 128), 1.0 / 1024.0, dtype=np.float32)
    identb = np.eye(128).astype(bf)
    onesb = np.ones((128, 128)).astype(bf)
    cm = np.zeros((128, 2, 256), dtype=np.float32)
    for kt in range(2):
        key = kt * 128 + np.arange(128)[:, None]
        q = np.arange(256)[None, :]
        cm[:, kt, :] = np.where(key <= q, 0.0, NEG)
    cmask = cm.reshape(128, 512).astype(bf)
    es = np.zeros((128, 9, 128), dtype=np.float32)
    for n in range(9):
        es[n, n, :] = -NEG
    esel = es.reshape(128, 9 * 128).astype(bf)
    return dict(ident=ident, onesf=onesf, identb=identb, onesb=onesb, cmask=cmask, esel=esel)


def kernel(x, mem, norm_mix_g, w_in, b_in, conv_w, conv_b, conv_ln_g, conv_ln_b,
           w_conv_out, w_att_out, w_out, norm_cross_g, norm_mem_g, w_cq, w_ckv,
           w_co, norm_ffn_g, w_ffn_in, w_ffn_out, norm_final_g):
    f = lambda a: np.ascontiguousarray(np.asarray(a, dtype=np.float32))
    x = f(x); mem = f(mem)
    if "nc" not in _CACHE:
        _CACHE["nc"] = build_program()
    nc = _CACHE["nc"]
    consts = _consts()

    def col(v, k):
        return np.asarray(v, np.float32).reshape(k, 128).T

    shared = dict(
        w_in=f(w_in[0]), w_conv_out=f(w_conv_out[0]), w_att_out=f(w_att_out[0]), w_out=f(w_out[0]),
        w_cq=f(w_cq[0]), w_ckv=f(w_ckv[0]), w_co=f(w_co[0]), w_ffn_in=f(w_ffn_in[0]), w_ffn_out=f(w_ffn_out[0]),
        gfin=f(norm_final_g), gmix=f(norm_mix_g[0]), gcross=f(norm_cross_g[0]), gmem=f(norm_mem_g[0]),
        gffn=f(norm_ffn_g[0]), **consts)
    base = np.zeros((128, NCOLP), np.float32)
    base[:, C_BIN:C_BIN + 72] = col(b_in[0], 72)
    base[:, C_GMIX:C_GMIX + 16] = col(norm_mix_g[0], 16)
    base[:, C_GCROSS:C_GCROSS + 16] = col(norm_cross_g[0], 16)
    base[:, C_GMEM:C_GMEM + 16] = col(norm_mem_g[0], 16)
    base[:, C_GFFN:C_GFFN + 16] = col(norm_ffn_g[0], 16)
    cw = np.asarray(conv_w[0], np.float32)
    base[:, C_CONVW:C_CONVW + 248] = cw.reshape(31, 8, 128).transpose(2, 1, 0).reshape(128, 248)
    base[:, C_CONVB:C_CONVB + 8] = col(conv_b[0], 8)
    base[:, C_LNG:C_LNG + 8] = col(conv_ln_g[0], 8)
    base[:, C_LNB:C_LNB + 8] = col(conv_ln_b[0], 8)
    in_maps = []
    for core in range(8):
        b, half = core // 2, core % 2
        cpm = base.copy()
        cpm[:, C_FLAG] = float(half)
        gm = np.zeros((8, 8), np.float32)
        for t in range(8):
            for n in range(8):
                valid = (n < 4 + t // 2) and (n >= 4 or half == 1)
                gm[t, n] = 0.0 if valid else -1e30
        cpm[:, C_GMASK:C_GMASK + 64] = gm.reshape(1, 64)
        m = dict(shared)
        m["xo"] = np.ascontiguousarray(x[b, half * 1024:(half + 1) * 1024])
        m["xp"] = np.ascontiguousarray(x[b, 0:1024])
        m["mem"] = np.ascontiguousarray(mem[b])
        m["colp"] = cpm
        in_maps.append(m)
    declared = set()
    for alloc in nc.allocations:
        if isinstance(alloc, mybir.MemoryLocationSet) and alloc.kind == "ExternalInput":
            declared.add(alloc.memorylocations[0].name)
    in_maps = [{k: v for k, v in m.items() if k in declared} for m in in_maps]
    cores = [int(c) for c in KCORES.split(",")] if KCORES else list(range(8))
    res = run_bass_kernel_spmd(nc, [in_maps[c] for c in cores], core_ids=list(range(len(cores))))
    _CACHE["last"] = res
    out = np.zeros((4, 2048, 2048), np.float32)
    for i, core in enumerate(cores):
        b, half = core // 2, core % 2
        out[b, half * 1024:(half + 1) * 1024] = res.results[i]["out"]
    return out
```
